# Optimizing a Trainium2 kernel written in Bass

```python
import math
import jax, jax.numpy as jnp
from jax import lax
import numpy as np

D_MODEL = 1024
BATCH = 8
SEQ = 2048
DEPTH = 2
DEC_BATCH = 128
DEC_SEQ = 8
PAST_LEN = 16384
PAGE_SIZE = 128

MIX_WIDTH = D_MODEL
W_A = MIX_WIDTH // 4
W_B = MIX_WIDTH // 4
W_C = MIX_WIDTH // 4
W_D = MIX_WIDTH - W_A - W_B - W_C
CONV_W = 4
RG_HEADS = 4
RG_HD = W_A // RG_HEADS
RG_C = 8.0
SSD_HEAD_DIM = 64
SSD_HEADS = W_B // SSD_HEAD_DIM
SSD_GROUPS = 2
SSD_STATE = 64
SSD_CHUNK = 64
SSD_BC = SSD_GROUPS * SSD_STATE
SSD_CONV_CH = W_B + 2 * SSD_BC
GDN_HEAD_DIM = 64
GDN_HEADS = W_C // GDN_HEAD_DIM
GDN_CHUNK = 64
GDN_CONV_CH = 3 * W_C
S5_GROUP = 16
S5_GROUPS = W_D // S5_GROUP
S5_STATE = 64
ALPHA = (2.0 * DEPTH) ** 0.25
BETA_OUT = (8.0 * DEPTH) ** -0.25
IN_SPLITS = (W_A, W_A,
             SSD_CONV_CH, SSD_HEADS, W_B,
             GDN_CONV_CH, GDN_HEADS, GDN_HEADS, W_C,
             W_D, W_D)
IN_COLS = sum(IN_SPLITS)

kernel_name = "hybrid_rglru_ssd_gdn_s5_step"

F32 = jnp.float32


def split_cols(z, sizes):
    idx = np.cumsum(np.array(sizes))[:-1].tolist()
    return jnp.split(z, idx, axis=-1)


def layer_norm(x, g, b, eps=1e-5):
    xf = x.astype(F32)
    mu = jnp.mean(xf, -1, keepdims=True)
    xc = xf - mu
    var = jnp.mean(xc * xc, -1, keepdims=True)
    return (xc * lax.rsqrt(var + eps) * g.astype(F32) + b.astype(F32)).astype(x.dtype)


def rms_norm(x, g, eps=1e-6):
    xf = x.astype(F32)
    return xf * lax.rsqrt(jnp.mean(xf * xf, -1, keepdims=True) + eps) * g.astype(F32)


def l2norm(x, eps=1e-6):
    return x * lax.rsqrt(jnp.sum(x * x, -1, keepdims=True) + eps)


def causal_conv(u, buf, w, b):
    L = u.shape[1]
    full = jnp.concatenate([buf.astype(u.dtype), u], axis=1)
    ff = full.astype(F32)
    y = b.astype(F32) + sum(ff[:, j:j + L] * w[j].astype(F32) for j in range(CONV_W))
    return y, full[:, L:]


def linear_scan(a, b, h0):
    b = b.at[:, 0].add(a[:, 0] * h0)

    def comb(l, r):
        return r[0] * l[0], r[0] * l[1] + r[1]

    _, h = lax.associative_scan(comb, (a, b), axis=1)
    return h


def complex_linear_scan(a_re, a_im, b_re, b_im, h0_re, h0_im):
    b_re = b_re.at[:, 0].add(a_re[:, 0] * h0_re - a_im[:, 0] * h0_im)
    b_im = b_im.at[:, 0].add(a_re[:, 0] * h0_im + a_im[:, 0] * h0_re)

    def comb(l, r):
        lar, lai, lbr, lbi = l
        rar, rai, rbr, rbi = r
        return (rar * lar - rai * lai, rar * lai + rai * lar,
                rar * lbr - rai * lbi + rbr, rar * lbi + rai * lbr + rbi)

    _, _, h_re, h_im = lax.associative_scan(comb, (a_re, a_im, b_re, b_im), axis=1)
    return h_re, h_im


def rglru_mixer(xr, conv_buf, h0, p):
    xc, new_buf = causal_conv(xr, conv_buf, p['rg_conv_w'], p['rg_conv_b'])
    bsz, L, _ = xc.shape
    xh = xc.reshape(bsz, L, RG_HEADS, RG_HD)
    gate_r = jax.nn.sigmoid(jnp.einsum('blhi,hij->blhj', xh, p['rg_gate_a_w'].astype(F32)).reshape(bsz, L, W_A)
                            + p['rg_gate_a_b'].astype(F32))
    gate_i = jax.nn.sigmoid(jnp.einsum('blhi,hij->blhj', xh, p['rg_gate_x_w'].astype(F32)).reshape(bsz, L, W_A)
                            + p['rg_gate_x_b'].astype(F32))
    log_a = -RG_C * jax.nn.softplus(-p['rg_lambda'].astype(F32)) * gate_r
    a = jnp.exp(log_a)
    b = jnp.sqrt(-jnp.expm1(2.0 * log_a)) * (gate_i * xc)
    h = linear_scan(a, b, h0.astype(F32))
    return h, new_buf, h[:, -1]


def ssd_mixer(xbc, dt_raw, conv_buf, h0, p):
    xbc, new_buf = causal_conv(xbc, conv_buf, p['ssd_conv_w'], p['ssd_conv_b'])
    xbc = jax.nn.silu(xbc)
    bsz, L, _ = xbc.shape
    x, bm, cm = split_cols(xbc, (W_B, SSD_BC, SSD_BC))
    c = math.gcd(L, SSD_CHUNK)
    nc = L // c
    rep = SSD_HEADS // SSD_GROUPS
    x = x.reshape(bsz, nc, c, SSD_HEADS, SSD_HEAD_DIM)
    bm = jnp.repeat(bm.reshape(bsz, nc, c, SSD_GROUPS, SSD_STATE), rep, axis=3)
    cm = jnp.repeat(cm.reshape(bsz, nc, c, SSD_GROUPS, SSD_STATE), rep, axis=3)
    dt = jax.nn.softplus(dt_raw.astype(F32) + p['ssd_dt_bias'].astype(F32)).reshape(bsz, nc, c, SSD_HEADS)
    a = -jnp.exp(p['ssd_a_log'].astype(F32))
    acum = jnp.cumsum(dt * a, axis=2)
    incl = jnp.tril(jnp.ones((c, c), bool))
    seg = jnp.where(incl[:, :, None], acum[:, :, :, None, :] - acum[:, :, None, :, :], -jnp.inf)
    lmat = jnp.exp(seg)
    cb = jnp.einsum('bzthn,bzshn->bztsh', cm, bm)
    y = jnp.einsum('bztsh,bzshp->bzthp', cb * lmat * dt[:, :, None, :, :], x)
    chunk_state = jnp.einsum('bzsh,bzshn,bzshp->bzhpn', jnp.exp(acum[:, :, -1:] - acum) * dt, bm, x)
    chunk_decay = jnp.exp(acum[:, :, -1])

    def step(h, inp):
        cs, cd = inp
        return cd[..., None, None] * h + cs, h

    h_last, h_prev = lax.scan(step, h0.astype(F32),
                              (jnp.moveaxis(chunk_state, 1, 0), jnp.moveaxis(chunk_decay, 1, 0)))
    h_prev = jnp.moveaxis(h_prev, 0, 1)
    y = y + jnp.einsum('bzthn,bzhpn->bzthp', cm * jnp.exp(acum)[..., None], h_prev)
    y = y + p['ssd_d'].astype(F32)[:, None] * x
    return y.reshape(bsz, L, W_B), new_buf, h_last


def gdn_mixer(qkv, beta_raw, decay_raw, conv_buf, s0, p):
    qkv, new_buf = causal_conv(qkv, conv_buf, p['gdn_conv_w'], p['gdn_conv_b'])
    qkv = jax.nn.silu(qkv)
    bsz, L, _ = qkv.shape
    qkv = qkv.reshape(bsz, L, 3, GDN_HEADS, GDN_HEAD_DIM)
    q = l2norm(qkv[:, :, 0]) * (GDN_HEAD_DIM ** -0.5)
    k = l2norm(qkv[:, :, 1])
    v = qkv[:, :, 2]
    beta = jax.nn.sigmoid(beta_raw.astype(F32))
    g = -jnp.exp(p['gdn_a_log'].astype(F32)) * jax.nn.softplus(
        decay_raw.astype(F32) + p['gdn_dt_bias'].astype(F32))
    c = math.gcd(L, GDN_CHUNK)
    nc = L // c

    def chunk_heads(t):
        return jnp.swapaxes(t.reshape((bsz, nc, c) + t.shape[2:]), 2, 3)

    q, k, v, beta, g = (chunk_heads(t) for t in (q, k, v, beta, g))
    decay = jnp.cumsum(g, axis=-1)
    incl = jnp.tril(jnp.ones((c, c), bool))
    strict = jnp.tril(jnp.ones((c, c), bool), -1)
    gamma = jnp.exp(jnp.where(incl, decay[..., :, None] - decay[..., None, :], -jnp.inf))
    kk = jnp.einsum('bzhtd,bzhsd->bzhts', k, k)
    m = jnp.where(strict, beta[..., :, None] * kk * gamma, 0.0)
    eye = jnp.eye(c, dtype=F32)
    tmat = lax.linalg.triangular_solve(eye + m, jnp.broadcast_to(eye, m.shape),
                                       left_side=True, lower=True, unit_diagonal=True)
    value = tmat @ (beta[..., None] * v)
    kcum = tmat @ ((beta * jnp.exp(decay))[..., None] * k)
    qk = jnp.einsum('bzhtd,bzhsd->bzhts', q, k) * gamma
    qdec = q * jnp.exp(decay)[..., None]
    kdec = k * jnp.exp(decay[..., -1:] - decay)[..., None]
    chunk_decay = jnp.exp(decay[..., -1])

    def step(s, inp):
        value_i, kcum_i, qdec_i, qk_i, kdec_i, cd_i = inp
        w = value_i - jnp.einsum('bhtk,bhkv->bhtv', kcum_i, s)
        o = jnp.einsum('bhtk,bhkv->bhtv', qdec_i, s) + jnp.einsum('bhts,bhsv->bhtv', qk_i, w)
        s = cd_i[..., None, None] * s + jnp.einsum('bhsk,bhsv->bhkv', kdec_i, w)
        return s, o

    xs = tuple(jnp.moveaxis(t, 1, 0) for t in (value, kcum, qdec, qk, kdec, chunk_decay))
    s_last, o = lax.scan(step, s0.astype(F32), xs)
    o = jnp.swapaxes(jnp.moveaxis(o, 0, 1), 2, 3).reshape(bsz, L, GDN_HEADS, GDN_HEAD_DIM)
    o = rms_norm(o, p['gdn_norm_w']).reshape(bsz, L, W_C)
    return o, new_buf, s_last


def s5_mixer(u, h0_re, h0_im, p):
    bsz, L, _ = u.shape
    ug = u.astype(F32).reshape(bsz, L, S5_GROUPS, S5_GROUP)
    dt = jnp.exp(p['s5_log_dt'].astype(F32))[:, None]
    lr = p['s5_lambda_re'].astype(F32)
    li = p['s5_lambda_im'].astype(F32)
    mag = jnp.exp(lr * dt)
    ang = li * dt
    ab_re = mag * jnp.cos(ang)
    ab_im = mag * jnp.sin(ang)
    den = lr * lr + li * li
    f_re = ((ab_re - 1.0) * lr + ab_im * li) / den
    f_im = (ab_im * lr - (ab_re - 1.0) * li) / den
    b_re = p['s5_b_re'].astype(F32)
    b_im = p['s5_b_im'].astype(F32)
    bb_re = f_re[..., None] * b_re - f_im[..., None] * b_im
    bb_im = f_re[..., None] * b_im + f_im[..., None] * b_re
    bu_re = jnp.einsum('blgi,gni->blgn', ug, bb_re)
    bu_im = jnp.einsum('blgi,gni->blgn', ug, bb_im)
    shp = bu_re.shape
    h_re, h_im = complex_linear_scan(jnp.broadcast_to(ab_re, shp), jnp.broadcast_to(ab_im, shp),
                                     bu_re, bu_im, h0_re.astype(F32), h0_im.astype(F32))
    y = (jnp.einsum('blgn,gin->blgi', h_re, p['s5_c_re'].astype(F32))
         - jnp.einsum('blgn,gin->blgi', h_im, p['s5_c_im'].astype(F32))
         + p['s5_d'].astype(F32).reshape(S5_GROUPS, S5_GROUP) * ug)
    y = jax.nn.gelu(y.reshape(bsz, L, W_D), approximate=False)
    y = y * jax.nn.sigmoid(jnp.einsum('ble,ef->blf', y, p['s5_glu_w'].astype(F32)) + p['s5_glu_b'].astype(F32))
    return y, h_re[:, -1], h_im[:, -1]


def layer_forward(x, states, p):
    conv_a, h_a, conv_b, h_b, conv_c, s_c, s5_re, s5_im = states
    z = jnp.einsum('bld,de->ble', x, p['w_in'])
    (a_x, a_gate, b_xbc, b_dt, b_gate, c_qkv, c_beta, c_decay, c_gate,
     d_u, d_gate) = split_cols(z, IN_SPLITS)
    ya, conv_a, h_a = rglru_mixer(a_x, conv_a, h_a, p)
    ya = ya * jax.nn.silu(a_gate.astype(F32))
    yb, conv_b, h_b = ssd_mixer(b_xbc, b_dt, conv_b, h_b, p)
    yb = rms_norm(yb * jax.nn.silu(b_gate.astype(F32)), p['ssd_norm_w'])
    yc, conv_c, s_c = gdn_mixer(c_qkv, c_beta, c_decay, conv_c, s_c, p)
    yc = yc * jax.nn.silu(c_gate.astype(F32))
    yd, s5_re, s5_im = s5_mixer(d_u, s5_re, s5_im, p)
    yd = yd * jax.nn.silu(d_gate.astype(F32))
    mix = jnp.concatenate([ya, yb, yc, yd], axis=-1).astype(x.dtype)
    out = jnp.einsum('ble,ed->bld', mix, p['w_out'])
    x = layer_norm(ALPHA * x + out, p['ln_g'], p['ln_b'])
    return x, (conv_a, h_a, conv_b, h_b, conv_c, s_c, s5_re, s5_im)


def zero_states(bsz):
    return (jnp.zeros((bsz, CONV_W - 1, W_A), F32),
            jnp.zeros((bsz, W_A), F32),
            jnp.zeros((bsz, CONV_W - 1, SSD_CONV_CH), F32),
            jnp.zeros((bsz, SSD_HEADS, SSD_HEAD_DIM, SSD_STATE), F32),
            jnp.zeros((bsz, CONV_W - 1, GDN_CONV_CH), F32),
            jnp.zeros((bsz, GDN_HEADS, GDN_HEAD_DIM, GDN_HEAD_DIM), F32),
            jnp.zeros((bsz, S5_GROUPS, S5_STATE), F32),
            jnp.zeros((bsz, S5_GROUPS, S5_STATE), F32))


def setup_inputs(seed: int = 0) -> dict:
    key = jax.random.key(seed)
    ks = iter(jax.random.split(key, 64))

    def nrm(shape, s):
        return s * jax.random.normal(next(ks), shape, F32)

    def uni(shape, lo, hi):
        return jax.random.uniform(next(ks), shape, F32, lo, hi)

    def dt_bias(shape):
        dt = jnp.exp(uni(shape, math.log(1e-3), math.log(1e-1)))
        return dt + jnp.log(-jnp.expm1(-dt))

    rg_s = uni((DEPTH, W_A), 0.9, 0.999) ** (1.0 / RG_C)
    n_idx = jnp.arange(S5_STATE, dtype=F32)
    return {
        'x_prompt': nrm((BATCH, SEQ, D_MODEL), 1.0),
        'x_sample': nrm((DEC_BATCH, DEC_SEQ, D_MODEL), 1.0),
        'cache_rglru_conv': nrm((DEPTH, DEC_BATCH, CONV_W - 1, W_A), 1.0),
        'state_rglru': nrm((DEPTH, DEC_BATCH, W_A), 0.5),
        'cache_ssd_conv': nrm((DEPTH, DEC_BATCH, CONV_W - 1, SSD_CONV_CH), 1.0),
        'state_ssd': nrm((DEPTH, DEC_BATCH, SSD_HEADS, SSD_HEAD_DIM, SSD_STATE), 0.1),
        'cache_gdn_conv': nrm((DEPTH, DEC_BATCH, CONV_W - 1, GDN_CONV_CH), 1.0),
        'state_gdn': nrm((DEPTH, DEC_BATCH, GDN_HEADS, GDN_HEAD_DIM, GDN_HEAD_DIM), 0.1),
        'state_s5_re': nrm((DEPTH, DEC_BATCH, S5_GROUPS, S5_STATE), 0.1),
        'state_s5_im': nrm((DEPTH, DEC_BATCH, S5_GROUPS, S5_STATE), 0.1),
        'w_in': nrm((DEPTH, D_MODEL, IN_COLS), D_MODEL ** -0.5),
        'w_out': nrm((DEPTH, MIX_WIDTH, D_MODEL), BETA_OUT * MIX_WIDTH ** -0.5),
        'ln_g': 1.0 + nrm((DEPTH, D_MODEL), 0.01),
        'ln_b': nrm((DEPTH, D_MODEL), 0.01),
        'rg_conv_w': nrm((DEPTH, CONV_W, W_A), CONV_W ** -0.5),
        'rg_conv_b': nrm((DEPTH, W_A), 0.01),
        'rg_gate_a_w': nrm((DEPTH, RG_HEADS, RG_HD, RG_HD), RG_HD ** -0.5),
        'rg_gate_a_b': nrm((DEPTH, W_A), 0.01),
        'rg_gate_x_w': nrm((DEPTH, RG_HEADS, RG_HD, RG_HD), RG_HD ** -0.5),
        'rg_gate_x_b': nrm((DEPTH, W_A), 0.01),
        'rg_lambda': jnp.log(rg_s) - jnp.log1p(-rg_s),
        'ssd_conv_w': nrm((DEPTH, CONV_W, SSD_CONV_CH), CONV_W ** -0.5),
        'ssd_conv_b': nrm((DEPTH, SSD_CONV_CH), 0.01),
        'ssd_dt_bias': dt_bias((DEPTH, SSD_HEADS)),
        'ssd_a_log': jnp.log(uni((DEPTH, SSD_HEADS), 1.0, 16.0)),
        'ssd_d': 1.0 + nrm((DEPTH, SSD_HEADS), 0.01),
        'ssd_norm_w': 1.0 + nrm((DEPTH, W_B), 0.01),
        'gdn_conv_w': nrm((DEPTH, CONV_W, GDN_CONV_CH), CONV_W ** -0.5),
        'gdn_conv_b': nrm((DEPTH, GDN_CONV_CH), 0.01),
        'gdn_dt_bias': dt_bias((DEPTH, GDN_HEADS)),
        'gdn_a_log': jnp.log(uni((DEPTH, GDN_HEADS), 1.0, 16.0)),
        'gdn_norm_w': 1.0 + nrm((DEPTH, GDN_HEAD_DIM), 0.01),
        's5_lambda_re': -0.5 + nrm((DEPTH, S5_GROUPS, S5_STATE), 0.01),
        's5_lambda_im': math.pi * n_idx + nrm((DEPTH, S5_GROUPS, S5_STATE), 0.01),
        's5_log_dt': uni((DEPTH, S5_GROUPS), math.log(1e-3), math.log(1e-1)),
        's5_b_re': nrm((DEPTH, S5_GROUPS, S5_STATE, S5_GROUP), (2.0 * S5_GROUP) ** -0.5),
        's5_b_im': nrm((DEPTH, S5_GROUPS, S5_STATE, S5_GROUP), (2.0 * S5_GROUP) ** -0.5),
        's5_c_re': nrm((DEPTH, S5_GROUPS, S5_GROUP, S5_STATE), S5_STATE ** -0.5),
        's5_c_im': nrm((DEPTH, S5_GROUPS, S5_GROUP, S5_STATE), S5_STATE ** -0.5),
        's5_d': nrm((DEPTH, W_D), 1.0),
        's5_glu_w': nrm((DEPTH, W_D, W_D), W_D ** -0.5),
        's5_glu_b': nrm((DEPTH, W_D), 0.01),
    }


def reference(x_prompt, x_sample, cache_rglru_conv, state_rglru, cache_ssd_conv, state_ssd,
              cache_gdn_conv, state_gdn, state_s5_re, state_s5_im,
              w_in, w_out, ln_g, ln_b,
              rg_conv_w, rg_conv_b, rg_gate_a_w, rg_gate_a_b, rg_gate_x_w, rg_gate_x_b, rg_lambda,
              ssd_conv_w, ssd_conv_b, ssd_dt_bias, ssd_a_log, ssd_d, ssd_norm_w,
              gdn_conv_w, gdn_conv_b, gdn_dt_bias, gdn_a_log, gdn_norm_w,
              s5_lambda_re, s5_lambda_im, s5_log_dt, s5_b_re, s5_b_im, s5_c_re, s5_c_im,
              s5_d, s5_glu_w, s5_glu_b):
    def layer_params(l):
        return dict(w_in=w_in[l], w_out=w_out[l], ln_g=ln_g[l], ln_b=ln_b[l],
                    rg_conv_w=rg_conv_w[l], rg_conv_b=rg_conv_b[l],
                    rg_gate_a_w=rg_gate_a_w[l], rg_gate_a_b=rg_gate_a_b[l],
                    rg_gate_x_w=rg_gate_x_w[l], rg_gate_x_b=rg_gate_x_b[l], rg_lambda=rg_lambda[l],
                    ssd_conv_w=ssd_conv_w[l], ssd_conv_b=ssd_conv_b[l], ssd_dt_bias=ssd_dt_bias[l],
                    ssd_a_log=ssd_a_log[l], ssd_d=ssd_d[l], ssd_norm_w=ssd_norm_w[l],
                    gdn_conv_w=gdn_conv_w[l], gdn_conv_b=gdn_conv_b[l], gdn_dt_bias=gdn_dt_bias[l],
                    gdn_a_log=gdn_a_log[l], gdn_norm_w=gdn_norm_w[l],
                    s5_lambda_re=s5_lambda_re[l], s5_lambda_im=s5_lambda_im[l], s5_log_dt=s5_log_dt[l],
                    s5_b_re=s5_b_re[l], s5_b_im=s5_b_im[l], s5_c_re=s5_c_re[l], s5_c_im=s5_c_im[l],
                    s5_d=s5_d[l], s5_glu_w=s5_glu_w[l], s5_glu_b=s5_glu_b[l])

    sample_states = (cache_rglru_conv, state_rglru, cache_ssd_conv, state_ssd,
                     cache_gdn_conv, state_gdn, state_s5_re, state_s5_im)
    prompt_init = zero_states(x_prompt.shape[0])
    yp = x_prompt
    ys = x_sample
    p_new = []
    s_new = []
    for l in range(DEPTH):
        prm = layer_params(l)
        yp, st_p = layer_forward(yp, prompt_init, prm)
        ys, st_s = layer_forward(ys, tuple(s[l] for s in sample_states), prm)
        p_new.append(st_p)
        s_new.append(st_s)

    (p_rg_conv, p_rg, p_ssd_conv, p_ssd, p_gdn_conv, p_gdn, p_s5_re, p_s5_im) = [
        jnp.stack([st[i] for st in p_new]) for i in range(8)]
    (s_rg_conv, s_rg, s_ssd_conv, s_ssd, s_gdn_conv, s_gdn, s_s5_re, s_s5_im) = [
        jnp.stack([st[i] for st in s_new]) for i in range(8)]
    return (yp, ys,
            p_rg_conv, p_rg, p_ssd_conv, p_ssd, p_gdn_conv, p_gdn, p_s5_re, p_s5_im,
            s_rg_conv, s_rg, s_ssd_conv, s_ssd, s_gdn_conv, s_gdn, s_s5_re, s_s5_im)
```

```python
import numpy as np
import os
import math
import itertools
INTERLEAVE = int(os.environ.get("KINTER", "1"))
NYBUF = int(os.environ.get("KYBUF", "2"))
SLAT = float(os.environ.get("KSLAT", "0.15"))
NXRES = int(os.environ.get("KXRES", "1"))
CSPLIT = int(os.environ.get("KSPLIT", "4"))
KSTOP = int(os.environ.get("KSTOP", "99"))
KSUB = int(os.environ.get("KSUB", "99"))
from contextlib import ExitStack
import concourse.bass as bass
import concourse.mybir as mybir
from concourse.bass_utils import run_bass_kernel_spmd

F32 = mybir.dt.float32
BF16 = mybir.dt.bfloat16
I32 = mybir.dt.int32
AF = mybir.ActivationFunctionType
ALU = mybir.AluOpType

NCORES = 8
D = 1024
L_P = 2048
NS_SEQ = 16
LS = 8
NB = 256
DEPTH = 2
INC = 2828
ALPHA = (2.0 * DEPTH) ** 0.25
C_AX, C_AG = 0, 256
C_BX, C_BB, C_BC, C_BDT, C_BG = 512, 768, 896, 1024, 1028
C_CQ, C_CK, C_CV, C_CBETA, C_CDEC, C_CG = 1284, 1540, 1796, 2052, 2056, 2060
C_DU, C_DG = 2316, 2572
NEG = -30000.0


class T:
    __slots__ = ("name", "w", "r", "dsem", "dcnt", "psum", "gen")

    def __init__(self, name):
        self.name = name
        self.psum = False
        self.gen = 0
        self.w = None
        self.r = []
        self.dsem = None
        self.dcnt = 0


class View:
    __slots__ = ("ap", "ts", "gen")

    def __init__(self, ap, ts, gen=None):
        self.ap = ap
        self.ts = ts
        self.gen = gen

    def __getitem__(self, idx):
        return View(self.ap[idx], self.ts, self.gen)

    def rr(self, pat, **kw):
        return View(self.ap.rearrange(pat, **kw), self.ts, self.gen)

    def bc(self, shape):
        return View(self.ap.to_broadcast(shape), self.ts, self.gen)


class Buf:
    def __init__(self, tensor, name, t=None, gen=None):
        self.tensor = tensor
        self.t = t or T(name)
        self.gen = gen

    def __getitem__(self, idx):
        return View(self.tensor[idx], [self.t], self.gen)

    def v(self):
        return View(self.tensor[:], [self.t], self.gen)


class Op:
    __slots__ = ("id", "e", "fn", "preds", "dur", "tbl", "kind", "tile", "dval", "idx", "prio", "fin", "kw")

    def __init__(self, id, e, fn, preds, dur, tbl=None, kind="c", tile=None, dval=0):
        self.id, self.e, self.fn, self.preds, self.dur, self.tbl, self.kind, self.tile, self.dval = id, e, fn, preds, dur, tbl, kind, tile, dval
        self.idx = 0
        self.prio = 0.0
        self.fin = 0.0


def _free(ap):
    n = 1
    for d in list(ap.shape)[1:]:
        n *= int(d)
    return n


ACT_TBL = {}


class Sched:
    def __init__(self, nc):
        self.nc = nc
        self.engs = {"pe": nc.tensor, "dve": nc.vector, "act": nc.scalar, "pool": nc.gpsimd, "sp": nc.sync}
        self.sem = {k: nc.alloc_semaphore(name="s_" + k) for k in self.engs}
        self.ops = []
        self.ninst = 0
        self.cnt = {k: 0 for k in self.engs}
        self.out_tiles = []

    def _record(self, e, fn, rt, wt, dur, tbl=None, kind="c", tile=None, dval=0):
        preds = {}
        for t in rt:
            if t.w is not None:
                preds[t.w] = "raw"
            if t.psum:
                for r in t.r:
                    if self.ops[r].e != e:
                        preds.setdefault(r, "rar")
        for t in wt:
            if t.w is not None:
                preds.setdefault(t.w, "waw:" + t.name)
            for r in t.r:
                preds.setdefault(r, "war:" + t.name)
        op = Op(len(self.ops), e, fn, preds, dur, tbl, kind, tile, dval)
        op.kw = getattr(self, "lbl", "")
        self.ops.append(op)
        for t in rt:
            t.r.append(op.id)
        for t in wt:
            t.w = op.id
            t.r = []
        self.ninst += 1
        self.cnt[e] += 1
        return op

    def emit(self, e, fn, reads, writes, dur=0.3, tbl=None):
        rt = []
        for v in reads:
            if isinstance(v, View):
                rt.extend(v.ts)
        wt = []
        for v in writes:
            wt.extend(v.ts)
        for v in list(reads) + list(writes):
            if isinstance(v, View) and v.gen is not None:
                assert v.gen == v.ts[0].gen, f"stale PSUM handle used: bank {v.ts[0].name} was re-allocated (gen {v.gen} vs {v.ts[0].gen})"
        return self._record(e, fn, rt, wt, dur, tbl)

    def dma_in(self, e, dst, src_ap, dram_reads=(), **kw):
        assert len(dst.ts) == 1
        t = dst.ts[0]
        if t.dsem is None:
            t.dsem = self.nc.alloc_semaphore(name="d_" + t.name)
        t.dcnt += 16
        nbytes = _free(dst.ap) * 4
        sib = None
        if t.w is not None and not t.r and self.ops[t.w].kind == "dma" and self.ops[t.w].kw == "dma_in":
            sib = self.ops[t.w]
            t.w = None
        op = self._record(e, lambda: self.engs[e].dma_start(out=dst.ap, in_=src_ap, **kw), list(dram_reads), [t],
                          2.0 + nbytes * 128 / 60e3, kind="dma", tile=t, dval=t.dcnt)
        if sib is not None:
            for p, kd in sib.preds.items():
                op.preds.setdefault(p, kd)
        op.kw = "dma_in"
        return op

    def dma_out(self, e, dst_ap, src, dram_writes=(), **kw):
        assert len(src.ts) == 1
        t = src.ts[0]
        if t.dsem is None:
            t.dsem = self.nc.alloc_semaphore(name="d_" + t.name)
        t.dcnt += 16
        if t not in self.out_tiles:
            self.out_tiles.append(t)
        nbytes = _free(src.ap) * 4
        return self._record(e, lambda: self.engs[e].dma_start(out=dst_ap, in_=src.ap, **kw), [t], list(dram_writes),
                            2.0 + nbytes * 128 / 60e3, kind="dma", tile=t, dval=t.dcnt)

    def drain(self, e, tiles):
        pass

    def finalize(self, reorder=True):
        ops = self.ops
        n = len(ops)
        LAT = float(os.environ.get("KLAT", "0.35"))
        succs = [[] for _ in range(n)]
        for op in ops:
            for p in op.preds:
                succs[p].append(op.id)
        for op in reversed(ops):
            m = 0.0
            for sid in succs[op.id]:
                if ops[sid].prio > m:
                    m = ops[sid].prio
            op.prio = op.dur + LAT + m
        order = {k: [] for k in self.engs}
        if not reorder:
            for op in ops:
                order[op.e].append(op)
        else:
            import heapq
            npred = [len(op.preds) for op in ops]
            ready_t = [0.0] * n
            avail = {k: [] for k in self.engs}
            for op in ops:
                if npred[op.id] == 0:
                    heapq.heappush(avail[op.e], (-op.prio, op.id))
            free = {k: 0.0 for k in self.engs}
            cur_tbl = None
            done = 0
            K = 48
            while done < n:
                best = None
                for e, hp in avail.items():
                    if not hp:
                        continue
                    cands = heapq.nsmallest(K, hp)
                    for (npr, oid) in cands:
                        op = ops[oid]
                        st = max(free[e], ready_t[oid])
                        if e == "act" and op.tbl is not None and op.tbl != cur_tbl:
                            st += 1.3
                        key = (st, npr)
                        if best is None or key < best[0]:
                            best = (key, e, oid, (npr, oid))
                (st, _), e, oid, item = best
                avail[e].remove(item)
                heapq.heapify(avail[e])
                op = ops[oid]
                if e == "act" and op.tbl is not None:
                    cur_tbl = op.tbl
                if op.kind == "dma":
                    free[e] = st + 0.1
                    op.fin = st + op.dur
                else:
                    free[e] = st + op.dur
                    op.fin = free[e]
                order[e].append(op)
                done += 1
                for sid in succs[oid]:
                    npred[sid] -= 1
                    lat = (0.0 if e == "pe" else SLAT) if ops[sid].e == e else LAT
                    if op.fin + lat > ready_t[sid]:
                        ready_t[sid] = op.fin + lat
                    if npred[sid] == 0:
                        heapq.heappush(avail[ops[sid].e], (-ops[sid].prio, sid))
            self.makespan = max(free.values())
        for e, lst in order.items():
            i = 0
            for op in lst:
                if op.kind != "dma":
                    i += 1
                    op.idx = i
        for e, lst in order.items():
            seen = {}
            eng = self.engs[e]

            def wait(key, sem, val):
                if seen.get(key, 0) >= val:
                    return
                eng.wait_ge(sem, val)
                seen[key] = val
            for op in lst:
                dmax = {}
                for p in op.preds:
                    P = ops[p]
                    if P.kind == "dma":
                        k = id(P.tile)
                        if k not in dmax or dmax[k][1] < P.dval:
                            dmax[k] = (P.tile.dsem, P.dval)
                for k in sorted(dmax, key=lambda kk: dmax[kk][1]):
                    wait(("d", k), dmax[k][0], dmax[k][1])
                for p in sorted(op.preds):
                    P = ops[p]
                    if P.kind == "dma":
                        continue
                    elif P.e == e:
                        if e != "pe":
                            wait(e, self.sem[e], P.idx)
                    else:
                        wait(P.e, self.sem[P.e], P.idx)
                ins = op.fn()
                if op.kind == "dma":
                    ins.then_inc(op.tile.dsem, 16)
                else:
                    ins.then_inc(self.sem[e], 1)
            if e == "sp":
                for t in self.out_tiles:
                    wait(("d", id(t)), t.dsem, t.dcnt)
                for k2 in ("pe", "dve", "act", "pool"):
                    last = [o for o in order[k2] if o.kind != "dma"]
                    if last:
                        wait(k2, self.sem[k2], last[-1].idx)

    def _e(self, e):
        return self.engs[e]

    def mm(self, out, lhsT, rhs, start=True, stop=True, tile_position=None):
        kw = {} if tile_position is None else {"tile_position": tile_position}
        n = _free(rhs.ap)
        return self.emit("pe", lambda: self.nc.tensor.matmul(out.ap, lhsT=lhsT.ap, rhs=rhs.ap, start=start, stop=stop, **kw),
                         [lhsT, rhs], [out], dur=0.03 + max(64, n) / 1400.0)

    def tr(self, out, in_, ident):
        return self.emit("pe", lambda: self.nc.tensor.transpose(out.ap, in_.ap, ident.ap), [in_, ident], [out], dur=0.11)

    def act(self, out, in_, func, bias=None, scale=None, e="act"):
        kw = {}
        rd = [in_]
        if bias is not None:
            if isinstance(bias, View):
                kw["bias"] = bias.ap
                rd.append(bias)
            else:
                kw["bias"] = float(bias)
        if scale is not None:
            if isinstance(scale, View):
                kw["scale"] = scale.ap
                rd.append(scale)
            else:
                kw["scale"] = float(scale)
        tbl = {AF.Exp: "el", AF.Ln: "el", AF.Sigmoid: "sg", AF.Silu: "si", AF.Gelu: "ge", AF.Sin: "sn"}.get(func)
        return self.emit("act", lambda: self.nc.scalar.activation(out=out.ap, in_=in_.ap, func=func, **kw), rd, [out],
                         dur=0.2 + _free(out.ap) * 0.00085, tbl=tbl)

    def _dd(self, out):
        return 0.13 + _free(out.ap) * 0.0007

    def copy(self, e, out, in_):
        if e == "act":
            return self.emit("act", lambda: self.nc.scalar.copy(out=out.ap, in_=in_.ap), [in_], [out], dur=0.2 + _free(out.ap) * 0.00085)
        return self.emit(e, lambda: self._e(e).tensor_copy(out=out.ap, in_=in_.ap), [in_], [out], dur=self._dd(out))

    def tt(self, e, out, a, b, op):
        return self.emit(e, lambda: self._e(e).tensor_tensor(out=out.ap, in0=a.ap, in1=b.ap, op=op), [a, b], [out], dur=self._dd(out) * 1.3)

    def ts(self, e, out, a, s1, op0, s2=None, op1=None):
        rd = [a]
        s1a = s1.ap if isinstance(s1, View) else float(s1)
        if isinstance(s1, View):
            rd.append(s1)
        kw = {}
        if op1 is not None:
            kw["op1"] = op1
            s2a = s2.ap if isinstance(s2, View) else float(s2)
            if isinstance(s2, View):
                rd.append(s2)
        else:
            s2a = None
        return self.emit(e, lambda: self._e(e).tensor_scalar(out=out.ap, in0=a.ap, scalar1=s1a, scalar2=s2a, op0=op0, **kw),
                         rd, [out], dur=self._dd(out))

    def stt(self, out, a, s, b, op0, op1):
        rd = [a, b]
        sa = s.ap if isinstance(s, View) else float(s)
        if isinstance(s, View):
            rd.append(s)
        return self.emit("dve", lambda: self.nc.vector.scalar_tensor_tensor(out=out.ap, in0=a.ap, scalar=sa, in1=b.ap,
                                                                           op0=op0, op1=op1), rd, [out], dur=self._dd(out) * 1.3)

    def scan(self, out, d0, d1, init, op0=ALU.mult, op1=ALU.add):
        rd = [d0, d1]
        ia = init.ap if isinstance(init, View) else float(init)
        if isinstance(init, View):
            rd.append(init)
        return self.emit("dve", lambda: self.nc.vector.tensor_tensor_scan(out=out.ap, data0=d0.ap, data1=d1.ap, initial=ia,
                                                                          op0=op0, op1=op1), rd, [out], dur=0.15 + _free(out.ap) * 0.0021)

    def memset(self, e, out, val):
        return self.emit(e, lambda: self._e(e).memset(out.ap, val), [], [out], dur=self._dd(out) * (4 if e == "pool" else 1))

    def recip(self, out, in_):
        return self.emit("dve", lambda: self.nc.vector.reciprocal(out=out.ap, in_=in_.ap), [in_], [out], dur=0.15 + _free(out.ap) * 0.005)


class Blk:
    def __init__(self, idx, sample):
        self.idx = idx
        self.sample = sample
        self.N = 128 if sample else NB
        self.G = NS_SEQ if sample else 1
        self.Lg = LS if sample else NB
        self.ntile = self.N // 128
        self.tok0 = 2048 if sample else idx * NB
        self.first = (not sample) and idx == 0
        self.last = (not sample) and idx == L_P // NB - 1


class Prog:
    def __init__(self, mixers="ABCD", dbg=None):
        self.mixers = mixers
        self.dbg_req = dbg or []
        self.nc = nc = bass.Bass("TRN2", target_bir_lowering=False)
        self.es = ExitStack()
        self.es.enter_context(nc.allow_non_contiguous_dma(reason="small strided parameter/state loads"))
        self.S = Sched(nc)
        self.uid = 0
        self.out_tiles = []
        self._declare_dram()
        self._alloc()
        self._consts()
        for l in range(DEPTH):
            self._layer(l)
        self.S.finalize(reorder=bool(int(os.environ.get("KSCHED", "1"))))
        self.es.close()

    def sb(self, name, shape, dt=F32):
        self.uid += 1
        t = self.es.enter_context(self.nc.sbuf_tensor(f"{name}_{self.uid}", list(shape), dt))
        return Buf(t, name)

    def _declare_dram(self):
        nc = self.nc
        di = lambda n, s: nc.dram_tensor(n, list(s), F32, kind="ExternalInput").ap()
        do = lambda n, s: nc.dram_tensor(n, list(s), F32, kind="ExternalOutput").ap()
        self.I = I = {}
        I["xp"] = di("xp", [L_P, D])
        I["xs"] = di("xs", [128, D])
        I["cache_rglru_conv"] = di("cache_rglru_conv", [DEPTH, 16, 3, 256])
        I["state_rglru"] = di("state_rglru", [DEPTH, 16, 256])
        I["cache_ssd_conv"] = di("cache_ssd_conv", [DEPTH, 16, 3, 512])
        I["state_ssd"] = di("state_ssd", [DEPTH, 16, 4, 64, 64])
        I["cache_gdn_conv"] = di("cache_gdn_conv", [DEPTH, 16, 3, 768])
        I["state_gdn"] = di("state_gdn", [DEPTH, 16, 4, 64, 64])
        I["state_s5_re"] = di("state_s5_re", [DEPTH, 16, 16, 64])
        I["state_s5_im"] = di("state_s5_im", [DEPTH, 16, 16, 64])
        for n, s in WSHAPES.items():
            I[n] = di(n, s)
        self.O = O = {}
        O["yp"] = do("yp", [L_P, D])
        O["ys"] = do("ys", [128, D])
        for pre, nb in (("p", 1), ("s", 16)):
            O[pre + "_rg_conv"] = do(pre + "_rg_conv", [DEPTH, nb, 3, 256])
            O[pre + "_rg"] = do(pre + "_rg", [DEPTH, nb, 256])
            O[pre + "_ssd_conv"] = do(pre + "_ssd_conv", [DEPTH, nb, 3, 512])
            O[pre + "_ssd"] = do(pre + "_ssd", [DEPTH, nb, 4, 64, 64])
            O[pre + "_gdn_conv"] = do(pre + "_gdn_conv", [DEPTH, nb, 3, 768])
            O[pre + "_gdn"] = do(pre + "_gdn", [DEPTH, nb, 4, 64, 64])
            O[pre + "_s5_re"] = do(pre + "_s5_re", [DEPTH, nb, 16, 64])
            O[pre + "_s5_im"] = do(pre + "_s5_im", [DEPTH, nb, 16, 64])
        self.x1 = nc.dram_tensor("x1_scratch", [L_P + 128, D], F32).ap()
        self.x1_t = [T(f"x1row{i}") for i in range((L_P + 128) // 128)]
        self.DBG = {}
        for n, s in self.dbg_req:
            self.DBG[n] = do("dbg_" + n, s)

    def _alloc(self):
        nc = self.nc
        self.psb = []
        for i in range(7):
            t = self.es.enter_context(nc.psum_tensor(f"ps{i}", [128, 512], F32))
            self.psb.append(Buf(t, f"ps{i}"))
            self.psb[-1].t.psum = True
        t = self.es.enter_context(nc.psum_tensor("pst", [128, 1024], BF16))
        self.pst = Buf(t, "pst")
        self.pst.t.psum = True
        self.ps_i = 0
        self.psC = 0
        self.psO = 0
        self.psD = 0
        self.psS = 0
        self.WbG = []
        for (c0, c1) in ((0, C_BX), (C_BX, C_CQ), (C_CQ, C_DU), (C_DU, INC)):
            self.WbG.append((c0, c1, self.sb(f"Wb{c0}", [128, 8, c1 - c0], BF16)))
        self.Wo = self.sb("Wo", [128, 8, D], BF16)
        self.Wsm = self.sb("Wsm", [128, 8, 12], BF16)
        self.wst_i = 0
        self.xrt = [self.sb(f"xres{i}", [128, D]) for i in range(NB // 128)]
        self.xbf = [self.sb("xbf0", [128, D], BF16)]
        self.xT = self.sb("xT", [128, 8, NB], BF16)
        self.xTl = self.sb("xTl", [128, 8, 48], BF16)
        self.mixT = [self.sb(f"mixT{i}", [128, NB], BF16) for i in range(8)]
        self.lng = self.sb("lng", [128, D])
        self.lnb = self.sb("lnb", [128, D])
        self.ybuf = [self.sb(f"ybuf{i}", [128, D]) for i in range(NYBUF)]
        self.lnstat = [self.sb(f"lnstat{i}", [128, 16]) for i in range(2)]
        self.cst_i = 0
        self.cpad = [self.sb(f"cpad{i}", [128, NB + 3], BF16) for i in range(12)]
        self.cacc = [self.sb(f"cacc{i}", [128, NB]) for i in range(4)]
        self.cd_i = 0
        self.cw = self.sb("cw", [128, 4, 12])
        self.cbias = self.sb("cbias", [128, 12])
        self.chist = [self.sb(f"chist{i}", [48, 128]) for i in range(2)]
        self.ftmp = [self.sb(f"ftmp{i}", [128, NB]) for i in range(10)]
        self.wtmp = [self.sb(f"wtmp{i}", [128, 512]) for i in range(3)]
        self.btmp = [self.sb(f"btmp{i}", [128, NB], BF16) for i in range(6)]
        self.xcA = [self.sb(f"xcA{i}", [128, NB]) for i in range(2)]
        self.xcbA = [self.sb(f"xcbA{i}", [128, NB], BF16) for i in range(2)]
        self.rgW = self.sb("rgW", [128, 2, 2, 128], BF16)
        self.rgp = self.sb("rgp", [128, 2, 8])
        self.rgh = self.sb("rgh", [128, 2])
        self.rgh0 = self.sb("rgh0", [128, 2, 16])
        self.rgst = self.sb("rgst", [16, 256])
        self.rgout = self.sb("rgout", [128, 2, 16])

        self.Um = {}; self.NEGM = {}; self.SG = {}
        for i, k in enumerate(("p", "s")):
            self.NEGM[k] = self.sb("NEGM" + k, [128, 128])
        self.Umb = {k: self.sb("Umb" + k, [128, 128], BF16) for k in ("p", "s")}
        self.SGb = {k: self.sb("SGb" + k, [128, 128], BF16) for k in ("p", "s")}
        self.dsplit = self.sb("dsplit", [128, 3, 8], BF16)
        self.dres = self.sb("dres", [128, 8])
        self.SeqSel = self.sb("SeqSel", [128, 16])
        self.SeqSelb = self.sb("SeqSelb", [128, 16], BF16)
        self.onesb = self.sb("onesb", [128, 128], BF16)
        self.smtok = self.sb("smtok", [128, NB // 128, 12])
        self.ssp = self.sb("ssp", [128, 16])
        self.ssD = self.sb("ssD", [128, 2])
        self.ssnw = self.sb("ssnw", [128, 2])
        self.H0T = self.sb("H0T", [128, 16, 2, 64], BF16)
        self.HTm = self.sb("HTm", [128, 2, 64])
        self.HTb = self.sb("HTb", [128, 2, 64], BF16)
        self.cq = [self.sb(f"cq{i}", [128, NB], BF16) for i in range(6)]
        self.cf = [self.sb(f"cf{i}", [128, NB]) for i in range(4)]
        self.csm = self.sb("csm", [128, 32])
        self.ctok = self.sb("ctok", [128, 512], BF16)
        self.csc = [self.sb(f"csc{i}", [128, 32]) for i in range(2)]
        self.cdsplit = self.sb("cdsplit", [128, 3, 8], BF16)
        self.cdres = self.sb("cdres", [128, 8])
        self.bw = [self.sb(f"bw{i}", [128, 512]) for i in range(2)]
        self.gdp = self.sb("gdp", [128, 16])
        self.gdnw = self.sb("gdnw", [128, 1])
        self.POSM = {k: self.sb("POSM" + k, [128, 128]) for k in ("p", "s")}
        self.blockones = self.sb("blockones", [128, 128], BF16)
        self.gM = [self.sb(f"gM{i}", [128, 4, 128], BF16) for i in range(3)]
        self.Moff = self.sb("Moff", [128, 4, 128], BF16)
        self.gR = self.sb("gR", [128, 4, 128], BF16)
        self.D32b = self.sb("D32b", [128, 128], BF16)
        self.gN = [self.sb(f"gN{i}", [128, 4, 128], BF16) for i in range(2)]
        self.gX = [self.sb(f"gX{i}", [128, 4, 128], BF16) for i in range(2)]
        self.gW = [self.gM[0], self.gM[1]]
        self.nGT = self.gN[0]
        self.QKG = self.sb("QKG", [128, 4, 128], BF16)
        self.XTb = self.sb("XTb", [128, 4, 128], BF16)
        self.Val = self.sb("Val", [128, 256])
        self.qdec = self.sb("qdec", [128, 4, 128], BF16)
        self.kdec = self.sb("kdec", [128, 4, 64], BF16)
        self.wtok = self.sb("wtok", [128, 4, 64], BF16)
        self.gsq = self.sb("gsq", [128, NB], BF16)
        self.Sm = self.sb("Sm", [128, 2, 64])
        self.Sb = self.sb("Sb", [128, 2, 64], BF16)
        self.s5Sel = self.sb("s5Sel", [128, 16, 128], BF16)
        self.s5SelU = self.sb("s5SelU", [128, 16, 32], BF16)
        self.s5Toep = self.sb("s5Toep", [128, 16, 128], BF16)
        self.s5RT = self.sb("s5RT", [128, 2, 16, 64], BF16)
        self.s5P = self.sb("s5P", [128, 2, 8, 128], BF16)
        self.s5E = self.sb("s5E", [128, 2, 8, 32])
        self.s5par = self.sb("s5par", [128, 8, 8])
        self.s5sm = self.sb("s5sm", [128, 8, 8])
        self.s5H = self.sb("s5H", [128, 2, 8])
        self.s5Hst = self.sb("s5Hst", [128, 2, 8, 32], BF16)
        self.s5Uf = self.sb("s5Uf", [128, 16, 32], BF16)
        self.s5Yf = self.sb("s5Yf", [128, 16, 32], BF16)
        self.s5uT = [self.sb(f"s5uT{i}", [96, NB], BF16) for i in range(3)]
        self.s5glu = self.sb("s5glu", [128, 2, 256], BF16)
        self.s5gb = self.sb("s5gb", [128, 2])
        self.s5Dfold = self.sb("s5Dfold", [128, 16])
        self.Cmask = self.sb("Cmask", [128, 128])
        self.s5q = self.sb("s5q", [128, 64])
        self.s5qi = self.sb("s5qi", [128, 64], I32)
        self.itmp = self.sb("itmp", [128, 256], I32)
        self.df = [self.sb(f"df{i}", [128, NB]) for i in range(6)]
        self.dbf = [self.sb(f"dbf{i}", [128, NB], BF16) for i in range(2)]
        self.tok = self.sb("tok", [128, 512], BF16)
        self.sc = [self.sb(f"sc{i}", [128, 32]) for i in range(2)]
        self.Wt = self.sb("Wt", [128, 4, 128], BF16)
        self.Cdec = self.sb("Cdec", [128, 4, 128], BF16)
        self.Bdec = self.sb("Bdec", [128, 4, 64], BF16)
        self.etotb = self.sb("etotb", [64, 4, 16])

    def bexp(self, hf):
        return self.gX[hf].v().rr("p h (a n) -> p (h a) n", n=64)

    def hnat(self, hh):
        src = self.xrt[1][0:64, :] if hh == 0 else self.ybuf[0][0:64, :]
        return src.rr("p (b n) -> p b n", n=64)

    def ps(self):
        cur = getattr(self, "cur", None)
        if cur == "C":
            b = self.psb[self.psC % 3]
            self.psC += 1
        elif cur == "B":
            b = self.psb[3 + self.psO % 2]
            self.psO += 1
        elif cur == "S1":
            b = self.psb[self.psS % 5]
            self.psS += 1
        elif cur == "D":
            b = self.psb[5 + self.psD % 2]
            self.psD += 1
        else:
            b = self.psb[self.ps_i % 7]
            self.ps_i += 1
        b.t.gen += 1
        return Buf(b.tensor, b.t.name, t=b.t, gen=b.t.gen)

    def _consts(self):
        nc, S = self.nc, self.S
        self.identf = self.sb("identf", [128, 128])
        self.ident = self.sb("ident", [128, 128], BF16)
        S.memset("pool", self.identf.v(), 1.0)
        S.emit("pool", lambda: nc.gpsimd.affine_select(out=self.identf.tensor[:], in_=self.identf.tensor[:], pattern=[[-1, 128]],
                                                       compare_op=ALU.is_equal, fill=0.0, base=0, channel_multiplier=1),
               [self.identf.v()], [self.identf.v()])
        S.copy("dve", self.ident.v(), self.identf.v())
        class _V:
            def __init__(s_, view):
                s_.view = view
            def v(s_):
                return s_.view
        for i, k in enumerate(("p", "s")):
            self.Um[k] = _V(self.ftmp[i][:, 0:128])
            self.SG[k] = _V(self.ftmp[2 + i][:, 0:128])
        S.memset("pool", self.onesb.v(), 1.0)
        asel = lambda buf, pat, cm, base: S.emit("pool", lambda: nc.gpsimd.affine_select(
            out=buf.ap, in_=buf.ap, pattern=pat, compare_op=ALU.is_ge, fill=0.0, base=base, channel_multiplier=cm), [buf], [buf])
        S.memset("pool", self.SG["p"].v(), 1.0)
        S.memset("pool", self.SG["s"].v(), 1.0)
        sg3 = self.SG["s"].v().rr("p (b j) -> p b j", j=8)
        asel(sg3, [[-8, 16], [0, 8]], 1, 0)
        asel(sg3, [[8, 16], [0, 8]], -1, 7)
        S.memset("pool", self.Um["p"].v(), 1.0)
        asel(self.Um["p"].v(), [[1, 128]], -1, 0)
        S.tt("pool", self.Um["s"].v(), self.Um["p"].v(), self.SG["s"].v(), ALU.mult)
        for k in ("p", "s"):
            S.ts("pool", self.NEGM[k].v(), self.Um[k].v(), -NEG, ALU.mult, NEG, ALU.add)
        for k in ("p", "s"):
            S.copy("pool", self.Umb[k].v(), self.Um[k].v())
            S.copy("pool", self.SGb[k].v(), self.SG[k].v())
        for k in ("p", "s"):
            S.memset("pool", self.POSM[k].v(), 1.0)
            asel(self.POSM[k].v(), [[-1, 128]], 1, -1)
            S.tt("pool", self.POSM[k].v(), self.POSM[k].v(), self.SG[k].v(), ALU.mult)
            S.ts("pool", self.POSM[k].v(), self.POSM[k].v(), NEG, ALU.mult, -NEG, ALU.add)
        bo = _V(self.ftmp[4][:, 0:128])
        S.memset("pool", bo.v(), 1.0)
        bo3 = bo.v().rr("p (b j) -> p b j", j=64)
        asel(bo3, [[-64, 2], [0, 64]], 1, 0)
        asel(bo3, [[64, 2], [0, 64]], -1, 63)
        S.copy("pool", self.blockones.v(), bo.v())
        S.memset("pool", bo.v(), 1.0)
        bo4 = bo.v().rr("p (b j) -> p b j", j=32)
        asel(bo4, [[-32, 4], [0, 32]], 1, 0)
        asel(bo4, [[32, 4], [0, 32]], -1, 31)
        S.copy("pool", self.D32b.v(), bo.v())
        sel = self.s5Sel
        S.memset("pool", sel.v(), 1.0)
        b0 = sel[0:32, :, :].rr("p (a s) c -> p a s c", a=2)
        S.emit("pool", lambda: nc.gpsimd.affine_select(out=b0.ap, in_=b0.ap, pattern=[[16, 2], [-16, 8], [1, 128]], compare_op=ALU.is_equal,
                                                       fill=0.0, base=0, channel_multiplier=-1), [b0], [b0])
        asel(b0, [[-16, 2], [0, 8], [0, 128]], 1, 0)
        asel(b0, [[16, 2], [0, 8], [0, 128]], -1, 15)
        S.copy("pool", sel[32:64, :, :], sel[0:32, :, :])
        S.copy("pool", sel[64:96, :, :], sel[0:32, :, :])
        su = self.s5SelU.v().rr("p (a s) c -> p a s c", a=2)
        S.memset("pool", su, 1.0)
        S.emit("pool", lambda: nc.gpsimd.affine_select(out=su.ap, in_=su.ap, pattern=[[16, 2], [-16, 8], [-1, 32]], compare_op=ALU.is_equal,
                                                       fill=0.0, base=0, channel_multiplier=1), [su], [su])
        asel(su, [[-16, 2], [0, 8], [1, 32]], 0, 0)
        asel(su, [[16, 2], [0, 8], [-1, 32]], 0, 15)
        S.memset("pool", self.Cmask.v(), 1.0)
        asel(self.Cmask.v().rr("p (t i) -> p t i", i=16), [[16, 8], [0, 16]], -1, 15)
        for seg, (base, step) in enumerate(((7, -1), (1, 1), (0, -1), (0, 1))):
            S.emit("pool", lambda seg=seg, step=step, base=base: nc.gpsimd.iota(self.s5qi.tensor[:, seg * 8:(seg + 1) * 8], pattern=[[step, 8]], base=base,
                                                  channel_multiplier=0), [], [self.s5qi.v()])
        S.emit("pool", lambda: nc.gpsimd.iota(self.s5qi.tensor[:, 32:64], pattern=[[1, 32]], base=1, channel_multiplier=0),
               [], [self.s5qi.v()])
        S.copy("pool", self.s5q.v(), self.s5qi.v())
        S.memset("pool", self.SeqSel.v(), 1.0)
        asel(self.SeqSel.v(), [[-8, 16]], 1, 0)
        asel(self.SeqSel.v(), [[8, 16]], -1, 7)
        S.copy("pool", self.SeqSelb.v(), self.SeqSel.v())

    def dbg(self, name, view):
        if name in self.DBG:
            self.S.dma_out("sp", self.DBG[name], view)
            self.out_tiles.extend(view.ts)

    def _load_weights(self, l):
        S, I = self.S, self.I
        for (g0, g1, buf) in self.WbG:
            for k in range(8):
                c0 = g0
                while c0 < g1:
                    w = min(512, g1 - c0)
                    S.dma_in("pool", buf[:, k, c0 - g0:c0 - g0 + w], I["w_in"][l, k * 128:(k + 1) * 128, c0:c0 + w])
                    c0 += w
        for k in range(8):
            for hf in range(2):
                S.dma_in("pool", self.Wo[:, k, hf * 512:(hf + 1) * 512], I["w_out"][l, k * 128:(k + 1) * 128, hf * 512:(hf + 1) * 512])
        S.copy("dve", self.Wsm[:, :, 0:4], self.wb(None, C_BDT, 4))
        S.copy("dve", self.Wsm[:, :, 4:12], self.wb(None, C_CBETA, 8))
        S.dma_in("sp", self.lng.v(), I["ln_g"][l:l + 1, :].to_broadcast([128, D]))
        S.dma_in("sp", self.lnb.v(), I["ln_b"][l:l + 1, :].to_broadcast([128, D]))
        for (nm, c0, nch) in (("rg", 0, 2), ("ssd", 2, 4), ("gdn", 6, 6)):
            for j in range(4):
                S.dma_in("sp", self.cw[:, j, c0:c0 + nch], I[nm + "_conv_w"][l, j].rearrange("(c p) -> p c", p=128),
                         allow_slow_non_contiguous=True)
            S.dma_in("sp", self.cbias[:, c0:c0 + nch], I[nm + "_conv_b"][l].rearrange("(c p) -> p c", p=128),
                     allow_slow_non_contiguous=True)
        if "A" in self.mixers:
            self._load_A(l)
        if "B" in self.mixers:
            self._load_B(l)
        if "C" in self.mixers:
            self._load_C(l)
        if "D" in self.mixers:
            self._load_D(l)

    def _layer(self, l):
        self._load_weights(l)
        blocks = [Blk(i, False) for i in range(L_P // NB)] + [Blk(0, True)]
        for blk in blocks:
            self._block(l, blk)

    def _xsrc(self, l, blk):
        if l == 0:
            return self.I["xs"] if blk.sample else self.I["xp"][blk.tok0:blk.tok0 + blk.N, :]
        return self.x1[blk.tok0:blk.tok0 + blk.N, :]

    def _ydst(self, l, blk):
        if l == DEPTH - 1:
            return self.O["ys"] if blk.sample else self.O["yp"][blk.tok0:blk.tok0 + blk.N, :]
        return self.x1[blk.tok0:blk.tok0 + blk.N, :]

    def _block(self, l, blk):
        S = self.S
        N = blk.N
        src = self._xsrc(l, blk).rearrange("(t p) d -> p t d", p=128)
        self.xr_i = getattr(self, "xr_i", 0) + 1
        t0 = blk.tok0 // 128
        for ti in range(blk.ntile):
            S.dma_in("sp", self.xrt[ti].v(), src[:, ti, :], dram_reads=([self.x1_t[t0 + ti]] if l > 0 else ()))
        for ti in range(blk.ntile):
            xb = self.xbf[0]
            S.copy("act", xb.v(), self.xrt[ti].v())
            for k in range(8):
                S.tr(self.pst[:, k * 128:(k + 1) * 128], xb[:, k * 128:(k + 1) * 128], self.ident.v())
            S.copy("act", self.xT[:, :, ti * 128:(ti + 1) * 128], self.pst.v().rr("p (k t) -> p k t", k=8))
        if blk.last or blk.sample:
            self._conv_cache_out(l, blk)
        if "B" in self.mixers or "C" in self.mixers:
            for ti in range(blk.ntile):
                p = self.ps()
                for k in range(8):
                    S.mm(p[:, 0:12], self.xT[:, k, ti * 128:(ti + 1) * 128], self.Wsm[:, k, :], start=(k == 0), stop=(k == 7))
                S.copy("dve", self.smtok[:, ti, :], p[:, 0:12])
        self._conv_phase(l, blk)
        gens = {}
        for mi, m in enumerate("ABCD"):
            if m in self.mixers:
                gens[m] = getattr(self, "_mixer_" + m)(l, blk)
            else:
                for i in (2 * mi, 2 * mi + 1):
                    S.memset("pool", self.mixT[i][:, 0:N], 0.0)
        if not INTERLEAVE or "C" not in gens or (blk.sample and "D" not in gens):
            for m in "ABCD":
                if m in gens:
                    for _ in gens[m]:
                        pass
        elif blk.sample:
            streams = [("S1", itertools.chain(*[gens[m] for m in "ABC" if m in gens])), ("D", gens["D"])]
            alive = list(streams)
            while alive:
                for item in list(alive):
                    self.cur = item[0]
                    try:
                        next(item[1])
                    except StopIteration:
                        alive.remove(item)
            self.cur = None
        else:
            streams = [("C", gens["C"]), ("B", itertools.chain(*[gens[m] for m in "AB" if m in gens]))]
            if "D" in gens:
                streams.append(("D", gens["D"]))
            alive = list(streams)
            while alive:
                for item in list(alive):
                    self.cur = item[0]
                    try:
                        next(item[1])
                    except StopIteration:
                        alive.remove(item)
            self.cur = None
        dst = self._ydst(l, blk).rearrange("(t p) d -> p t d", p=128)
        for ti in range(blk.ntile):
            self._outproj_tile(l, blk, ti, dst)

    def _conv_phase(self, l, blk):
        S = self.S
        N = blk.N
        jobs = []
        if "A" in self.mixers:
            jobs += [("A", fc) for fc in range(2)]
        if "B" in self.mixers:
            jobs += [("B", i) for i in range(4)]
        if "C" in self.mixers:
            jobs += [("C", i) for i in range(6)]
        for (m, i) in jobs:
            if m == "A":
                pc = self.conv_chunk(blk, l, i, C_AX + i * 128, None)
                S.act(self.xcA[i][:, 0:N], pc, AF.Identity, bias=self.cbias[:, i:i + 1], scale=1.0)
                S.act(self.xcbA[i][:, 0:N], pc, AF.Identity, bias=self.cbias[:, i:i + 1], scale=1.0)
            elif m == "B":
                cc = 2 + i
                pc = self.conv_chunk(blk, l, cc, C_BX + i * 128, None)
                S.act(self.btmp[i][:, 0:N], pc, AF.Silu, bias=self.cbias[:, cc:cc + 1], scale=1.0)
            else:
                cc = 6 + i
                pc = self.conv_chunk(blk, l, cc, C_CQ + i * 128, None)
                S.act(self.cq[i][:, 0:N], pc, AF.Silu, bias=self.cbias[:, cc:cc + 1], scale=1.0)

    def wb(self, k, c0, w):
        for (g0, g1, buf) in self.WbG:
            if g0 <= c0 and c0 + w <= g1:
                return buf[:, :, c0 - g0:c0 - g0 + w] if k is None else buf[:, k, c0 - g0:c0 - g0 + w]
        raise AssertionError((c0, w))

    def zmm(self, blk, c0, width, out_ps):
        for k in range(8):
            self.S.mm(out_ps, self.wb(k, c0, width), self.xT[:, k, 0:blk.N], start=(k == 0), stop=(k == 7))

    def _conv_cache_out(self, l, blk):
        S = self.S
        pre = "s" if blk.sample else "p"
        if blk.sample:
            M = 48
            for k in range(8):
                S.copy("dve", self.xTl[:, k, :].rr("p (b j) -> p b j", j=3),
                       self.xT[:, k, 0:128].rr("p (b j) -> p b j", j=8)[:, :, 5:8])
            lhs = lambda k: self.xTl[:, k, :]
        else:
            M = 3
            lhs = lambda k: self.xT[:, k, NB - 3:NB]
        for (nm, c0, w) in (("_rg_conv", C_AX, 256), ("_ssd_conv", C_BX, 512), ("_gdn_conv", C_CQ, 512), ("_gdn_conv2", C_CQ + 512, 256)):
            p = self.ps()
            for k in range(8):
                S.mm(p[0:M, 0:w], lhs(k), self.wb(k, c0, w), start=(k == 0), stop=(k == 7))
            st = self.ybuf[0]
            self.cst_i += 1
            S.copy("act", st[0:M, 0:w], p[0:M, 0:w])
            if nm == "_gdn_conv2":
                dst = self.O[pre + "_gdn_conv"][l].rearrange("b j c -> (b j) c")[:, 512:768]
            elif nm == "_gdn_conv":
                dst = self.O[pre + "_gdn_conv"][l].rearrange("b j c -> (b j) c")[:, 0:512]
            else:
                dst = self.O[pre + nm][l].rearrange("b j c -> (b j) c")
            S.dma_out("sp", dst, st[0:M, 0:w])
            self.out_tiles.append(st.t)

    def _outproj_tile(self, l, blk, ti, dst):
        S, nc = self.S, self.nc
        pa, pb = self.ps(), self.ps()
        tsl = slice(ti * 128, (ti + 1) * 128)
        for half, p in ((0, pa), (1, pb)):
            for k in range(8):
                S.mm(p.v(), self.mixT[k][:, tsl], self.Wo[:, k, half * 512:(half + 1) * 512], start=(k == 0), stop=(k == 7))
        self.yb_i = getattr(self, "yb_i", 0) + 1
        r = self.ybuf[self.yb_i % NYBUF]
        st = self.lnstat[ti % 2]
        for half, p in ((0, pa), (1, pb)):
            hs = slice(half * 512, (half + 1) * 512)
            S.stt(r[:, hs], self.xrt[ti][:, hs], ALPHA, p.v(), ALU.mult, ALU.add)
            S.emit("dve", lambda half=half, hs=hs: nc.vector.bn_stats(out=st.tensor[:, half * 6:(half + 1) * 6], in_=r.tensor[:, hs]), [r.v()], [st.v()])
        S.emit("dve", lambda: nc.vector.bn_aggr(out=st.tensor[:, 12:14], in_=st.tensor[:, 0:12]), [st.v()], [st.v()])
        self.rsqrt(st[:, 14:15], st[:, 13:14], 1.0, self.eps5())
        S.stt(st[:, 15:16], st[:, 12:13], -1.0, st[:, 14:15], ALU.mult, ALU.mult)
        y = r
        S.act(y.v(), r.v(), AF.Identity, bias=st[:, 15:16], scale=st[:, 14:15])
        S.tt("dve", y.v(), y.v(), self.lng.v(), ALU.mult)
        S.tt("dve", y.v(), y.v(), self.lnb.v(), ALU.add)
        S.dma_out("sp", dst[:, ti, :], y.v(), dram_writes=([self.x1_t[blk.tok0 // 128 + ti]] if l < DEPTH - 1 else ()))
        if l == 0 and "x1" in self.DBG:
            r0 = blk.tok0 + ti * 128
            S.dma_out("sp", self.DBG["x1"][r0:r0 + 128, :], y.v())
        self.out_tiles.append(y.t)

    def eps5(self):
        if not hasattr(self, "_eps5"):
            self._eps5 = self.sb("eps5", [128, 1])
            self.S.memset("pool", self._eps5.v(), 1e-5)
        return self._eps5.v()

    def conv_chunk(self, blk, l, cc, zc0, out_ps, m0=0, m1=128):
        S = self.S
        N = blk.N
        cp = self.cpad[cc]
        acc = self.cacc[self.cd_i % 4]
        self.cd_i += 1
        pz = self.ps()
        self.zmm(blk, zc0, 128, pz[:, 0:N])
        if blk.sample:
            cpv = cp[:, 0:176].rr("p (g l) -> p g l", l=11)
            ph = self.ps()
            ch = self.chist[cc % 2]
            nm, c0 = (("cache_rglru_conv", cc * 128) if cc < 2 else ("cache_ssd_conv", (cc - 2) * 128) if cc < 6
                      else ("cache_gdn_conv", (cc - 6) * 128))
            S.dma_in("sp", ch.v(), self.I[nm][l].rearrange("b j c -> (b j) c")[:, c0:c0 + 128])
            S.tr(ph[:, 0:48], ch[0:48, :], self.identf[0:48, 0:48])
            S.copy("dve", cpv[:, :, 0:3], ph[:, 0:48].rr("p (b j) -> p b j", j=3))
            S.copy("act", cpv[:, :, 3:11], pz[:, 0:N].rr("p (b j) -> p b j", j=8))
            rhs = lambda j: cpv[:, :, j:j + 8]
            accv = acc[:, 0:N].rr("p (b j) -> p b j", j=8)
        else:
            if blk.first:
                S.memset("pool", cp[:, 0:3], 0.0)
            else:
                S.copy("dve", cp[:, 0:3], cp[:, NB:NB + 3])
            S.copy("act", cp[:, 3:NB + 3], pz[:, 0:N])
            rhs = lambda j: cp[:, j:j + NB]
            accv = acc[:, 0:N]
        S.act(accv, rhs(0), AF.Copy, scale=self.cw[:, 0, cc:cc + 1])
        for j in range(1, 4):
            S.stt(accv, rhs(j), self.cw[:, j, cc:cc + 1], accv, ALU.mult, ALU.add)
        return acc[:, 0:N]

    def _load_A(self, l):
        S, I = self.S, self.I
        rgWst = self.wtmp[0].v().rr("p (a b c) -> p a b c", a=2, b=2)
        S.memset("pool", rgWst, 0.0)
        for gi, nm in enumerate(("rg_gate_a_w", "rg_gate_x_w")):
            for fc in range(2):
                for h2 in range(2):
                    S.dma_in("sp", rgWst[h2 * 64:(h2 + 1) * 64, fc, gi, h2 * 64:(h2 + 1) * 64], I[nm][l, fc * 2 + h2])
        S.copy("dve", self.rgW.v(), rgWst)
        for i, nm in enumerate(("rg_gate_a_b", "rg_gate_x_b", "rg_lambda")):
            S.dma_in("sp", self.rgp[:, :, i], I[nm][l].rearrange("(c p) -> p c", p=128), allow_slow_non_contiguous=True)
        S.act(self.rgp[:, :, 3], self.rgp[:, :, 2], AF.Exp, scale=-1.0)
        S.act(self.rgp[:, :, 3], self.rgp[:, :, 3], AF.Ln, bias=1.0)
        S.ts("dve", self.rgp[:, :, 4], self.rgp[:, :, 3], -16.0, ALU.mult)
        S.ts("dve", self.rgp[:, :, 3], self.rgp[:, :, 3], -8.0, ALU.mult)
        S.dma_in("sp", self.rgst.v(), I["state_rglru"][l])
        for fc in range(2):
            p = self.ps()
            S.tr(p[:, 0:16], self.rgst[0:16, fc * 128:(fc + 1) * 128], self.identf[0:16, 0:16])
            S.copy("dve", self.rgh0[:, fc, :], p[:, 0:16])

    def _mixer_A(self, l, blk):
        S = self.S
        N = blk.N
        AT = self.ftmp
        for fc in range(2):
            xc = self.xcA[fc]
            xcb = self.xcbA[fc]
            yield
            pr, pi = self.ps(), self.ps()
            S.mm(pr[:, 0:N], self.rgW[:, fc, 0, :], xcb[:, 0:N])
            S.mm(pi[:, 0:N], self.rgW[:, fc, 1, :], xcb[:, 0:N])
            gr, gi = AT[1], AT[2]
            S.act(gr[:, 0:N], pr[:, 0:N], AF.Sigmoid, bias=self.rgp[:, fc, 0:1], scale=1.0)
            S.act(gi[:, 0:N], pi[:, 0:N], AF.Sigmoid, bias=self.rgp[:, fc, 1:2], scale=1.0)
            a, a2 = AT[3], AT[4]
            S.act(a[:, 0:N], gr[:, 0:N], AF.Exp, scale=self.rgp[:, fc, 3:4])
            S.act(a2[:, 0:N], gr[:, 0:N], AF.Exp, scale=self.rgp[:, fc, 4:5])
            S.act(a2[:, 0:N], a2[:, 0:N], AF.Ln, bias=1.0, scale=-1.0)
            S.act(a2[:, 0:N], a2[:, 0:N], AF.Exp, scale=0.5)
            bb = AT[5]
            S.tt("dve", bb[:, 0:N], a2[:, 0:N], gi[:, 0:N], ALU.mult)
            S.tt("dve", bb[:, 0:N], bb[:, 0:N], xc[:, 0:N], ALU.mult)
            h = AT[6]
            if blk.sample:
                a3 = a[:, 0:N].rr("p (b j) -> p b j", j=8)
                b3 = bb[:, 0:N].rr("p (b j) -> p b j", j=8)
                tmp = AT[7]
                S.tt("dve", tmp[:, 0:16], a3[:, :, 0], self.rgh0[:, fc, :], ALU.mult)
                S.tt("dve", b3[:, :, 0], b3[:, :, 0], tmp[:, 0:16], ALU.add)
                S.memset("dve", a3[:, :, 0], 0.0)
                S.scan(h[:, 0:N], a[:, 0:N], bb[:, 0:N], 0.0)
            else:
                init = 0.0 if blk.first else self.rgh[:, fc:fc + 1]
                S.scan(h[:, 0:N], a[:, 0:N], bb[:, 0:N], init)
                S.copy("dve", self.rgh[:, fc:fc + 1], h[:, N - 1:N])
            yield
            pg = self.ps()
            self.zmm(blk, C_AG + fc * 128, 128, pg[:, 0:N])
            sg = AT[8]
            S.act(sg[:, 0:N], pg[:, 0:N], AF.Silu)
            S.tt("dve", self.mixT[fc][:, 0:N], h[:, 0:N], sg[:, 0:N], ALU.mult)
            if blk.last:
                S.dma_out("sp", self.O["p_rg"][l, 0, fc * 128:(fc + 1) * 128].rearrange("(p o) -> p o", o=1), self.rgh[:, fc:fc + 1])
                self.out_tiles.append(self.rgh.t)
            if blk.sample:
                h3 = h[:, 0:N].rr("p (b j) -> p b j", j=8)
                S.copy("dve", self.rgout[:, fc, :], h3[:, :, 7])
                S.dma_out("sp", self.O["s_rg"][l][:, fc * 128:(fc + 1) * 128].rearrange("b p -> p b"), self.rgout[:, fc, :],
                          allow_slow_non_contiguous=True)
                self.out_tiles.append(self.rgout.t)

    def _load_B(self, l):
        S, I = self.S, self.I
        S.dma_in("sp", self.ssp[:, 0:4], I["ssd_dt_bias"][l:l + 1, :].to_broadcast([128, 4]))
        S.dma_in("sp", self.ssp[:, 4:8], I["ssd_a_log"][l:l + 1, :].to_broadcast([128, 4]))
        S.act(self.ssp[:, 8:12], self.ssp[:, 4:8], AF.Exp)
        S.ts("dve", self.ssp[:, 8:12], self.ssp[:, 8:12], -1.0, ALU.mult)
        for h in range(4):
            h2, fc = h % 2, h // 2
            S.dma_in("sp", self.ssD[64 * h2:64 * h2 + 64, fc:fc + 1], I["ssd_d"][l:l + 1, h:h + 1].to_broadcast([64, 1]))
        S.dma_in("sp", self.ssnw.v(), I["ssd_norm_w"][l].rearrange("(c p) -> p c", p=128), allow_slow_non_contiguous=True)

    def _load_B_sample(self, l):
        S, I = self.S, self.I
        for h in range(4):
            g, h2 = h // 2, h % 2
            if h2 == 0:
                for hh in range(2):
                    S.dma_in("sp", self.hnat(hh), I["state_ssd"][l][:, 2 * g + hh].rearrange("b p n -> p b n"))
            for bb in range(2):
                p = self.ps()
                for j in range(8):
                    S.tr(p[0:64, j * 64:(j + 1) * 64], self.hnat(h2)[:, bb * 8 + j, :], self.identf[0:64, 0:64])
                S.copy("act" if bb else "dve", self.H0T[64 * g:64 * g + 64, bb * 8:(bb + 1) * 8, h2, :],
                       p[0:64, :].rr("n (b p) -> n b p", p=64))

    def chunk_decay(self, blk, ti, dta, sc, pAC, pACm, dsplit=None, dres=None):
        S = self.S
        mk = "s" if blk.sample else "p"
        dsplit = dsplit or self.dsplit
        dres = dres or self.dres
        self.split3(dta, 4, dsplit, dres)
        p = self.ps()
        for i in range(3):
            S.mm(p[:, 0:4], self.Umb[mk].v(), dsplit[:, i, 0:4], start=(i == 0), stop=(i == 2))
        for i in range(3):
            S.mm(p[:, 4:8], self.SGb[mk].v(), dsplit[:, i, 0:4], start=(i == 0), stop=(i == 2))
        S.copy("dve", sc[:, 0:8], p[:, 0:8])
        S.ts("dve", sc[:, 8:12], sc[:, 0:4], -1.0, ALU.mult)
        for h in range(4):
            hs = slice(h * 128, (h + 1) * 128)
            for i in range(3):
                S.mm(pAC[:, hs], dsplit[:, i, h:h + 1].bc([128, 128]), self.Umb[mk].v(), start=(i == 0), stop=(i == 2))

    def split3(self, src, w, dsplit, dres):
        S = self.S
        S.copy("dve", dsplit[:, 0, 0:w], src)
        S.tt("dve", dres[:, 0:w], src, dsplit[:, 0, 0:w], ALU.subtract)
        S.copy("dve", dsplit[:, 1, 0:w], dres[:, 0:w])
        S.tt("dve", dres[:, 0:w], dres[:, 0:w], dsplit[:, 1, 0:w], ALU.subtract)
        S.copy("dve", dsplit[:, 2, 0:w], dres[:, 0:w])

    def _mixer_B(self, l, blk):
        S = self.S
        N = blk.N
        xsb = [self.btmp[0], self.btmp[1]]
        BT, CT = self.btmp[2], self.btmp[3]
        dt = self.ftmp[0]
        dta = self.ftmp[1]
        nt = blk.ntile
        dt3 = dt[:, 0:nt * 4].rr("p (t h) -> p t h", h=4)
        dta3 = dta[:, 0:nt * 4].rr("p (t h) -> p t h", h=4)
        for ti in range(nt):
            S.tt("dve", dt3[:, ti, :], self.smtok[:, ti, 0:4], self.ssp[:, 0:4], ALU.add)
        S.act(dt[:, 0:nt * 4], dt[:, 0:nt * 4], AF.Exp)
        S.act(dt[:, 0:nt * 4], dt[:, 0:nt * 4], AF.Ln, bias=1.0)
        for ti in range(nt):
            S.tt("dve", dta3[:, ti, :], dt3[:, ti, :], self.ssp[:, 8:12], ALU.mult)
        yb = [self.ftmp[2], self.ftmp[3]]
        if blk.first:
            S.memset("pool", self.HTm.v(), 0.0)
            S.memset("pool", self.HTb.v(), 0.0)
        if blk.sample:
            self._load_B_sample(l)
        for ti in range(nt):
            tsl = slice(ti * 128, (ti + 1) * 128)
            sc = self.sc[ti % 2]
            for i, srcb in enumerate((xsb[0], xsb[1], BT)):
                S.tr(self.pst[:, i * 128:(i + 1) * 128], srcb[:, tsl], self.ident.v())
            S.copy("act", self.tok[:, 0:384], self.pst[:, 0:384])
            yield
            pAC, pACm = self.ps(), None
            self.chunk_decay(blk, ti, dta3[:, ti, :], sc, pAC, pACm)
            S.tt("dve", sc[:, 12:16], sc[:, 4:8], sc[:, 0:4], ALU.subtract)
            S.act(sc[:, 12:16], sc[:, 12:16], AF.Exp)
            S.tt("dve", sc[:, 12:16], sc[:, 12:16], dt3[:, ti, :], ALU.mult)
            S.act(sc[:, 16:20], sc[:, 4:8], AF.Exp)
            Edec, LT = self.bw
            S.act(Edec.v(), pAC.v(), AF.Exp)
            for h in range(4):
                hs = slice(h * 128, (h + 1) * 128)
                S.stt(LT[:, hs], pAC[:, hs], sc[:, 8 + h:9 + h], self.NEGM["s" if blk.sample else "p"].v(), ALU.add, ALU.add)
            S.act(LT.v(), LT.v(), AF.Exp)
            yield
            pG = [self.ps(), self.ps()]
            for g in range(2):
                S.mm(pG[g][:, 0:128], BT[64 * g:64 * g + 64, tsl], CT[64 * g:64 * g + 64, tsl])
            for h in range(4):
                hs = slice(h * 128, (h + 1) * 128)
                S.stt(self.Wt[:, h, :], LT[:, hs], dt3[:, ti, h:h + 1], pG[h // 2][:, 0:128], ALU.mult, ALU.mult)
            for g in range(2):
                gs = slice(64 * g, 64 * g + 64)
                S.tt("dve", self.Cdec[gs, 2 * g:2 * g + 2, :], Edec[gs, :].rr("p (h t) -> p h t", h=4)[:, 2 * g:2 * g + 2, :],
                     CT[gs, tsl].rr("p (o t) -> p o t", o=1).bc([64, 2, 128]), ALU.mult)
            for h in range(4):
                S.ts("dve", self.Bdec[:, h, :], self.tok[:, 256 + (h // 2) * 64:256 + (h // 2) * 64 + 64], sc[:, 12 + h:13 + h], ALU.mult)
            yield
            py = [self.ps(), self.ps()]
            for h in range(4):
                g, h2, fc = h // 2, h % 2, h // 2
                gs = slice(64 * g, 64 * g + 64)
                out = py[fc][64 * h2:64 * h2 + 64, 0:128]
                if blk.sample:
                    S.mm(out, self.tok[:, h * 64:(h + 1) * 64], self.Wt[:, h, :], start=True, stop=False)
                    for b in range(16):
                        S.mm(py[fc][64 * h2:64 * h2 + 64, 8 * b:8 * b + 8], self.H0T[gs, b, h2, :], self.Cdec[gs, h, 8 * b:8 * b + 8],
                             start=False, stop=(b == 15))
                else:
                    nostate = blk.first and ti == 0
                    S.mm(out, self.tok[:, h * 64:(h + 1) * 64], self.Wt[:, h, :], start=True, stop=nostate)
                    if not nostate:
                        S.mm(out, self.HTb[gs, h2, :], self.Cdec[gs, h, :], start=False, stop=True)
            for fc in range(2):
                S.stt(yb[fc][:, tsl], xsb[fc][:, tsl], self.ssD[:, fc:fc + 1], py[fc][:, 0:128], ALU.mult, ALU.add)
            if blk.sample:
                yield
                pE = self.ps()
                for h in range(4):
                    for i in range(3):
                        S.mm(pE[0:64, h * 16:(h + 1) * 16], self.dsplit[:, i, h:h + 1].bc([128, 64]), self.SeqSelb.v(),
                             start=(i == 0), stop=(i == 2))
                S.act(self.etotb.v().rr("p h b -> p (h b)"), pE[0:64, 0:64], AF.Exp)
                for h in range(4):
                    if h % 2 == 0:
                        for hh in range(2):
                            S.dma_in("sp", self.hnat(hh), self.I["state_ssd"][l][:, h + hh].rearrange("b p n -> p b n"))
                    for hf in range(2):
                        S.tt("dve", self.bexp(hf), self.Bdec[:, h, :].rr("p (o n) -> p o n", o=1).bc([128, 8, 64]),
                             self.SeqSelb[:, hf * 8:(hf + 1) * 8].rr("p (b o) -> p b o", o=1).bc([128, 8, 64]), ALU.mult)
                    yield
                    pn = [self.ps(), self.ps()]
                    for hf in range(2):
                        S.mm(pn[hf][0:64, :], self.tok[:, h * 64:(h + 1) * 64],
                             self.bexp(hf).rr("p b n -> p (b n)"))
                    for hf in range(2):
                        bs = slice(hf * 8, hf * 8 + 8)
                        S.tt("dve", self.hnat(h % 2)[:, bs, :], self.hnat(h % 2)[:, bs, :],
                             self.etotb[:, h, bs].rr("p (b o) -> p b o", o=1).bc([64, 8, 64]), ALU.mult)
                        S.tt("dve", self.hnat(h % 2)[:, bs, :], self.hnat(h % 2)[:, bs, :], pn[hf][0:64, :].rr("p (b n) -> p b n", n=64), ALU.add)
                    if h % 2 == 1:
                        for hh in range(2):
                            S.dma_out("sp", self.O["s_ssd"][l][:, h - 1 + hh].rearrange("b p n -> p b n"), self.hnat(hh))
                self.out_tiles.extend([self.xrt[1].t, self.ybuf[0].t])
            else:
                yield
                pH = self.ps()
                for h in range(4):
                    g, h2 = h // 2, h % 2
                    gs = slice(64 * g, 64 * g + 64)
                    S.mm(pH[gs, h2 * 64:(h2 + 1) * 64], self.Bdec[:, h, :], self.tok[:, h * 64:(h + 1) * 64])
                for h in range(4):
                    g, h2 = h // 2, h % 2
                    gs = slice(64 * g, 64 * g + 64)
                    S.stt(self.HTm[gs, h2, :], self.HTm[gs, h2, :], sc[gs, 16 + h:17 + h], pH[gs, h2 * 64:(h2 + 1) * 64], ALU.mult, ALU.add)
                S.copy("dve", self.HTb.v(), self.HTm.v())
        if blk.last:
            for h in range(4):
                g, h2 = h // 2, h % 2
                gs = slice(64 * g, 64 * g + 64)
                yield
                p = self.ps()
                S.tr(p[0:64, 0:64], self.HTm[gs, h2, :], self.identf[gs, gs])
                S.copy("act", self.ftmp[9][0:64, h * 64:(h + 1) * 64], p[0:64, 0:64])
            S.dma_out("sp", self.O["p_ssd"][l, 0].rearrange("h p n -> p h n"), self.ftmp[9][0:64, 0:256].rr("p (h n) -> p h n", h=4))
            self.out_tiles.append(self.ftmp[9].t)
        yg = [self.ftmp[6], self.ftmp[7]]
        for fc in range(2):
            yield
            pg = self.ps()
            self.zmm(blk, C_BG + fc * 128, 128, pg[:, 0:N])
            sg = self.ftmp[8]
            S.act(sg[:, 0:N], pg[:, 0:N], AF.Silu)
            S.tt("dve", yg[fc][:, 0:N], yb[fc][:, 0:N], sg[:, 0:N], ALU.mult)
            sq = self.btmp[4 + fc]
            S.act(sq[:, 0:N], yg[fc][:, 0:N], AF.Square)
        yield
        pss = self.ps()
        for fc in range(2):
            S.mm(pss[:, 0:N], self.onesb.v(), self.btmp[4 + fc][:, 0:N], start=(fc == 0), stop=(fc == 1))
        rstd = self.ftmp[9]
        self.rsqrt(rstd[:, 0:N], pss[:, 0:N], 1.0 / 256.0, self.eps6())
        for fc in range(2):
            S.stt(self.mixT[2 + fc][:, 0:N], yg[fc][:, 0:N], self.ssnw[:, fc:fc + 1], rstd[:, 0:N], ALU.mult, ALU.mult)

    def rsqrt(self, out, in_, scale, eps):
        self.S.act(out, in_, AF.Ln, bias=eps, scale=scale)
        self.S.act(out, out, AF.Exp, scale=-0.5)

    def eps6(self):
        if not hasattr(self, "_eps6"):
            self._eps6 = self.sb("eps6", [128, 1])
            self.S.memset("pool", self._eps6.v(), 1e-6)
        return self._eps6.v()

    def _load_C(self, l):
        S, I = self.S, self.I
        S.dma_in("sp", self.gdp[:, 0:4], I["gdn_dt_bias"][l:l + 1, :].to_broadcast([128, 4]))
        S.dma_in("sp", self.gdp[:, 4:8], I["gdn_a_log"][l:l + 1, :].to_broadcast([128, 4]))
        S.act(self.gdp[:, 8:12], self.gdp[:, 4:8], AF.Exp)
        S.ts("dve", self.gdp[:, 8:12], self.gdp[:, 8:12], -1.0, ALU.mult)
        for h2 in range(2):
            S.dma_in("sp", self.gdnw[64 * h2:64 * h2 + 64, :], I["gdn_norm_w"][l].rearrange("(p o) -> p o", o=1))

    def _load_C_sample(self, l):
        S, I = self.S, self.I
        for fc in range(2):
            for hh in range(2):
                S.dma_in("sp", self.hnat(hh), I["state_gdn"][l][:, 2 * fc + hh].rearrange("b k v -> k b v"))
            for hh in range(2):
                S.copy("act" if hh else "dve", self.H0T[64 * hh:64 * hh + 64, :, fc, :], self.hnat(hh))

    def _mixer_C(self, l, blk):
        S = self.S
        N = blk.N
        nt = blk.ntile
        mk = "s" if blk.sample else "p"
        nlev = 3 if blk.sample else 7
        qkv = self.cq
        for i in range(4):
            sq = self.gsq
            S.act(sq[:, 0:N], qkv[i][:, 0:N], AF.Square)
            yield
            pss = self.ps()
            S.mm(pss[:, 0:N], self.blockones.v(), sq[:, 0:N])
            rn = self.cf[2]
            self.rsqrt(rn[:, 0:N], pss[:, 0:N], 1.0, self.eps6())
            if i < 2:
                S.stt(qkv[i][:, 0:N], qkv[i][:, 0:N], 0.125, rn[:, 0:N], ALU.mult, ALU.mult)
            else:
                S.tt("dve", qkv[i][:, 0:N], qkv[i][:, 0:N], rn[:, 0:N], ALU.mult)
        bet, gl = self.csm[:, 0:16], self.csm[:, 16:32]
        bet3 = bet[:, 0:nt * 4].rr("p (t h) -> p t h", h=4)
        g3 = gl[:, 0:nt * 4].rr("p (t h) -> p t h", h=4)
        for ti in range(nt):
            S.act(bet3[:, ti, :], self.smtok[:, ti, 4:8], AF.Sigmoid)
            S.tt("dve", g3[:, ti, :], self.smtok[:, ti, 8:12], self.gdp[:, 0:4], ALU.add)
        S.act(gl[:, 0:nt * 4], gl[:, 0:nt * 4], AF.Exp)
        S.act(gl[:, 0:nt * 4], gl[:, 0:nt * 4], AF.Ln, bias=1.0)
        for ti in range(nt):
            S.tt("dve", g3[:, ti, :], g3[:, ti, :], self.gdp[:, 8:12], ALU.mult)
        oT = [self.cf[0], self.cf[1]]
        if blk.first:
            S.memset("pool", self.Sm.v(), 0.0)
            S.memset("pool", self.Sb.v(), 0.0)
        H4 = [(h, h // 2, h % 2, slice(64 * (h % 2), 64 * (h % 2) + 64), slice(h * 128, (h + 1) * 128)) for h in range(4)]
        for ti in range(nt):
            tsl = slice(ti * 128, (ti + 1) * 128)
            sc = self.csc[ti % 2]
            for i, srcb in enumerate((qkv[2], qkv[3], qkv[4], qkv[5])):
                S.tr(self.pst[:, i * 128:(i + 1) * 128], srcb[:, tsl], self.ident.v())
            S.copy("act", self.ctok[:, 0:512], self.pst[:, 0:512])
            yield
            pAC = self.ps()
            self.chunk_decay(blk, ti, g3[:, ti, :], sc, pAC, None, self.cdsplit, self.cdres)
            S.act(sc[:, 12:16], sc[:, 0:4], AF.Exp)
            S.tt("dve", sc[:, 16:20], sc[:, 12:16], bet3[:, ti, :], ALU.mult)
            S.tt("dve", sc[:, 20:24], sc[:, 4:8], sc[:, 0:4], ALU.subtract)
            S.act(sc[:, 20:24], sc[:, 20:24], AF.Exp)
            S.act(sc[:, 24:28], sc[:, 4:8], AF.Exp)
            S.ts("dve", sc[:, 28:32], bet3[:, ti, :], -1.0, ALU.mult)
            Edec, GT, gam = self.wtmp[0], self.wtmp[1], self.wtmp[2]
            S.act(Edec.v(), pAC.v(), AF.Exp)
            for (h, fc, h2, hb, hs) in H4:
                S.stt(GT[:, hs], pAC[:, hs], sc[:, 8 + h:9 + h], self.NEGM[mk].v(), ALU.add, ALU.add)
                S.stt(gam[:, hs], pAC[:, hs], sc[:, h:h + 1], self.POSM[mk].v(), ALU.subtract, ALU.add)
            S.act(GT.v(), GT.v(), AF.Exp)
            S.act(gam.v(), gam.v(), AF.Exp, scale=-1.0)
            yield
            pKK = [self.ps(), self.ps()]
            for (h, fc, h2, hb, hs) in H4:
                cs = slice(fc * 128, (fc + 1) * 128)
                S.mm(pKK[h2][:, cs], qkv[2 + fc][hb, tsl], qkv[2 + fc][hb, tsl])
            M0 = self.gM[2]
            for (h, fc, h2, hb, hs) in H4:
                cs = slice(fc * 128, (fc + 1) * 128)
                S.stt(M0[:, h, :], gam[:, hs], sc[:, 28 + h:29 + h], pKK[h2][:, cs], ALU.mult, ALU.mult)
            yield
            pQK = [self.ps(), self.ps()]
            for (h, fc, h2, hb, hs) in H4:
                cs = slice(fc * 128, (fc + 1) * 128)
                S.mm(pQK[h2][:, cs], qkv[2 + fc][hb, tsl], qkv[fc][hb, tsl])
            for (h, fc, h2, hb, hs) in H4:
                cs = slice(fc * 128, (fc + 1) * 128)
                S.tt("dve", self.QKG[:, h, :], GT[:, hs], pQK[h2][:, cs], ALU.mult)
            Md, Nd, Z = self.gM[0], self.gN[0], self.gX[0]
            d32 = self.D32b.v().rr("p (o t) -> p o t", o=1).bc([128, 4, 128])
            idb = self.ident.v().rr("p (o t) -> p o t", o=1).bc([128, 4, 128])
            S.tt("dve", Md.v(), M0.v(), d32, ALU.mult)
            S.tt("dve", self.Moff.v(), M0.v(), Md.v(), ALU.subtract)
            for (h, fc, h2, hb, hs) in H4:
                S.tr(self.pst[:, hs], Md[:, h, :], self.ident.v())
            S.copy("act", Nd.v().rr("p h t -> p (h t)"), self.pst[:, 0:512])
            S.tt("dve", Z.v(), self.pst[:, 0:512].rr("p (h t) -> p h t", h=4), idb, ALU.add)
            R = self.gR
            for (h, fc, h2, hb, hs) in H4:
                kc = slice(64 * h2, 64 * h2 + 64)
                vc = slice(64 * (1 - h2), 64 * (1 - h2) + 64)
                S.ts("dve", R[:, h, kc], self.ctok[:, h * 64:(h + 1) * 64], sc[:, 16 + h:17 + h], ALU.mult)
                S.ts("dve", R[:, h, vc], self.ctok[:, 256 + h * 64:256 + (h + 1) * 64], bet3[:, ti, h:h + 1], ALU.mult)
            nsq = 2 if blk.sample else 4
            for j in range(nsq):
                Mn, Nn2, Zn = self.gM[(j + 1) % 2], self.gN[(j + 1) % 2], self.gX[(j + 1) % 2]
                yield
                pM = self.ps()
                for (h, fc, h2, hb, hs) in H4:
                    S.mm(pM[:, hs], Nd[:, h, :], Md[:, h, :])
                S.copy("act", Mn.v().rr("p h t -> p (h t)"), pM.v())
                if j < nsq - 1:
                    yield
                    pN = self.ps()
                    for (h, fc, h2, hb, hs) in H4:
                        S.mm(pN[:, hs], Md[:, h, :], Nd[:, h, :])
                    S.copy("act", Nn2.v().rr("p h t -> p (h t)"), pN.v())
                yield
                pZ = self.ps()
                for (h, fc, h2, hb, hs) in H4:
                    S.mm(pZ[:, hs], Mn[:, h, :], Z[:, h, :])
                S.tt("dve", Zn.v().rr("p h t -> p (h t)"), pZ.v(), Z.v().rr("p h t -> p (h t)"), ALU.add)
                Md, Nd, Z = Mn, Nn2, Zn
            TdT = Z
            pXT, pV = None, None
            if blk.sample:
                yield
                pXT, pV = self.ps(), self.ps()
                for (h, fc, h2, hb, hs) in H4:
                    vc = slice(64 * (1 - h2), 64 * (1 - h2) + 64)
                    S.mm(pXT[:, hs], R[:, h, :], TdT[:, h, :])
                    S.mm(pV[:, h * 64:(h + 1) * 64], TdT[:, h, :], R[:, h, vc])
            else:
                yield
                pG = self.ps()
                for (h, fc, h2, hb, hs) in H4:
                    S.mm(pG[:, hs], self.Moff[:, h, :], TdT[:, h, :])
                S.copy("act", self.nGT.v().rr("p h t -> p (h t)"), pG.v())
                W = gC = self.gM[2]
                yield
                pW0 = self.ps()
                for (h, fc, h2, hb, hs) in H4:
                    S.mm(pW0[:, hs], TdT[:, h, :], R[:, h, :])
                S.copy("act", gC.v().rr("p h t -> p (h t)"), pW0.v())
                for it in range(2):
                    Wn = self.gW[it]
                    yield
                    pWi = self.ps()
                    for (h, fc, h2, hb, hs) in H4:
                        S.mm(pWi[:, hs], self.nGT[:, h, :], W[:, h, :])
                    S.tt("dve", Wn.v().rr("p h t -> p (h t)"), pWi.v(), gC.v().rr("p h t -> p (h t)"), ALU.add)
                    W = Wn
                yield
                pXT, pV = self.ps(), self.ps()
                for (h, fc, h2, hb, hs) in H4:
                    vc = slice(64 * (1 - h2), 64 * (1 - h2) + 64)
                    S.mm(pXT[:, hs], R[:, h, :], TdT[:, h, :], start=True, stop=False)
                    S.mm(pXT[:, hs], W[:, h, :], self.nGT[:, h, :], start=False, stop=True)
                    S.mm(pV[:, h * 64:(h + 1) * 64], TdT[:, h, :], R[:, h, vc], start=True, stop=False)
                    S.mm(pV[:, h * 64:(h + 1) * 64], self.nGT[:, h, :], W[:, h, vc], start=False, stop=True)
            S.copy("act", self.XTb.v().rr("p h t -> p (h t)"), pXT.v())
            S.copy("act", self.Val.v(), pV[:, 0:256])
            for (h, fc, h2, hb, hs) in H4:
                S.tt("dve", self.qdec[hb, h, :], qkv[fc][hb, tsl], Edec[hb, hs], ALU.mult)
                S.ts("dve", self.kdec[:, h, :], self.ctok[:, h * 64:(h + 1) * 64], sc[:, 20 + h:21 + h], ALU.mult)
            if not blk.sample:
                nostate = blk.first and ti == 0
                yield
                pW = [self.ps(), self.ps()]
                for (h, fc, h2, hb, hs) in H4:
                    if nostate:
                        S.copy("dve", self.wtok[:, h, :], self.Val[:, h * 64:(h + 1) * 64])
                    else:
                        S.mm(pW[h2][:, fc * 64:(fc + 1) * 64], self.XTb[hb, h, :], self.Sb[hb, fc, :])
                if not nostate:
                    for (h, fc, h2, hb, hs) in H4:
                        S.tt("dve", self.wtok[:, h, :], self.Val[:, h * 64:(h + 1) * 64], pW[h2][:, fc * 64:(fc + 1) * 64], ALU.subtract)
                yield
                po = [self.ps(), self.ps()]
                for (h, fc, h2, hb, hs) in H4:
                    out = po[h2][hb, fc * 128:(fc + 1) * 128]
                    S.mm(out, self.wtok[:, h, :], self.QKG[:, h, :], start=True, stop=nostate)
                    if not nostate:
                        S.mm(out, self.Sb[hb, fc, :], self.qdec[hb, h, :], start=False, stop=True)
                for fc in range(2):
                    S.copy("act", oT[fc][0:64, tsl], po[0][0:64, fc * 128:(fc + 1) * 128])
                    S.copy("act", oT[fc][64:128, tsl], po[1][64:128, fc * 128:(fc + 1) * 128])
                yield
                pS = self.ps()
                for (h, fc, h2, hb, hs) in H4:
                    S.mm(pS[hb, fc * 64:(fc + 1) * 64], self.kdec[:, h, :], self.wtok[:, h, :])
                for (h, fc, h2, hb, hs) in H4:
                    S.stt(self.Sm[hb, fc, :], self.Sm[hb, fc, :], sc[hb, 24 + h:25 + h], pS[hb, fc * 64:(fc + 1) * 64], ALU.mult, ALU.add)
                S.copy("dve", self.Sb.v(), self.Sm.v())
            else:
                self._load_C_sample(l)
                yield
                pWT = [self.ps(), self.ps()]
                for (h, fc, h2, hb, hs) in H4:
                    ob = slice(64 * (1 - h2), 64 * (1 - h2) + 64)
                    for b in range(16):
                        S.mm(pWT[h2][ob, fc * 128 + 8 * b:fc * 128 + 8 * b + 8], self.H0T[hb, b, fc, :], self.XTb[hb, h, 8 * b:8 * b + 8])
                for (h, fc, h2, hb, hs) in H4:
                    ob = slice(64 * (1 - h2), 64 * (1 - h2) + 64)
                    S.tt("dve", self.gW[1][ob, h, :], self.XTb[ob, h, :], pWT[h2][ob, fc * 128:(fc + 1) * 128], ALU.subtract)
                for par in range(2):
                    ob = slice(64 * (1 - par), 64 * (1 - par) + 64)
                    for h in (par, par + 2):
                        S.tr(self.pst[:, h * 64:(h + 1) * 64], self.gW[1][ob, h, :], self.ident[ob, ob])
                    for h in (par, par + 2):
                        S.copy("act", self.wtok[:, h, :], self.pst[:, h * 64:(h + 1) * 64])
                yield
                po = [self.ps(), self.ps()]
                for (h, fc, h2, hb, hs) in H4:
                    S.mm(po[h2][hb, fc * 128:(fc + 1) * 128], self.wtok[:, h, :], self.QKG[:, h, :], start=True, stop=False)
                    for b in range(16):
                        S.mm(po[h2][hb, fc * 128 + 8 * b:fc * 128 + 8 * b + 8], self.H0T[hb, b, fc, :], self.qdec[hb, h, 8 * b:8 * b + 8],
                             start=False, stop=(b == 15))
                for fc in range(2):
                    S.copy("act", oT[fc][0:64, tsl], po[0][0:64, fc * 128:(fc + 1) * 128])
                    S.copy("act", oT[fc][64:128, tsl], po[1][64:128, fc * 128:(fc + 1) * 128])
                yield
                pE = self.ps()
                for h in range(4):
                    for i in range(3):
                        S.mm(pE[0:64, h * 16:(h + 1) * 16], self.cdsplit[:, i, h:h + 1].bc([128, 64]), self.SeqSelb.v(),
                             start=(i == 0), stop=(i == 2))
                S.act(self.etotb.v().rr("p h b -> p (h b)"), pE[0:64, 0:64], AF.Exp)
                for h in range(4):
                    if h % 2 == 0:
                        for hh in range(2):
                            S.dma_in("sp", self.hnat(hh), self.I["state_gdn"][l][:, h + hh].rearrange("b k v -> k b v"))
                    for hf in range(2):
                        S.tt("dve", self.bexp(hf), self.wtok[:, h, :].rr("p (o n) -> p o n", o=1).bc([128, 8, 64]),
                             self.SeqSelb[:, hf * 8:(hf + 1) * 8].rr("p (b o) -> p b o", o=1).bc([128, 8, 64]), ALU.mult)
                    yield
                    pn = [self.ps(), self.ps()]
                    for hf in range(2):
                        S.mm(pn[hf][0:64, :], self.kdec[:, h, :], self.bexp(hf).rr("p b n -> p (b n)"))
                    for hf in range(2):
                        bs = slice(hf * 8, hf * 8 + 8)
                        S.tt("dve", self.hnat(h % 2)[:, bs, :], self.hnat(h % 2)[:, bs, :],
                             self.etotb[:, h, bs].rr("p (b o) -> p b o", o=1).bc([64, 8, 64]), ALU.mult)
                        S.tt("dve", self.hnat(h % 2)[:, bs, :], self.hnat(h % 2)[:, bs, :], pn[hf][0:64, :].rr("p (b n) -> p b n", n=64), ALU.add)
                    if h % 2 == 1:
                        for hh in range(2):
                            S.dma_out("sp", self.O["s_gdn"][l][:, h - 1 + hh].rearrange("b k v -> k b v"), self.hnat(hh))
                self.out_tiles.extend([self.xrt[1].t, self.ybuf[0].t])
        if blk.last:
            for (h, fc, h2, hb, hs) in H4:
                S.dma_out("sp", self.O["p_gdn"][l, 0, h], self.Sm[hb, fc, :])
            self.out_tiles.append(self.Sm.t)
        for fc in range(2):
            sq = self.gsq
            S.act(sq[:, 0:N], oT[fc][:, 0:N], AF.Square)
            yield
            pss = self.ps()
            S.mm(pss[:, 0:N], self.blockones.v(), sq[:, 0:N])
            rs = self.cf[2]
            self.rsqrt(rs[:, 0:N], pss[:, 0:N], 1.0 / 64.0, self.eps6())
            yield
            pg = self.ps()
            self.zmm(blk, C_CG + fc * 128, 128, pg[:, 0:N])
            sg = self.cf[3]
            S.act(sg[:, 0:N], pg[:, 0:N], AF.Silu)
            S.stt(oT[fc][:, 0:N], oT[fc][:, 0:N], self.gdnw[:, 0:1], rs[:, 0:N], ALU.mult, ALU.mult)
            S.tt("dve", self.mixT[4 + fc][:, 0:N], oT[fc][:, 0:N], sg[:, 0:N], ALU.mult)

    def sin_rr(self, dst, x, scr, iscr):
        S = self.S
        PI = math.pi
        self.reduce_pi(dst, x, scr, iscr)
        S.act(dst, dst, AF.Sin)

    def reduce_pi(self, dst, x, scr, iscr):
        S = self.S
        PI = math.pi
        S.ts("dve", scr, x, 1.0 / (2 * PI), ALU.mult)
        S.copy("dve", iscr, scr)
        S.copy("dve", scr, iscr)
        S.stt(dst, scr, -2 * PI, x, ALU.mult, ALU.add)
        S.ts("dve", scr, dst, PI, ALU.is_gt)
        S.stt(dst, scr, -2 * PI, dst, ALU.mult, ALU.add)
        S.ts("dve", scr, dst, -PI, ALU.is_lt)
        S.stt(dst, scr, 2 * PI, dst, ALU.mult, ALU.add)
        S.ts("dve", dst, dst, PI, ALU.min, -PI, ALU.max)

    def _load_D(self, l):
        S, I, nc = self.S, self.I, self.nc
        PI = math.pi
        par, sm = self.s5par, self.s5sm
        lam = lambda nm: I[nm][l].rearrange("(k g2) n -> (g2 n) k", g2=2)
        S.dma_in("sp", par[:, :, 0], lam("s5_lambda_re"), allow_slow_non_contiguous=True)
        S.dma_in("sp", par[:, :, 1], lam("s5_lambda_im"), allow_slow_non_contiguous=True)
        ldt = I["s5_log_dt"][l].rearrange("(k g2) -> g2 k", g2=2)
        for g2 in range(2):
            S.dma_in("sp", par[64 * g2:64 * g2 + 64, :, 2], ldt[g2:g2 + 1, :].to_broadcast([64, 8]))
        S.act(par[:, :, 2], par[:, :, 2], AF.Exp)
        S.tt("dve", par[:, :, 3], par[:, :, 0], par[:, :, 2], ALU.mult)
        S.tt("dve", par[:, :, 4], par[:, :, 1], par[:, :, 2], ALU.mult)
        S.act(par[:, :, 5], par[:, :, 3], AF.Exp, scale=8.0)
        w0, w1, w2 = self.wtmp
        v3 = lambda v: v.rr("p (k q) -> p k q", q=32)
        A1, A2 = w0[:, 0:256], w0[:, 256:512]
        MG, PWc = w1[:, 0:256], w1[:, 256:512]
        PWs, SC = w2[:, 0:256], w2[:, 256:512]
        col = lambda j: par[:, :, j].rr("p (k o) -> p k o", o=1).bc([128, 8, 32])
        qrow = self.s5q[:, 0:32].rr("p (o q) -> p o q", o=1).bc([128, 8, 32])
        S.tt("dve", v3(A1), col(4), qrow, ALU.mult)
        S.tt("dve", v3(MG), col(3), qrow, ALU.mult)
        S.act(MG, MG, AF.Exp)
        self.sin_rr(PWs, A1, SC, self.itmp.v())
        S.ts("dve", A2, A1, PI / 2, ALU.add)
        self.sin_rr(PWc, A2, SC, self.itmp.v())
        S.tt("dve", PWc, PWc, MG, ALU.mult)
        S.tt("dve", PWs, PWs, MG, ALU.mult)
        PWre, PWim = v3(PWc), v3(PWs)
        S.copy("dve", par[:, :, 6], PWre[:, :, 15])
        S.copy("dve", par[:, :, 7], PWim[:, :, 15])
        lr, li = par[:, :, 0], par[:, :, 1]
        t0, t1, t2, t3, fre, fim = (sm[:, :, j] for j in range(6))
        abim = PWim[:, :, 8]
        S.ts("dve", t0, PWre[:, :, 8], -1.0, ALU.add)
        S.tt("dve", t1, lr, lr, ALU.mult)
        S.tt("dve", t2, li, li, ALU.mult)
        S.tt("dve", t1, t1, t2, ALU.add)
        S.recip(t1, t1)
        S.tt("dve", t2, t0, lr, ALU.mult)
        S.tt("dve", t3, abim, li, ALU.mult)
        S.tt("dve", t2, t2, t3, ALU.add)
        S.tt("dve", fre, t2, t1, ALU.mult)
        S.tt("dve", t2, abim, lr, ALU.mult)
        S.tt("dve", t3, t0, li, ALU.mult)
        S.tt("dve", t2, t2, t3, ALU.subtract)
        S.tt("dve", fim, t2, t1, ALU.mult)
        f0, f1, f2, f3 = (self.ftmp[j][:, 0:128].rr("p (k i) -> p k i", i=16) for j in range(4))
        bsrc = lambda nm: I[nm][l].rearrange("(k g2) n i -> (g2 n) k i", g2=2)
        S.dma_in("sp", f0, bsrc("s5_b_re"))
        S.dma_in("sp", f1, bsrc("s5_b_im"))
        fb = lambda f: f.rr("p (k o) -> p k o", o=1).bc([128, 8, 16])
        s5bb = self.cf[2].v().rr("p (a k i) -> p a k i", a=2, k=8)
        s5c = self.cf[3].v().rr("p (a k i) -> p a k i", a=2, k=8)
        s5cb = [self.cq[0][:, 0:128], self.cq[0][:, 128:256], self.cq[1][:, 0:128], self.cq[1][:, 128:256]]
        bbre, bbim = s5bb[:, 0], s5bb[:, 1]
        S.tt("dve", f2, f0, fb(fre), ALU.mult)
        S.tt("dve", f3, f1, fb(fim), ALU.mult)
        S.tt("dve", bbre, f2, f3, ALU.subtract)
        S.tt("dve", f2, f1, fb(fre), ALU.mult)
        S.tt("dve", f3, f0, fb(fim), ALU.mult)
        S.tt("dve", bbim, f2, f3, ALU.add)
        for part, nm in enumerate(("s5_c_re", "s5_c_im")):
            cn = self.ftmp[4][:, 0:128].rr("p (c n) -> p c n", c=2)
            S.dma_in("sp", cn, I[nm][l].rearrange("(c gl) i n -> (gl i) c n", c=2))
            p = self.ps()
            for c in range(2):
                S.tr(p[0:64, c * 128:(c + 1) * 128], cn[:, c, :], self.identf.v())
            pv = p[0:64, 0:256].rr("n (k g2 i) -> n k g2 i", g2=2, i=16)
            for g2 in range(2):
                S.copy("act" if g2 else "dve", s5c[64 * g2:64 * g2 + 64, part, :, :], pv[:, :, g2, :])
        cre, cim = s5c[:, 0], s5c[:, 1]
        dsrc = I["s5_d"][l].rearrange("(g i) -> i g", i=16)
        for sg in range(8):
            S.dma_in("sp", self.s5Dfold[16 * sg:16 * sg + 16, :], dsrc, allow_slow_non_contiguous=True)
        T = [self.ftmp[j][:, 0:128].rr("p (s i) -> p s i", i=16) for j in range(5, 9)]
        for k in range(8):
            def cplx(outre, outim, pre, pim, xre, xim, neg_im=False, pw_outer=True):
                a = lambda v: v.rr("p (s o) -> p s o", o=1).bc([128, 8, 16])
                b = lambda v: v.rr("p (o i) -> p o i", o=1).bc([128, 8, 16])
                S.tt("dve", T[0], a(pre), b(xre), ALU.mult)
                S.tt("dve", T[1], a(pim), b(xim), ALU.mult)
                S.tt("dve", outre, T[0], T[1], ALU.subtract)
                S.tt("dve", T[2], a(pre), b(xim), ALU.mult)
                S.tt("dve", T[3], a(pim), b(xre), ALU.mult)
                if neg_im:
                    S.stt(outim, T[2], -1.0, T[3], ALU.mult, ALU.subtract)
                else:
                    S.tt("dve", outim, T[2], T[3], ALU.add)
            cb3 = [c.rr("p (s i) -> p s i", i=16) for c in s5cb]
            cplx(cb3[0], cb3[1], PWre[:, k, 16:24], PWim[:, k, 16:24], bbre[:, k, :], bbim[:, k, :], neg_im=True)
            cplx(cb3[2], cb3[3], PWre[:, k, 24:32], PWim[:, k, 24:32], cre[:, k, :], cim[:, k, :])
            pT = [self.ps(), self.ps()]
            for g2 in range(2):
                hb = slice(64 * g2, 64 * g2 + 64)
                S.mm(pT[g2][:, 0:128], s5cb[0][hb, :], s5cb[2][hb, :], start=True, stop=False)
                S.mm(pT[g2][:, 0:128], s5cb[1][hb, :], s5cb[3][hb, :], start=False, stop=True)
            for g2 in range(2):
                g = 2 * k + g2
                tm = self.ftmp[9][:, 0:128]
                S.tt("dve", tm, pT[g2][:, 0:128], self.Cmask.v(), ALU.mult)
                S.stt(self.s5Toep[:, g, :], self.identf.v(), self.s5Dfold[:, g:g + 1], tm, ALU.mult, ALU.add)
            Rre = self.ftmp[9][:, 0:128].rr("p (s i) -> p s i", i=16)
            Rim = self.ftmp[9][:, 128:256].rr("p (s i) -> p s i", i=16)
            cplx(Rre, Rim, PWre[:, k, 0:8], PWim[:, k, 0:8], bbre[:, k, :], bbim[:, k, :])
            pR = self.ps()
            for part in range(2):
                S.tr(pR[:, part * 128:(part + 1) * 128], self.ftmp[9][:, part * 128:(part + 1) * 128], self.identf.v())
            for part in range(2):
                S.copy("act", self.s5RT[:, part, 2 * k:2 * k + 2, :], pR[:, part * 128:(part + 1) * 128].rr("p (g n) -> p g n", g=2))
            cplx(self.s5P[:, 0, k, :].rr("p (s i) -> p s i", i=16), self.s5P[:, 1, k, :].rr("p (s i) -> p s i", i=16),
                 PWre[:, k, 8:16], PWim[:, k, 8:16], cre[:, k, :], cim[:, k, :], neg_im=True)
        phi = sm[:, :, 6]
        S.ts("dve", t0, par[:, :, 4], 8.0, ALU.mult)
        self.reduce_pi(phi, t0, t1, self.itmp[:, 0:8])
        th = w0[:, 0:256].rr("p (k b) -> p k b", b=32)
        S.tt("dve", th, phi.rr("p (k o) -> p k o", o=1).bc([128, 8, 32]), self.s5q[:, 32:64].rr("p (o q) -> p o q", o=1).bc([128, 8, 32]),
             ALU.mult)
        self.sin_rr(self.s5E[:, 1, :, :].rr("p k b -> p (k b)"), w0[:, 0:256], SC, self.itmp.v())
        S.ts("dve", A2, w0[:, 0:256], PI / 2, ALU.add)
        self.sin_rr(self.s5E[:, 0, :, :].rr("p k b -> p (k b)"), A2, SC, self.itmp.v())
        S.dma_in("sp", w1.v().rr("p (c f) -> p c f", c=2), I["s5_glu_w"][l].rearrange("(c p) f -> p c f", p=128))
        S.copy("dve", self.s5glu.v().rr("p c f -> p (c f)"), w1.v())
        S.dma_in("sp", self.s5gb.v(), I["s5_glu_b"][l].rearrange("(c p) -> p c", p=128), allow_slow_non_contiguous=True)

    def _mixer_D(self, l, blk):
        S = self.S
        N = blk.N
        nb = N // 8
        for c, (c0, w) in enumerate(((0, 96), (96, 96), (192, 64))):
            yield
            p = self.ps()
            self.zmm(blk, C_DU + c0, w, p[0:w, 0:N])
            S.copy("act" if c % 2 else "dve", self.s5uT[c][0:w, 0:N], p[0:w, 0:N])
        for q in range(3):
            yield
            pFq = self.ps()
            ks = list(range(q, 8, 3))
            for si, k in enumerate(ks):
                uc = k // 3
                for g2 in range(2):
                    out = pFq[:, (si * 2 + g2) * nb:(si * 2 + g2 + 1) * nb]
                    for sg in range(8):
                        rhs = self.s5uT[uc][32 * q:32 * q + 32, 0:N].rr("p (b s) -> p b s", s=8)[:, :, sg]
                        S.mm(out, self.s5Sel[32 * q:32 * q + 32, g2 * 8 + sg, :], rhs, start=(sg == 0), stop=(sg == 7))
            for si, k in enumerate(ks):
                S.copy("act" if (q + si) % 2 else "dve", self.s5Uf[:, 2 * k:2 * k + 2, 0:nb],
                       pFq[:, si * 2 * nb:(si + 1) * 2 * nb].rr("p (g b) -> p g b", g=2))
        if blk.sample:
            stg = self.ybuf[1]
            s5H0 = self.df[4].v().rr("p (a k b) -> p a k b", a=2, k=8)
            for part, nm in ((0, "state_s5_re"), (1, "state_s5_im")):
                S.dma_in("sp", stg[0:16, :], self.I[nm][l].rearrange("b g n -> b (g n)"))
                yield
                p = self.ps()
                for k in range(8):
                    S.tr(p[:, k * 16:(k + 1) * 16], stg[0:16, k * 128:(k + 1) * 128], self.identf[0:16, 0:16])
                S.copy("dve", s5H0[:, part, :, :], p[:, 0:128].rr("p (k b) -> p k b", b=16))
                S.copy("act", self.s5Hst[:, part, :, 0:16], p[:, 0:128].rr("p (k b) -> p k b", b=16))
        yield
        pXr, pXi = self.ps(), self.ps()
        for g in range(16):
            k, g2 = g // 2, g % 2
            hb = slice(64 * g2, 64 * g2 + 64)
            S.mm(pXr[hb, k * nb:(k + 1) * nb], self.s5RT[:, 0, g, :], self.s5Uf[:, g, 0:nb])
            S.mm(pXi[hb, k * nb:(k + 1) * nb], self.s5RT[:, 1, g, :], self.s5Uf[:, g, 0:nb])
        Xr3 = pXr[:, 0:8 * nb].rr("p (k b) -> p k b", b=nb)
        Xi3 = pXi[:, 0:8 * nb].rr("p (k b) -> p k b", b=nb)
        F = [self.df[j][:, 0:8 * nb].rr("p (k b) -> p k b", b=nb) for j in range(6)]
        F = F + [F[2], F[3]]
        par = self.s5par
        if not blk.sample:
            Ec, Es = self.s5E[:, 0, :, 0:nb], self.s5E[:, 1, :, 0:nb]
            S.tt("dve", F[0], Xr3, Ec, ALU.mult)
            S.tt("dve", F[1], Xi3, Es, ALU.mult)
            S.tt("dve", F[2], F[0], F[1], ALU.add)
            S.tt("dve", F[0], Xi3, Ec, ALU.mult)
            S.tt("dve", F[1], Xr3, Es, ALU.mult)
            S.tt("dve", F[3], F[0], F[1], ALU.subtract)
            for k in range(8):
                rho = par[:, k, 5:6].bc([128, nb])
                for part, Z in ((0, F[2]), (1, F[3])):
                    init = 0.0 if blk.first else self.s5H[:, part, k:k + 1]
                    S.scan(F[4 + part][:, k, :], rho, Z[:, k, :], init)
            Wr, Wi = F[4], F[5]
            S.tt("dve", F[0], Wr, Ec, ALU.mult)
            S.tt("dve", F[1], Wi, Es, ALU.mult)
            S.tt("dve", F[6], F[0], F[1], ALU.subtract)
            S.tt("dve", F[0], Wi, Ec, ALU.mult)
            S.tt("dve", F[1], Wr, Es, ALU.mult)
            S.tt("dve", F[7], F[0], F[1], ALU.add)
            for part, Hn in ((0, F[6]), (1, F[7])):
                if blk.first:
                    S.memset("pool", self.s5Hst[:, part, :, 0:1], 0.0)
                else:
                    S.copy("dve", self.s5Hst[:, part, :, 0], self.s5H[:, part, :])
                S.copy("act", self.s5Hst[:, part, :, 1:nb], Hn[:, :, 0:nb - 1])
                S.copy("dve", self.s5H[:, part, :], Hn[:, :, nb - 1])
            if blk.last:
                for part, nm in ((0, "p_s5_re"), (1, "p_s5_im")):
                    S.dma_out("sp", self.O[nm][l, 0].rearrange("(k g2) n -> (g2 n) k", g2=2), self.s5H[:, part, :], allow_slow_non_contiguous=True)
                self.out_tiles.append(self.s5H.t)
        else:
            stg = self.ybuf[1]
            s5H0 = self.df[4].v().rr("p (a k b) -> p a k b", a=2, k=8)
            a8r = par[:, :, 6].rr("p (k o) -> p k o", o=1).bc([128, 8, 16])
            a8i = par[:, :, 7].rr("p (k o) -> p k o", o=1).bc([128, 8, 16])
            h0r, h0i = s5H0[:, 0], s5H0[:, 1]
            S.tt("dve", F[0], h0r, a8r, ALU.mult)
            S.tt("dve", F[1], h0i, a8i, ALU.mult)
            S.tt("dve", F[0], F[0], F[1], ALU.subtract)
            S.tt("dve", F[2], F[0], Xr3, ALU.add)
            S.tt("dve", F[0], h0i, a8r, ALU.mult)
            S.tt("dve", F[1], h0r, a8i, ALU.mult)
            S.tt("dve", F[0], F[0], F[1], ALU.add)
            S.tt("dve", F[3], F[0], Xi3, ALU.add)
            for part, nm, Hn in ((0, "s_s5_re", F[2]), (1, "s_s5_im", F[3])):
                yield
                pp = [self.ps(), self.ps()]
                for k in range(8):
                    S.tr(pp[k // 4][0:16, (k % 4) * 128:(k % 4 + 1) * 128], Hn[:, k, :], self.identf.v())
                for hf in range(2):
                    S.copy("act" if hf else "dve", stg[0:16, hf * 512:(hf + 1) * 512], pp[hf][0:16, :])
                S.dma_out("sp", self.O[nm][l].rearrange("b g n -> b (g n)"), stg[0:16, :])
            self.out_tiles.append(stg.ts[0] if isinstance(stg, View) else stg.t)
        yield
        pY = [self.ps(), self.ps()]
        for g in range(16):
            k, g2 = g // 2, g % 2
            hb = slice(64 * g2, 64 * g2 + 64)
            out = pY[g2][:, k * nb:(k + 1) * nb]
            S.mm(out, self.s5Toep[:, g, :], self.s5Uf[:, g, 0:nb], start=True, stop=False)
            S.mm(out, self.s5P[hb, 0, k, :], self.s5Hst[hb, 0, k, 0:nb], start=False, stop=False)
            S.mm(out, self.s5P[hb, 1, k, :], self.s5Hst[hb, 1, k, 0:nb], start=False, stop=True)
        Yf4 = self.s5Yf[:, :, 0:nb].rr("p (k g2) b -> p k g2 b", g2=2)
        for g2 in range(2):
            S.copy("act" if g2 else "dve", Yf4[:, :, g2, :], pY[g2][:, 0:8 * nb].rr("p (k b) -> p k b", b=nb))
        y = [self.df[0], self.df[1]]
        yb = self.dbf
        for c in range(2):
            yield
            pU = self.ps()
            for j in range(4):
                k = 4 * c + j
                tp = (0, 96) if j == 3 else None
                for tau in range(8):
                    out = pU[32 * j:32 * j + 32, 0:N].rr("p (b s) -> p b s", s=8)[:, :, tau]
                    S.mm(out, self.s5SelU[:, tau, :], self.s5Yf[:, 2 * k, 0:nb], start=True, stop=False, tile_position=tp)
                    S.mm(out, self.s5SelU[:, 8 + tau, :], self.s5Yf[:, 2 * k + 1, 0:nb], start=False, stop=True, tile_position=tp)
            S.act(y[c][:, 0:N], pU[:, 0:N], AF.Gelu)
            S.copy("dve", yb[c][:, 0:N], y[c][:, 0:N])
        for fo in range(2):
            yield
            pg = self.ps()
            for ec in range(2):
                S.mm(pg[:, 0:N], self.s5glu[:, ec, fo * 128:(fo + 1) * 128], yb[ec][:, 0:N], start=(ec == 0), stop=(ec == 1))
            sig = self.df[4]
            S.act(sig[:, 0:N], pg[:, 0:N], AF.Sigmoid, bias=self.s5gb[:, fo:fo + 1], scale=1.0)
            yield
            pz = self.ps()
            self.zmm(blk, C_DG + fo * 128, 128, pz[:, 0:N])
            sg = self.df[5]
            S.act(sg[:, 0:N], pz[:, 0:N], AF.Silu)
            S.tt("dve", sig[:, 0:N], sig[:, 0:N], y[fo][:, 0:N], ALU.mult)
            S.tt("dve", self.mixT[6 + fo][:, 0:N], sig[:, 0:N], sg[:, 0:N], ALU.mult)


WSHAPES = {
    "w_in": [DEPTH, D, INC], "w_out": [DEPTH, D, D], "ln_g": [DEPTH, D], "ln_b": [DEPTH, D],
    "rg_conv_w": [DEPTH, 4, 256], "rg_conv_b": [DEPTH, 256], "rg_gate_a_w": [DEPTH, 4, 64, 64], "rg_gate_a_b": [DEPTH, 256],
    "rg_gate_x_w": [DEPTH, 4, 64, 64], "rg_gate_x_b": [DEPTH, 256], "rg_lambda": [DEPTH, 256],
    "ssd_conv_w": [DEPTH, 4, 512], "ssd_conv_b": [DEPTH, 512], "ssd_dt_bias": [DEPTH, 4], "ssd_a_log": [DEPTH, 4],
    "ssd_d": [DEPTH, 4], "ssd_norm_w": [DEPTH, 256],
    "gdn_conv_w": [DEPTH, 4, 768], "gdn_conv_b": [DEPTH, 768], "gdn_dt_bias": [DEPTH, 4], "gdn_a_log": [DEPTH, 4],
    "gdn_norm_w": [DEPTH, 64],
    "s5_lambda_re": [DEPTH, 16, 64], "s5_lambda_im": [DEPTH, 16, 64], "s5_log_dt": [DEPTH, 16],
    "s5_b_re": [DEPTH, 16, 64, 16], "s5_b_im": [DEPTH, 16, 64, 16], "s5_c_re": [DEPTH, 16, 16, 64], "s5_c_im": [DEPTH, 16, 16, 64],
    "s5_d": [DEPTH, 256], "s5_glu_w": [DEPTH, 256, 256], "s5_glu_b": [DEPTH, 256],
}

STATE_NAMES = ["cache_rglru_conv", "state_rglru", "cache_ssd_conv", "state_ssd", "cache_gdn_conv", "state_gdn",
               "state_s5_re", "state_s5_im"]
OUT_STATE = ["_rg_conv", "_rg", "_ssd_conv", "_ssd", "_gdn_conv", "_gdn", "_s5_re", "_s5_im"]

_PROG_CACHE = {}


def get_prog(mixers="ABCD", dbg=None):
    key = (mixers, tuple(n for n, _ in (dbg or [])))
    if key not in _PROG_CACHE:
        _PROG_CACHE[key] = Prog(mixers, dbg)
    return _PROG_CACHE[key]


def make_in_maps(inputs, cores):
    maps = []
    f = lambda a: np.ascontiguousarray(np.asarray(a, dtype=np.float32))
    for c in cores:
        m = {"xp": f(inputs["x_prompt"][c]), "xs": f(inputs["x_sample"][16 * c:16 * c + 16].reshape(128, D))}
        for n in STATE_NAMES:
            m[n] = f(inputs[n][:, 16 * c:16 * c + 16])
        for n in WSHAPES:
            m[n] = f(inputs[n])
        maps.append(m)
    return maps


def kernel(**inputs):
    prog = get_prog()
    cores = list(range(NCORES))
    res = run_bass_kernel_spmd(prog.nc, make_in_maps(inputs, cores), core_ids=cores)
    R = res.results
    yp = np.stack([R[c]["yp"] for c in cores], 0)
    ys = np.concatenate([R[c]["ys"].reshape(16, 8, D) for c in cores], 0)
    outs = [yp, ys]
    for pre in ("p", "s"):
        for n in OUT_STATE:
            outs.append(np.concatenate([R[c][pre + n] for c in cores], axis=1))
    return tuple(np.ascontiguousarray(o.astype(np.float32)) for o in outs)
```

```python
import numpy as np
import os
import math
import itertools
INTERLEAVE = int(os.environ.get("KINTER", "1"))
NYBUF = int(os.environ.get("KYBUF", "2"))
SLAT = float(os.environ.get("KSLAT", "0.15"))
NXRES = int(os.environ.get("KXRES", "1"))
CSPLIT = int(os.environ.get("KSPLIT", "4"))
KSTOP = int(os.environ.get("KSTOP", "99"))
KSUB = int(os.environ.get("KSUB", "99"))
from contextlib import ExitStack
import concourse.bass as bass
import concourse.mybir as mybir
from concourse.bass_utils import run_bass_kernel_spmd

F32 = mybir.dt.float32
BF16 = mybir.dt.bfloat16
I32 = mybir.dt.int32
AF = mybir.ActivationFunctionType
ALU = mybir.AluOpType

NCORES = 8
D = 1024
L_P = 2048
NS_SEQ = 16
LS = 8
NB = 256
DEPTH = 2
INC = 2828
ALPHA = (2.0 * DEPTH) ** 0.25
C_AX, C_AG = 0, 256
C_BX, C_BB, C_BC, C_BDT, C_BG = 512, 768, 896, 1024, 1028
C_CQ, C_CK, C_CV, C_CBETA, C_CDEC, C_CG = 1284, 1540, 1796, 2052, 2056, 2060
C_DU, C_DG = 2316, 2572
NEG = -30000.0


class T:
    __slots__ = ("name", "w", "r", "dsem", "dcnt", "psum", "gen")

    def __init__(self, name):
        self.name = name
        self.psum = False
        self.gen = 0
        self.w = None
        self.r = []
        self.dsem = None
        self.dcnt = 0


class View:
    __slots__ = ("ap", "ts", "gen")

    def __init__(self, ap, ts, gen=None):
        self.ap = ap
        self.ts = ts
        self.gen = gen

    def __getitem__(self, idx):
        return View(self.ap[idx], self.ts, self.gen)

    def rr(self, pat, **kw):
        return View(self.ap.rearrange(pat, **kw), self.ts, self.gen)

    def bc(self, shape):
        return View(self.ap.to_broadcast(shape), self.ts, self.gen)


class Buf:
    def __init__(self, tensor, name, t=None, gen=None):
        self.tensor = tensor
        self.t = t or T(name)
        self.gen = gen

    def __getitem__(self, idx):
        return View(self.tensor[idx], [self.t], self.gen)

    def v(self):
        return View(self.tensor[:], [self.t], self.gen)


class Op:
    __slots__ = ("id", "e", "fn", "preds", "dur", "tbl", "kind", "tile", "dval", "idx", "prio", "fin", "kw")

    def __init__(self, id, e, fn, preds, dur, tbl=None, kind="c", tile=None, dval=0):
        self.id, self.e, self.fn, self.preds, self.dur, self.tbl, self.kind, self.tile, self.dval = id, e, fn, preds, dur, tbl, kind, tile, dval
        self.idx = 0
        self.prio = 0.0
        self.fin = 0.0


def _free(ap):
    n = 1
    for d in list(ap.shape)[1:]:
        n *= int(d)
    return n


ACT_TBL = {}


class Sched:
    def __init__(self, nc):
        self.nc = nc
        self.engs = {"pe": nc.tensor, "dve": nc.vector, "act": nc.scalar, "pool": nc.gpsimd, "sp": nc.sync}
        self.sem = {k: nc.alloc_semaphore(name="s_" + k) for k in self.engs}
        self.ops = []
        self.ninst = 0
        self.cnt = {k: 0 for k in self.engs}
        self.out_tiles = []

    def _record(self, e, fn, rt, wt, dur, tbl=None, kind="c", tile=None, dval=0):
        preds = {}
        for t in rt:
            if t.w is not None:
                preds[t.w] = "raw"
            if t.psum:
                for r in t.r:
                    if self.ops[r].e != e:
                        preds.setdefault(r, "rar")
        for t in wt:
            if t.w is not None:
                preds.setdefault(t.w, "waw:" + t.name)
            for r in t.r:
                preds.setdefault(r, "war:" + t.name)
        op = Op(len(self.ops), e, fn, preds, dur, tbl, kind, tile, dval)
        op.kw = getattr(self, "lbl", "")
        self.ops.append(op)
        for t in rt:
            t.r.append(op.id)
        for t in wt:
            t.w = op.id
            t.r = []
        self.ninst += 1
        self.cnt[e] += 1
        return op

    def emit(self, e, fn, reads, writes, dur=0.3, tbl=None):
        rt = []
        for v in reads:
            if isinstance(v, View):
                rt.extend(v.ts)
        wt = []
        for v in writes:
            wt.extend(v.ts)
        for v in list(reads) + list(writes):
            if isinstance(v, View) and v.gen is not None:
                assert v.gen == v.ts[0].gen, f"stale PSUM handle used: bank {v.ts[0].name} was re-allocated (gen {v.gen} vs {v.ts[0].gen})"
        return self._record(e, fn, rt, wt, dur, tbl)

    def dma_in(self, e, dst, src_ap, dram_reads=(), **kw):
        assert len(dst.ts) == 1
        t = dst.ts[0]
        if t.dsem is None:
            t.dsem = self.nc.alloc_semaphore(name="d_" + t.name)
        t.dcnt += 16
        nbytes = _free(dst.ap) * 4
        sib = None
        if t.w is not None and not t.r and self.ops[t.w].kind == "dma" and self.ops[t.w].kw == "dma_in":
            sib = self.ops[t.w]
            t.w = None
        op = self._record(e, lambda: self.engs[e].dma_start(out=dst.ap, in_=src_ap, **kw), list(dram_reads), [t],
                          2.0 + nbytes * 128 / 150e3, kind="dma", tile=t, dval=t.dcnt)
        if sib is not None:
            for p, kd in sib.preds.items():
                op.preds.setdefault(p, kd)
        op.kw = "dma_in"
        return op

    def dma_out(self, e, dst_ap, src, dram_writes=(), **kw):
        assert len(src.ts) == 1
        t = src.ts[0]
        if t.dsem is None:
            t.dsem = self.nc.alloc_semaphore(name="d_" + t.name)
        t.dcnt += 16
        if t not in self.out_tiles:
            self.out_tiles.append(t)
        nbytes = _free(src.ap) * 4
        return self._record(e, lambda: self.engs[e].dma_start(out=dst_ap, in_=src.ap, **kw), [t], list(dram_writes),
                            2.0 + nbytes * 128 / 150e3, kind="dma", tile=t, dval=t.dcnt)

    def drain(self, e, tiles):
        pass

    def finalize(self, reorder=True):
        ops = self.ops
        n = len(ops)
        LAT = float(os.environ.get("KLAT", "0.35"))
        succs = [[] for _ in range(n)]
        for op in ops:
            for p in op.preds:
                succs[p].append(op.id)
        for op in reversed(ops):
            m = 0.0
            for sid in succs[op.id]:
                if ops[sid].prio > m:
                    m = ops[sid].prio
            op.prio = op.dur + LAT + m
        order = {k: [] for k in self.engs}
        if not reorder:
            for op in ops:
                order[op.e].append(op)
        else:
            import heapq
            npred = [len(op.preds) for op in ops]
            ready_t = [0.0] * n
            avail = {k: [] for k in self.engs}
            for op in ops:
                if npred[op.id] == 0:
                    heapq.heappush(avail[op.e], (-op.prio, op.id))
            free = {k: 0.0 for k in self.engs}
            cur_tbl = None
            done = 0
            K = 24
            while done < n:
                best = None
                for e, hp in avail.items():
                    if not hp:
                        continue
                    cands = heapq.nsmallest(K, hp)
                    for (npr, oid) in cands:
                        op = ops[oid]
                        st = max(free[e], ready_t[oid])
                        if e == "act" and op.tbl is not None and op.tbl != cur_tbl:
                            st += 1.3
                        key = (st, npr)
                        if best is None or key < best[0]:
                            best = (key, e, oid, (npr, oid))
                (st, _), e, oid, item = best
                avail[e].remove(item)
                heapq.heapify(avail[e])
                op = ops[oid]
                if e == "act" and op.tbl is not None:
                    cur_tbl = op.tbl
                if op.kind == "dma":
                    free[e] = st + 0.1
                    op.fin = st + op.dur
                else:
                    free[e] = st + op.dur
                    op.fin = free[e]
                order[e].append(op)
                done += 1
                for sid in succs[oid]:
                    npred[sid] -= 1
                    lat = (0.0 if e == "pe" else SLAT) if ops[sid].e == e else LAT
                    if op.fin + lat > ready_t[sid]:
                        ready_t[sid] = op.fin + lat
                    if npred[sid] == 0:
                        heapq.heappush(avail[ops[sid].e], (-ops[sid].prio, sid))
            self.makespan = max(free.values())
        for e, lst in order.items():
            i = 0
            for op in lst:
                if op.kind != "dma":
                    i += 1
                    op.idx = i
        for e, lst in order.items():
            seen = {}
            eng = self.engs[e]

            def wait(key, sem, val):
                if seen.get(key, 0) >= val:
                    return
                eng.wait_ge(sem, val)
                seen[key] = val
            for op in lst:
                dmax = {}
                for p in op.preds:
                    P = ops[p]
                    if P.kind == "dma":
                        k = id(P.tile)
                        if k not in dmax or dmax[k][1] < P.dval:
                            dmax[k] = (P.tile.dsem, P.dval)
                for k in sorted(dmax, key=lambda kk: dmax[kk][1]):
                    wait(("d", k), dmax[k][0], dmax[k][1])
                for p in sorted(op.preds):
                    P = ops[p]
                    if P.kind == "dma":
                        continue
                    elif P.e == e:
                        if e != "pe":
                            wait(e, self.sem[e], P.idx)
                    else:
                        wait(P.e, self.sem[P.e], P.idx)
                ins = op.fn()
                if op.kind == "dma":
                    ins.then_inc(op.tile.dsem, 16)
                else:
                    ins.then_inc(self.sem[e], 1)
            if e == "sp":
                for t in self.out_tiles:
                    wait(("d", id(t)), t.dsem, t.dcnt)
                for k2 in ("pe", "dve", "act", "pool"):
                    last = [o for o in order[k2] if o.kind != "dma"]
                    if last:
                        wait(k2, self.sem[k2], last[-1].idx)

    def _e(self, e):
        return self.engs[e]

    def mm(self, out, lhsT, rhs, start=True, stop=True, tile_position=None):
        kw = {} if tile_position is None else {"tile_position": tile_position}
        n = _free(rhs.ap)
        return self.emit("pe", lambda: self.nc.tensor.matmul(out.ap, lhsT=lhsT.ap, rhs=rhs.ap, start=start, stop=stop, **kw),
                         [lhsT, rhs], [out], dur=0.03 + max(64, n) / 1400.0)

    def tr(self, out, in_, ident):
        return self.emit("pe", lambda: self.nc.tensor.transpose(out.ap, in_.ap, ident.ap), [in_, ident], [out], dur=0.11)

    def act(self, out, in_, func, bias=None, scale=None, e="act"):
        kw = {}
        rd = [in_]
        if bias is not None:
            if isinstance(bias, View):
                kw["bias"] = bias.ap
                rd.append(bias)
            else:
                kw["bias"] = float(bias)
        if scale is not None:
            if isinstance(scale, View):
                kw["scale"] = scale.ap
                rd.append(scale)
            else:
                kw["scale"] = float(scale)
        tbl = {AF.Exp: "el", AF.Ln: "el", AF.Sigmoid: "sg", AF.Silu: "si", AF.Gelu: "ge", AF.Sin: "sn"}.get(func)
        return self.emit("act", lambda: self.nc.scalar.activation(out=out.ap, in_=in_.ap, func=func, **kw), rd, [out],
                         dur=0.2 + _free(out.ap) * 0.00085, tbl=tbl)

    def _dd(self, out):
        return 0.13 + _free(out.ap) * 0.0007

    def copy(self, e, out, in_):
        if e == "act":
            return self.emit("act", lambda: self.nc.scalar.copy(out=out.ap, in_=in_.ap), [in_], [out], dur=0.2 + _free(out.ap) * 0.00085)
        return self.emit(e, lambda: self._e(e).tensor_copy(out=out.ap, in_=in_.ap), [in_], [out], dur=self._dd(out))

    def tt(self, e, out, a, b, op):
        return self.emit(e, lambda: self._e(e).tensor_tensor(out=out.ap, in0=a.ap, in1=b.ap, op=op), [a, b], [out], dur=self._dd(out) * 1.3)

    def ts(self, e, out, a, s1, op0, s2=None, op1=None):
        rd = [a]
        s1a = s1.ap if isinstance(s1, View) else float(s1)
        if isinstance(s1, View):
            rd.append(s1)
        kw = {}
        if op1 is not None:
            kw["op1"] = op1
            s2a = s2.ap if isinstance(s2, View) else float(s2)
            if isinstance(s2, View):
                rd.append(s2)
        else:
            s2a = None
        return self.emit(e, lambda: self._e(e).tensor_scalar(out=out.ap, in0=a.ap, scalar1=s1a, scalar2=s2a, op0=op0, **kw),
                         rd, [out], dur=self._dd(out))

    def stt(self, out, a, s, b, op0, op1):
        rd = [a, b]
        sa = s.ap if isinstance(s, View) else float(s)
        if isinstance(s, View):
            rd.append(s)
        return self.emit("dve", lambda: self.nc.vector.scalar_tensor_tensor(out=out.ap, in0=a.ap, scalar=sa, in1=b.ap,
                                                                           op0=op0, op1=op1), rd, [out], dur=self._dd(out) * 1.3)

    def scan(self, out, d0, d1, init, op0=ALU.mult, op1=ALU.add):
        rd = [d0, d1]
        ia = init.ap if isinstance(init, View) else float(init)
        if isinstance(init, View):
            rd.append(init)
        return self.emit("dve", lambda: self.nc.vector.tensor_tensor_scan(out=out.ap, data0=d0.ap, data1=d1.ap, initial=ia,
                                                                          op0=op0, op1=op1), rd, [out], dur=0.15 + _free(out.ap) * 0.0021)

    def memset(self, e, out, val):
        return self.emit(e, lambda: self._e(e).memset(out.ap, val), [], [out], dur=self._dd(out) * (4 if e == "pool" else 1))

    def recip(self, out, in_):
        return self.emit("dve", lambda: self.nc.vector.reciprocal(out=out.ap, in_=in_.ap), [in_], [out], dur=0.15 + _free(out.ap) * 0.005)


class Blk:
    def __init__(self, idx, sample):
        self.idx = idx
        self.sample = sample
        self.N = 128 if sample else NB
        self.G = NS_SEQ if sample else 1
        self.Lg = LS if sample else NB
        self.ntile = self.N // 128
        self.tok0 = 2048 if sample else idx * NB
        self.first = (not sample) and idx == 0
        self.last = (not sample) and idx == L_P // NB - 1


class Prog:
    def __init__(self, mixers="ABCD", dbg=None):
        self.mixers = mixers
        self.dbg_req = dbg or []
        self.nc = nc = bass.Bass("TRN2", target_bir_lowering=False)
        self.es = ExitStack()
        self.es.enter_context(nc.allow_non_contiguous_dma(reason="small strided parameter/state loads"))
        self.S = Sched(nc)
        self.uid = 0
        self.out_tiles = []
        self._declare_dram()
        self._alloc()
        self._consts()
        for l in range(DEPTH):
            self._layer(l)
        self.S.finalize(reorder=bool(int(os.environ.get("KSCHED", "1"))))
        self.es.close()

    def sb(self, name, shape, dt=F32):
        self.uid += 1
        t = self.es.enter_context(self.nc.sbuf_tensor(f"{name}_{self.uid}", list(shape), dt))
        return Buf(t, name)

    def _declare_dram(self):
        nc = self.nc
        di = lambda n, s: nc.dram_tensor(n, list(s), F32, kind="ExternalInput").ap()
        do = lambda n, s: nc.dram_tensor(n, list(s), F32, kind="ExternalOutput").ap()
        self.I = I = {}
        I["xp"] = di("xp", [L_P, D])
        I["xs"] = di("xs", [128, D])
        I["cache_rglru_conv"] = di("cache_rglru_conv", [DEPTH, 16, 3, 256])
        I["state_rglru"] = di("state_rglru", [DEPTH, 16, 256])
        I["cache_ssd_conv"] = di("cache_ssd_conv", [DEPTH, 16, 3, 512])
        I["state_ssd"] = di("state_ssd", [DEPTH, 16, 4, 64, 64])
        I["cache_gdn_conv"] = di("cache_gdn_conv", [DEPTH, 16, 3, 768])
        I["state_gdn"] = di("state_gdn", [DEPTH, 16, 4, 64, 64])
        I["state_s5_re"] = di("state_s5_re", [DEPTH, 16, 16, 64])
        I["state_s5_im"] = di("state_s5_im", [DEPTH, 16, 16, 64])
        for n, s in WSHAPES.items():
            I[n] = di(n, s)
        self.O = O = {}
        O["yp"] = do("yp", [L_P, D])
        O["ys"] = do("ys", [128, D])
        for pre, nb in (("p", 1), ("s", 16)):
            O[pre + "_rg_conv"] = do(pre + "_rg_conv", [DEPTH, nb, 3, 256])
            O[pre + "_rg"] = do(pre + "_rg", [DEPTH, nb, 256])
            O[pre + "_ssd_conv"] = do(pre + "_ssd_conv", [DEPTH, nb, 3, 512])
            O[pre + "_ssd"] = do(pre + "_ssd", [DEPTH, nb, 4, 64, 64])
            O[pre + "_gdn_conv"] = do(pre + "_gdn_conv", [DEPTH, nb, 3, 768])
            O[pre + "_gdn"] = do(pre + "_gdn", [DEPTH, nb, 4, 64, 64])
            O[pre + "_s5_re"] = do(pre + "_s5_re", [DEPTH, nb, 16, 64])
            O[pre + "_s5_im"] = do(pre + "_s5_im", [DEPTH, nb, 16, 64])
        self.x1 = nc.dram_tensor("x1_scratch", [L_P + 128, D], F32).ap()
        self.x1_t = [T(f"x1row{i}") for i in range((L_P + 128) // 128)]
        self.DBG = {}
        for n, s in self.dbg_req:
            self.DBG[n] = do("dbg_" + n, s)

    def _alloc(self):
        nc = self.nc
        self.psb = []
        for i in range(7):
            t = self.es.enter_context(nc.psum_tensor(f"ps{i}", [128, 512], F32))
            self.psb.append(Buf(t, f"ps{i}"))
            self.psb[-1].t.psum = True
        t = self.es.enter_context(nc.psum_tensor("pst", [128, 1024], BF16))
        self.pst = Buf(t, "pst")
        self.pst.t.psum = True
        self.ps_i = 0
        self.psC = 0
        self.psO = 0
        self.psD = 0
        self.psS = 0
        self.WbG = []
        for (c0, c1) in ((0, C_BX), (C_BX, C_CQ), (C_CQ, C_DU), (C_DU, INC)):
            self.WbG.append((c0, c1, self.sb(f"Wb{c0}", [128, 8, c1 - c0], BF16)))
        self.Wo = self.sb("Wo", [128, 8, D], BF16)
        self.Wsm = self.sb("Wsm", [128, 8, 12], BF16)
        self.wst_i = 0
        self.xrt = [self.sb(f"xres{i}", [128, D]) for i in range(NB // 128)]
        self.xbf = [self.sb("xbf0", [128, D], BF16)]
        self.xT = self.sb("xT", [128, 8, NB], BF16)
        self.xTl = self.sb("xTl", [128, 8, 48], BF16)
        self.mixT = [self.sb(f"mixT{i}", [128, NB], BF16) for i in range(8)]
        self.lng = self.sb("lng", [128, D])
        self.lnb = self.sb("lnb", [128, D])
        self.ybuf = [self.sb(f"ybuf{i}", [128, D]) for i in range(NYBUF)]
        self.lnstat = [self.sb(f"lnstat{i}", [128, 16]) for i in range(2)]
        self.cst_i = 0
        self.cpad = [self.sb(f"cpad{i}", [128, NB + 3], BF16) for i in range(12)]
        self.cacc = [self.sb(f"cacc{i}", [128, NB]) for i in range(4)]
        self.cd_i = 0
        self.cw = self.sb("cw", [128, 4, 12])
        self.cbias = self.sb("cbias", [128, 12])
        self.chist = [self.sb(f"chist{i}", [48, 128]) for i in range(2)]
        self.ftmp = [self.sb(f"ftmp{i}", [128, NB]) for i in range(10)]
        self.wtmp = [self.sb(f"wtmp{i}", [128, 512]) for i in range(3)]
        self.btmp = [self.sb(f"btmp{i}", [128, NB], BF16) for i in range(6)]
        self.xcA = [self.sb(f"xcA{i}", [128, NB]) for i in range(2)]
        self.xcbA = [self.sb(f"xcbA{i}", [128, NB], BF16) for i in range(2)]
        self.rgW = self.sb("rgW", [128, 2, 2, 128], BF16)
        self.rgp = self.sb("rgp", [128, 2, 8])
        self.rgh = self.sb("rgh", [128, 2])
        self.rgh0 = self.sb("rgh0", [128, 2, 16])
        self.rgst = self.sb("rgst", [16, 256])
        self.rgout = self.sb("rgout", [128, 2, 16])

        self.Um = {}; self.NEGM = {}; self.SG = {}
        for i, k in enumerate(("p", "s")):
            self.NEGM[k] = self.sb("NEGM" + k, [128, 128])
        self.Umb = {k: self.sb("Umb" + k, [128, 128], BF16) for k in ("p", "s")}
        self.SGb = {k: self.sb("SGb" + k, [128, 128], BF16) for k in ("p", "s")}
        self.dsplit = self.sb("dsplit", [128, 3, 8], BF16)
        self.dres = self.sb("dres", [128, 8])
        self.SeqSel = self.sb("SeqSel", [128, 16])
        self.SeqSelb = self.sb("SeqSelb", [128, 16], BF16)
        self.onesb = self.sb("onesb", [128, 128], BF16)
        self.smtok = self.sb("smtok", [128, NB // 128, 12])
        self.ssp = self.sb("ssp", [128, 16])
        self.ssD = self.sb("ssD", [128, 2])
        self.ssnw = self.sb("ssnw", [128, 2])
        self.H0T = self.sb("H0T", [128, 16, 2, 64], BF16)
        self.HTm = self.sb("HTm", [128, 2, 64])
        self.HTb = self.sb("HTb", [128, 2, 64], BF16)
        self.cq = [self.sb(f"cq{i}", [128, NB], BF16) for i in range(6)]
        self.cf = [self.sb(f"cf{i}", [128, NB]) for i in range(4)]
        self.csm = self.sb("csm", [128, 32])
        self.ctok = self.sb("ctok", [128, 512], BF16)
        self.csc = [self.sb(f"csc{i}", [128, 32]) for i in range(2)]
        self.cdsplit = self.sb("cdsplit", [128, 3, 8], BF16)
        self.cdres = self.sb("cdres", [128, 8])
        self.bw = [self.sb(f"bw{i}", [128, 512]) for i in range(2)]
        self.gdp = self.sb("gdp", [128, 16])
        self.gdnw = self.sb("gdnw", [128, 1])
        self.POSM = {k: self.sb("POSM" + k, [128, 128]) for k in ("p", "s")}
        self.blockones = self.sb("blockones", [128, 128], BF16)
        self.gM = [self.sb(f"gM{i}", [128, 4, 128], BF16) for i in range(3)]
        self.Moff = self.sb("Moff", [128, 4, 128], BF16)
        self.gR = self.sb("gR", [128, 4, 128], BF16)
        self.D32b = self.sb("D32b", [128, 128], BF16)
        self.gN = [self.sb(f"gN{i}", [128, 4, 128], BF16) for i in range(2)]
        self.gX = [self.sb(f"gX{i}", [128, 4, 128], BF16) for i in range(2)]
        self.gW = [self.gM[0], self.gM[1]]
        self.nGT = self.gN[0]
        self.QKG = self.sb("QKG", [128, 4, 128], BF16)
        self.XTb = self.sb("XTb", [128, 4, 128], BF16)
        self.Val = self.sb("Val", [128, 256])
        self.qdec = self.sb("qdec", [128, 4, 128], BF16)
        self.kdec = self.sb("kdec", [128, 4, 64], BF16)
        self.wtok = self.sb("wtok", [128, 4, 64], BF16)
        self.gsq = self.sb("gsq", [128, NB], BF16)
        self.Sm = self.sb("Sm", [128, 2, 64])
        self.Sb = self.sb("Sb", [128, 2, 64], BF16)
        self.s5Sel = self.sb("s5Sel", [128, 16, 128], BF16)
        self.s5SelU = self.sb("s5SelU", [128, 16, 32], BF16)
        self.s5Toep = self.sb("s5Toep", [128, 16, 128], BF16)
        self.s5RT = self.sb("s5RT", [128, 2, 16, 64], BF16)
        self.s5P = self.sb("s5P", [128, 2, 8, 128], BF16)
        self.s5E = self.sb("s5E", [128, 2, 8, 32])
        self.s5par = self.sb("s5par", [128, 8, 8])
        self.s5sm = self.sb("s5sm", [128, 8, 8])
        self.s5H = self.sb("s5H", [128, 2, 8])
        self.s5Hst = self.sb("s5Hst", [128, 2, 8, 32], BF16)
        self.s5Uf = self.sb("s5Uf", [128, 16, 32], BF16)
        self.s5Yf = self.sb("s5Yf", [128, 16, 32], BF16)
        self.s5uT = [self.sb(f"s5uT{i}", [96, NB], BF16) for i in range(3)]
        self.s5glu = self.sb("s5glu", [128, 2, 256], BF16)
        self.s5gb = self.sb("s5gb", [128, 2])
        self.s5Dfold = self.sb("s5Dfold", [128, 16])
        self.Cmask = self.sb("Cmask", [128, 128])
        self.s5q = self.sb("s5q", [128, 64])
        self.s5qi = self.sb("s5qi", [128, 64], I32)
        self.itmp = self.sb("itmp", [128, 256], I32)
        self.df = [self.sb(f"df{i}", [128, NB]) for i in range(6)]
        self.dbf = [self.sb(f"dbf{i}", [128, NB], BF16) for i in range(2)]
        self.tok = self.sb("tok", [128, 512], BF16)
        self.sc = [self.sb(f"sc{i}", [128, 32]) for i in range(2)]
        self.Wt = self.sb("Wt", [128, 4, 128], BF16)
        self.Cdec = self.sb("Cdec", [128, 4, 128], BF16)
        self.Bdec = self.sb("Bdec", [128, 4, 64], BF16)
        self.etotb = self.sb("etotb", [64, 4, 16])

    def bexp(self, hf):
        return self.gX[hf].v().rr("p h (a n) -> p (h a) n", n=64)

    def hnat(self, hh):
        src = self.xrt[1][0:64, :] if hh == 0 else self.ybuf[0][0:64, :]
        return src.rr("p (b n) -> p b n", n=64)

    def ps(self):
        cur = getattr(self, "cur", None)
        if cur == "C":
            b = self.psb[self.psC % 3]
            self.psC += 1
        elif cur == "B":
            b = self.psb[3 + self.psO % 2]
            self.psO += 1
        elif cur == "S1":
            b = self.psb[self.psS % 5]
            self.psS += 1
        elif cur == "D":
            b = self.psb[5 + self.psD % 2]
            self.psD += 1
        else:
            b = self.psb[self.ps_i % 7]
            self.ps_i += 1
        b.t.gen += 1
        return Buf(b.tensor, b.t.name, t=b.t, gen=b.t.gen)

    def _consts(self):
        nc, S = self.nc, self.S
        self.identf = self.sb("identf", [128, 128])
        self.ident = self.sb("ident", [128, 128], BF16)
        S.memset("pool", self.identf.v(), 1.0)
        S.emit("pool", lambda: nc.gpsimd.affine_select(out=self.identf.tensor[:], in_=self.identf.tensor[:], pattern=[[-1, 128]],
                                                       compare_op=ALU.is_equal, fill=0.0, base=0, channel_multiplier=1),
               [self.identf.v()], [self.identf.v()])
        S.copy("dve", self.ident.v(), self.identf.v())
        class _V:
            def __init__(s_, view):
                s_.view = view
            def v(s_):
                return s_.view
        for i, k in enumerate(("p", "s")):
            self.Um[k] = _V(self.ftmp[i][:, 0:128])
            self.SG[k] = _V(self.ftmp[2 + i][:, 0:128])
        S.memset("pool", self.onesb.v(), 1.0)
        asel = lambda buf, pat, cm, base: S.emit("pool", lambda: nc.gpsimd.affine_select(
            out=buf.ap, in_=buf.ap, pattern=pat, compare_op=ALU.is_ge, fill=0.0, base=base, channel_multiplier=cm), [buf], [buf])
        S.memset("pool", self.SG["p"].v(), 1.0)
        S.memset("pool", self.SG["s"].v(), 1.0)
        sg3 = self.SG["s"].v().rr("p (b j) -> p b j", j=8)
        asel(sg3, [[-8, 16], [0, 8]], 1, 0)
        asel(sg3, [[8, 16], [0, 8]], -1, 7)
        S.memset("pool", self.Um["p"].v(), 1.0)
        asel(self.Um["p"].v(), [[1, 128]], -1, 0)
        S.tt("pool", self.Um["s"].v(), self.Um["p"].v(), self.SG["s"].v(), ALU.mult)
        for k in ("p", "s"):
            S.ts("pool", self.NEGM[k].v(), self.Um[k].v(), -NEG, ALU.mult, NEG, ALU.add)
        for k in ("p", "s"):
            S.copy("pool", self.Umb[k].v(), self.Um[k].v())
            S.copy("pool", self.SGb[k].v(), self.SG[k].v())
        for k in ("p", "s"):
            S.memset("pool", self.POSM[k].v(), 1.0)
            asel(self.POSM[k].v(), [[-1, 128]], 1, -1)
            S.tt("pool", self.POSM[k].v(), self.POSM[k].v(), self.SG[k].v(), ALU.mult)
            S.ts("pool", self.POSM[k].v(), self.POSM[k].v(), NEG, ALU.mult, -NEG, ALU.add)
        bo = _V(self.ftmp[4][:, 0:128])
        S.memset("pool", bo.v(), 1.0)
        bo3 = bo.v().rr("p (b j) -> p b j", j=64)
        asel(bo3, [[-64, 2], [0, 64]], 1, 0)
        asel(bo3, [[64, 2], [0, 64]], -1, 63)
        S.copy("pool", self.blockones.v(), bo.v())
        S.memset("pool", bo.v(), 1.0)
        bo4 = bo.v().rr("p (b j) -> p b j", j=32)
        asel(bo4, [[-32, 4], [0, 32]], 1, 0)
        asel(bo4, [[32, 4], [0, 32]], -1, 31)
        S.copy("pool", self.D32b.v(), bo.v())
        sel = self.s5Sel
        S.memset("pool", sel.v(), 1.0)
        b0 = sel[0:32, :, :].rr("p (a s) c -> p a s c", a=2)
        S.emit("pool", lambda: nc.gpsimd.affine_select(out=b0.ap, in_=b0.ap, pattern=[[16, 2], [-16, 8], [1, 128]], compare_op=ALU.is_equal,
                                                       fill=0.0, base=0, channel_multiplier=-1), [b0], [b0])
        asel(b0, [[-16, 2], [0, 8], [0, 128]], 1, 0)
        asel(b0, [[16, 2], [0, 8], [0, 128]], -1, 15)
        S.copy("pool", sel[32:64, :, :], sel[0:32, :, :])
        S.copy("pool", sel[64:96, :, :], sel[0:32, :, :])
        su = self.s5SelU.v().rr("p (a s) c -> p a s c", a=2)
        S.memset("pool", su, 1.0)
        S.emit("pool", lambda: nc.gpsimd.affine_select(out=su.ap, in_=su.ap, pattern=[[16, 2], [-16, 8], [-1, 32]], compare_op=ALU.is_equal,
                                                       fill=0.0, base=0, channel_multiplier=1), [su], [su])
        asel(su, [[-16, 2], [0, 8], [1, 32]], 0, 0)
        asel(su, [[16, 2], [0, 8], [-1, 32]], 0, 15)
        S.memset("pool", self.Cmask.v(), 1.0)
        asel(self.Cmask.v().rr("p (t i) -> p t i", i=16), [[16, 8], [0, 16]], -1, 15)
        for seg, (base, step) in enumerate(((7, -1), (1, 1), (0, -1), (0, 1))):
            S.emit("pool", lambda seg=seg, step=step, base=base: nc.gpsimd.iota(self.s5qi.tensor[:, seg * 8:(seg + 1) * 8], pattern=[[step, 8]], base=base,
                                                  channel_multiplier=0), [], [self.s5qi.v()])
        S.emit("pool", lambda: nc.gpsimd.iota(self.s5qi.tensor[:, 32:64], pattern=[[1, 32]], base=1, channel_multiplier=0),
               [], [self.s5qi.v()])
        S.copy("pool", self.s5q.v(), self.s5qi.v())
        S.memset("pool", self.SeqSel.v(), 1.0)
        asel(self.SeqSel.v(), [[-8, 16]], 1, 0)
        asel(self.SeqSel.v(), [[8, 16]], -1, 7)
        S.copy("pool", self.SeqSelb.v(), self.SeqSel.v())

    def dbg(self, name, view):
        if name in self.DBG:
            self.S.dma_out("sp", self.DBG[name], view)
            self.out_tiles.extend(view.ts)

    def _load_weights(self, l):
        S, I = self.S, self.I
        for (g0, g1, buf) in self.WbG:
            for k in range(8):
                c0 = g0
                while c0 < g1:
                    w = min(512, g1 - c0)
                    S.dma_in("pool", buf[:, k, c0 - g0:c0 - g0 + w], I["w_in"][l, k * 128:(k + 1) * 128, c0:c0 + w])
                    c0 += w
        for k in range(8):
            for hf in range(2):
                S.dma_in("pool", self.Wo[:, k, hf * 512:(hf + 1) * 512], I["w_out"][l, k * 128:(k + 1) * 128, hf * 512:(hf + 1) * 512])
        S.copy("dve", self.Wsm[:, :, 0:4], self.wb(None, C_BDT, 4))
        S.copy("dve", self.Wsm[:, :, 4:12], self.wb(None, C_CBETA, 8))
        S.dma_in("sp", self.lng.v(), I["ln_g"][l:l + 1, :].to_broadcast([128, D]))
        S.dma_in("sp", self.lnb.v(), I["ln_b"][l:l + 1, :].to_broadcast([128, D]))
        for (nm, c0, nch) in (("rg", 0, 2), ("ssd", 2, 4), ("gdn", 6, 6)):
            for j in range(4):
                S.dma_in("sp", self.cw[:, j, c0:c0 + nch], I[nm + "_conv_w"][l, j].rearrange("(c p) -> p c", p=128),
                         allow_slow_non_contiguous=True)
            S.dma_in("sp", self.cbias[:, c0:c0 + nch], I[nm + "_conv_b"][l].rearrange("(c p) -> p c", p=128),
                     allow_slow_non_contiguous=True)
        if "A" in self.mixers:
            self._load_A(l)
        if "B" in self.mixers:
            self._load_B(l)
        if "C" in self.mixers:
            self._load_C(l)
        if "D" in self.mixers:
            self._load_D(l)

    def _layer(self, l):
        self._load_weights(l)
        blocks = [Blk(i, False) for i in range(L_P // NB)] + [Blk(0, True)]
        for blk in blocks:
            self._block(l, blk)

    def _xsrc(self, l, blk):
        if l == 0:
            return self.I["xs"] if blk.sample else self.I["xp"][blk.tok0:blk.tok0 + blk.N, :]
        return self.x1[blk.tok0:blk.tok0 + blk.N, :]

    def _ydst(self, l, blk):
        if l == DEPTH - 1:
            return self.O["ys"] if blk.sample else self.O["yp"][blk.tok0:blk.tok0 + blk.N, :]
        return self.x1[blk.tok0:blk.tok0 + blk.N, :]

    def _block(self, l, blk):
        S = self.S
        N = blk.N
        src = self._xsrc(l, blk).rearrange("(t p) d -> p t d", p=128)
        self.xr_i = getattr(self, "xr_i", 0) + 1
        t0 = blk.tok0 // 128
        for ti in range(blk.ntile):
            S.dma_in("sp", self.xrt[ti].v(), src[:, ti, :], dram_reads=([self.x1_t[t0 + ti]] if l > 0 else ()))
        for ti in range(blk.ntile):
            xb = self.xbf[0]
            S.copy("act", xb.v(), self.xrt[ti].v())
            for k in range(8):
                S.tr(self.pst[:, k * 128:(k + 1) * 128], xb[:, k * 128:(k + 1) * 128], self.ident.v())
            S.copy("act", self.xT[:, :, ti * 128:(ti + 1) * 128], self.pst.v().rr("p (k t) -> p k t", k=8))
        if blk.last or blk.sample:
            self._conv_cache_out(l, blk)
        if "B" in self.mixers or "C" in self.mixers:
            for ti in range(blk.ntile):
                p = self.ps()
                for k in range(8):
                    S.mm(p[:, 0:12], self.xT[:, k, ti * 128:(ti + 1) * 128], self.Wsm[:, k, :], start=(k == 0), stop=(k == 7))
                S.copy("dve", self.smtok[:, ti, :], p[:, 0:12])
        self._conv_phase(l, blk)
        gens = {}
        for mi, m in enumerate("ABCD"):
            if m in self.mixers:
                gens[m] = getattr(self, "_mixer_" + m)(l, blk)
            else:
                for i in (2 * mi, 2 * mi + 1):
                    S.memset("pool", self.mixT[i][:, 0:N], 0.0)
        if not INTERLEAVE or "C" not in gens or (blk.sample and "D" not in gens):
            for m in "ABCD":
                if m in gens:
                    for _ in gens[m]:
                        pass
        elif blk.sample:
            streams = [("S1", itertools.chain(*[gens[m] for m in "ABC" if m in gens])), ("D", gens["D"])]
            alive = list(streams)
            while alive:
                for item in list(alive):
                    self.cur = item[0]
                    try:
                        next(item[1])
                    except StopIteration:
                        alive.remove(item)
            self.cur = None
        else:
            streams = [("C", gens["C"]), ("B", itertools.chain(*[gens[m] for m in "AB" if m in gens]))]
            if "D" in gens:
                streams.append(("D", gens["D"]))
            alive = list(streams)
            while alive:
                for item in list(alive):
                    self.cur = item[0]
                    try:
                        next(item[1])
                    except StopIteration:
                        alive.remove(item)
            self.cur = None
        dst = self._ydst(l, blk).rearrange("(t p) d -> p t d", p=128)
        for ti in range(blk.ntile):
            self._outproj_tile(l, blk, ti, dst)

    def _conv_phase(self, l, blk):
        S = self.S
        N = blk.N
        jobs = []
        if "A" in self.mixers:
            jobs += [("A", fc) for fc in range(2)]
        if "B" in self.mixers:
            jobs += [("B", i) for i in range(4)]
        if "C" in self.mixers:
            jobs += [("C", i) for i in range(6)]
        for (m, i) in jobs:
            if m == "A":
                pc = self.conv_chunk(blk, l, i, C_AX + i * 128, None)
                S.act(self.xcA[i][:, 0:N], pc, AF.Identity, bias=self.cbias[:, i:i + 1], scale=1.0)
                S.act(self.xcbA[i][:, 0:N], pc, AF.Identity, bias=self.cbias[:, i:i + 1], scale=1.0)
            elif m == "B":
                cc = 2 + i
                pc = self.conv_chunk(blk, l, cc, C_BX + i * 128, None)
                S.act(self.btmp[i][:, 0:N], pc, AF.Silu, bias=self.cbias[:, cc:cc + 1], scale=1.0)
            else:
                cc = 6 + i
                pc = self.conv_chunk(blk, l, cc, C_CQ + i * 128, None)
                S.act(self.cq[i][:, 0:N], pc, AF.Silu, bias=self.cbias[:, cc:cc + 1], scale=1.0)

    def wb(self, k, c0, w):
        for (g0, g1, buf) in self.WbG:
            if g0 <= c0 and c0 + w <= g1:
                return buf[:, :, c0 - g0:c0 - g0 + w] if k is None else buf[:, k, c0 - g0:c0 - g0 + w]
        raise AssertionError((c0, w))

    def zmm(self, blk, c0, width, out_ps):
        for k in range(8):
            self.S.mm(out_ps, self.wb(k, c0, width), self.xT[:, k, 0:blk.N], start=(k == 0), stop=(k == 7))

    def _conv_cache_out(self, l, blk):
        S = self.S
        pre = "s" if blk.sample else "p"
        if blk.sample:
            M = 48
            for k in range(8):
                S.copy("dve", self.xTl[:, k, :].rr("p (b j) -> p b j", j=3),
                       self.xT[:, k, 0:128].rr("p (b j) -> p b j", j=8)[:, :, 5:8])
            lhs = lambda k: self.xTl[:, k, :]
        else:
            M = 3
            lhs = lambda k: self.xT[:, k, NB - 3:NB]
        for (nm, c0, w) in (("_rg_conv", C_AX, 256), ("_ssd_conv", C_BX, 512), ("_gdn_conv", C_CQ, 512), ("_gdn_conv2", C_CQ + 512, 256)):
            p = self.ps()
            for k in range(8):
                S.mm(p[0:M, 0:w], lhs(k), self.wb(k, c0, w), start=(k == 0), stop=(k == 7))
            st = self.ybuf[0]
            self.cst_i += 1
            S.copy("act", st[0:M, 0:w], p[0:M, 0:w])
            if nm == "_gdn_conv2":
                dst = self.O[pre + "_gdn_conv"][l].rearrange("b j c -> (b j) c")[:, 512:768]
            elif nm == "_gdn_conv":
                dst = self.O[pre + "_gdn_conv"][l].rearrange("b j c -> (b j) c")[:, 0:512]
            else:
                dst = self.O[pre + nm][l].rearrange("b j c -> (b j) c")
            S.dma_out("sp", dst, st[0:M, 0:w])
            self.out_tiles.append(st.t)

    def _outproj_tile(self, l, blk, ti, dst):
        S, nc = self.S, self.nc
        pa, pb = self.ps(), self.ps()
        tsl = slice(ti * 128, (ti + 1) * 128)
        for half, p in ((0, pa), (1, pb)):
            for k in range(8):
                S.mm(p.v(), self.mixT[k][:, tsl], self.Wo[:, k, half * 512:(half + 1) * 512], start=(k == 0), stop=(k == 7))
        self.yb_i = getattr(self, "yb_i", 0) + 1
        r = self.ybuf[self.yb_i % NYBUF]
        st = self.lnstat[ti % 2]
        for half, p in ((0, pa), (1, pb)):
            hs = slice(half * 512, (half + 1) * 512)
            S.stt(r[:, hs], self.xrt[ti][:, hs], ALPHA, p.v(), ALU.mult, ALU.add)
            S.emit("dve", lambda half=half, hs=hs: nc.vector.bn_stats(out=st.tensor[:, half * 6:(half + 1) * 6], in_=r.tensor[:, hs]), [r.v()], [st.v()])
        S.emit("dve", lambda: nc.vector.bn_aggr(out=st.tensor[:, 12:14], in_=st.tensor[:, 0:12]), [st.v()], [st.v()])
        self.rsqrt(st[:, 14:15], st[:, 13:14], 1.0, self.eps5())
        S.stt(st[:, 15:16], st[:, 12:13], -1.0, st[:, 14:15], ALU.mult, ALU.mult)
        y = r
        S.act(y.v(), r.v(), AF.Identity, bias=st[:, 15:16], scale=st[:, 14:15])
        S.tt("dve", y.v(), y.v(), self.lng.v(), ALU.mult)
        S.tt("dve", y.v(), y.v(), self.lnb.v(), ALU.add)
        S.dma_out("sp", dst[:, ti, :], y.v(), dram_writes=([self.x1_t[blk.tok0 // 128 + ti]] if l < DEPTH - 1 else ()))
        if l == 0 and "x1" in self.DBG:
            r0 = blk.tok0 + ti * 128
            S.dma_out("sp", self.DBG["x1"][r0:r0 + 128, :], y.v())
        self.out_tiles.append(y.t)

    def eps5(self):
        if not hasattr(self, "_eps5"):
            self._eps5 = self.sb("eps5", [128, 1])
            self.S.memset("pool", self._eps5.v(), 1e-5)
        return self._eps5.v()

    def conv_chunk(self, blk, l, cc, zc0, out_ps, m0=0, m1=128):
        S = self.S
        N = blk.N
        cp = self.cpad[cc]
        acc = self.cacc[self.cd_i % 4]
        self.cd_i += 1
        pz = self.ps()
        self.zmm(blk, zc0, 128, pz[:, 0:N])
        if blk.sample:
            cpv = cp[:, 0:176].rr("p (g l) -> p g l", l=11)
            ph = self.ps()
            ch = self.chist[cc % 2]
            nm, c0 = (("cache_rglru_conv", cc * 128) if cc < 2 else ("cache_ssd_conv", (cc - 2) * 128) if cc < 6
                      else ("cache_gdn_conv", (cc - 6) * 128))
            S.dma_in("sp", ch.v(), self.I[nm][l].rearrange("b j c -> (b j) c")[:, c0:c0 + 128])
            S.tr(ph[:, 0:48], ch[0:48, :], self.identf[0:48, 0:48])
            S.copy("dve", cpv[:, :, 0:3], ph[:, 0:48].rr("p (b j) -> p b j", j=3))
            S.copy("act", cpv[:, :, 3:11], pz[:, 0:N].rr("p (b j) -> p b j", j=8))
            rhs = lambda j: cpv[:, :, j:j + 8]
            accv = acc[:, 0:N].rr("p (b j) -> p b j", j=8)
        else:
            if blk.first:
                S.memset("pool", cp[:, 0:3], 0.0)
            else:
                S.copy("dve", cp[:, 0:3], cp[:, NB:NB + 3])
            S.copy("act", cp[:, 3:NB + 3], pz[:, 0:N])
            rhs = lambda j: cp[:, j:j + NB]
            accv = acc[:, 0:N]
        S.act(accv, rhs(0), AF.Copy, scale=self.cw[:, 0, cc:cc + 1])
        for j in range(1, 4):
            S.stt(accv, rhs(j), self.cw[:, j, cc:cc + 1], accv, ALU.mult, ALU.add)
        return acc[:, 0:N]

    def _load_A(self, l):
        S, I = self.S, self.I
        rgWst = self.wtmp[0].v().rr("p (a b c) -> p a b c", a=2, b=2)
        S.memset("pool", rgWst, 0.0)
        for gi, nm in enumerate(("rg_gate_a_w", "rg_gate_x_w")):
            for fc in range(2):
                for h2 in range(2):
                    S.dma_in("sp", rgWst[h2 * 64:(h2 + 1) * 64, fc, gi, h2 * 64:(h2 + 1) * 64], I[nm][l, fc * 2 + h2])
        S.copy("dve", self.rgW.v(), rgWst)
        for i, nm in enumerate(("rg_gate_a_b", "rg_gate_x_b", "rg_lambda")):
            S.dma_in("sp", self.rgp[:, :, i], I[nm][l].rearrange("(c p) -> p c", p=128), allow_slow_non_contiguous=True)
        S.act(self.rgp[:, :, 3], self.rgp[:, :, 2], AF.Exp, scale=-1.0)
        S.act(self.rgp[:, :, 3], self.rgp[:, :, 3], AF.Ln, bias=1.0)
        S.ts("dve", self.rgp[:, :, 4], self.rgp[:, :, 3], -16.0, ALU.mult)
        S.ts("dve", self.rgp[:, :, 3], self.rgp[:, :, 3], -8.0, ALU.mult)
        S.dma_in("sp", self.rgst.v(), I["state_rglru"][l])
        for fc in range(2):
            p = self.ps()
            S.tr(p[:, 0:16], self.rgst[0:16, fc * 128:(fc + 1) * 128], self.identf[0:16, 0:16])
            S.copy("dve", self.rgh0[:, fc, :], p[:, 0:16])

    def _mixer_A(self, l, blk):
        S = self.S
        N = blk.N
        AT = self.ftmp
        for fc in range(2):
            xc = self.xcA[fc]
            xcb = self.xcbA[fc]
            yield
            pr, pi = self.ps(), self.ps()
            S.mm(pr[:, 0:N], self.rgW[:, fc, 0, :], xcb[:, 0:N])
            S.mm(pi[:, 0:N], self.rgW[:, fc, 1, :], xcb[:, 0:N])
            gr, gi = AT[1], AT[2]
            S.act(gr[:, 0:N], pr[:, 0:N], AF.Sigmoid, bias=self.rgp[:, fc, 0:1], scale=1.0)
            S.act(gi[:, 0:N], pi[:, 0:N], AF.Sigmoid, bias=self.rgp[:, fc, 1:2], scale=1.0)
            a, a2 = AT[3], AT[4]
            S.act(a[:, 0:N], gr[:, 0:N], AF.Exp, scale=self.rgp[:, fc, 3:4])
            S.act(a2[:, 0:N], gr[:, 0:N], AF.Exp, scale=self.rgp[:, fc, 4:5])
            S.act(a2[:, 0:N], a2[:, 0:N], AF.Ln, bias=1.0, scale=-1.0)
            S.act(a2[:, 0:N], a2[:, 0:N], AF.Exp, scale=0.5)
            bb = AT[5]
            S.tt("dve", bb[:, 0:N], a2[:, 0:N], gi[:, 0:N], ALU.mult)
            S.tt("dve", bb[:, 0:N], bb[:, 0:N], xc[:, 0:N], ALU.mult)
            h = AT[6]
            if blk.sample:
                a3 = a[:, 0:N].rr("p (b j) -> p b j", j=8)
                b3 = bb[:, 0:N].rr("p (b j) -> p b j", j=8)
                tmp = AT[7]
                S.tt("dve", tmp[:, 0:16], a3[:, :, 0], self.rgh0[:, fc, :], ALU.mult)
                S.tt("dve", b3[:, :, 0], b3[:, :, 0], tmp[:, 0:16], ALU.add)
                S.memset("dve", a3[:, :, 0], 0.0)
                S.scan(h[:, 0:N], a[:, 0:N], bb[:, 0:N], 0.0)
            else:
                init = 0.0 if blk.first else self.rgh[:, fc:fc + 1]
                S.scan(h[:, 0:N], a[:, 0:N], bb[:, 0:N], init)
                S.copy("dve", self.rgh[:, fc:fc + 1], h[:, N - 1:N])
            yield
            pg = self.ps()
            self.zmm(blk, C_AG + fc * 128, 128, pg[:, 0:N])
            sg = AT[8]
            S.act(sg[:, 0:N], pg[:, 0:N], AF.Silu)
            S.tt("dve", self.mixT[fc][:, 0:N], h[:, 0:N], sg[:, 0:N], ALU.mult)
            if blk.last:
                S.dma_out("sp", self.O["p_rg"][l, 0, fc * 128:(fc + 1) * 128].rearrange("(p o) -> p o", o=1), self.rgh[:, fc:fc + 1])
                self.out_tiles.append(self.rgh.t)
            if blk.sample:
                h3 = h[:, 0:N].rr("p (b j) -> p b j", j=8)
                S.copy("dve", self.rgout[:, fc, :], h3[:, :, 7])
                S.dma_out("sp", self.O["s_rg"][l][:, fc * 128:(fc + 1) * 128].rearrange("b p -> p b"), self.rgout[:, fc, :],
                          allow_slow_non_contiguous=True)
                self.out_tiles.append(self.rgout.t)

    def _load_B(self, l):
        S, I = self.S, self.I
        S.dma_in("sp", self.ssp[:, 0:4], I["ssd_dt_bias"][l:l + 1, :].to_broadcast([128, 4]))
        S.dma_in("sp", self.ssp[:, 4:8], I["ssd_a_log"][l:l + 1, :].to_broadcast([128, 4]))
        S.act(self.ssp[:, 8:12], self.ssp[:, 4:8], AF.Exp)
        S.ts("dve", self.ssp[:, 8:12], self.ssp[:, 8:12], -1.0, ALU.mult)
        for h in range(4):
            h2, fc = h % 2, h // 2
            S.dma_in("sp", self.ssD[64 * h2:64 * h2 + 64, fc:fc + 1], I["ssd_d"][l:l + 1, h:h + 1].to_broadcast([64, 1]))
        S.dma_in("sp", self.ssnw.v(), I["ssd_norm_w"][l].rearrange("(c p) -> p c", p=128), allow_slow_non_contiguous=True)

    def _load_B_sample(self, l):
        S, I = self.S, self.I
        for h in range(4):
            g, h2 = h // 2, h % 2
            if h2 == 0:
                for hh in range(2):
                    S.dma_in("sp", self.hnat(hh), I["state_ssd"][l][:, 2 * g + hh].rearrange("b p n -> p b n"))
            for bb in range(2):
                p = self.ps()
                for j in range(8):
                    S.tr(p[0:64, j * 64:(j + 1) * 64], self.hnat(h2)[:, bb * 8 + j, :], self.identf[0:64, 0:64])
                S.copy("act" if bb else "dve", self.H0T[64 * g:64 * g + 64, bb * 8:(bb + 1) * 8, h2, :],
                       p[0:64, :].rr("n (b p) -> n b p", p=64))

    def chunk_decay(self, blk, ti, dta, sc, pAC, pACm, dsplit=None, dres=None):
        S = self.S
        mk = "s" if blk.sample else "p"
        dsplit = dsplit or self.dsplit
        dres = dres or self.dres
        self.split3(dta, 4, dsplit, dres)
        p = self.ps()
        for i in range(3):
            S.mm(p[:, 0:4], self.Umb[mk].v(), dsplit[:, i, 0:4], start=(i == 0), stop=(i == 2))
        for i in range(3):
            S.mm(p[:, 4:8], self.SGb[mk].v(), dsplit[:, i, 0:4], start=(i == 0), stop=(i == 2))
        S.copy("dve", sc[:, 0:8], p[:, 0:8])
        S.ts("dve", sc[:, 8:12], sc[:, 0:4], -1.0, ALU.mult)
        for h in range(4):
            hs = slice(h * 128, (h + 1) * 128)
            for i in range(3):
                S.mm(pAC[:, hs], dsplit[:, i, h:h + 1].bc([128, 128]), self.Umb[mk].v(), start=(i == 0), stop=(i == 2))

    def split3(self, src, w, dsplit, dres):
        S = self.S
        S.copy("dve", dsplit[:, 0, 0:w], src)
        S.tt("dve", dres[:, 0:w], src, dsplit[:, 0, 0:w], ALU.subtract)
        S.copy("dve", dsplit[:, 1, 0:w], dres[:, 0:w])
        S.tt("dve", dres[:, 0:w], dres[:, 0:w], dsplit[:, 1, 0:w], ALU.subtract)
        S.copy("dve", dsplit[:, 2, 0:w], dres[:, 0:w])

    def _mixer_B(self, l, blk):
        S = self.S
        N = blk.N
        xsb = [self.btmp[0], self.btmp[1]]
        BT, CT = self.btmp[2], self.btmp[3]
        dt = self.ftmp[0]
        dta = self.ftmp[1]
        nt = blk.ntile
        dt3 = dt[:, 0:nt * 4].rr("p (t h) -> p t h", h=4)
        dta3 = dta[:, 0:nt * 4].rr("p (t h) -> p t h", h=4)
        for ti in range(nt):
            S.tt("dve", dt3[:, ti, :], self.smtok[:, ti, 0:4], self.ssp[:, 0:4], ALU.add)
        S.act(dt[:, 0:nt * 4], dt[:, 0:nt * 4], AF.Exp)
        S.act(dt[:, 0:nt * 4], dt[:, 0:nt * 4], AF.Ln, bias=1.0)
        for ti in range(nt):
            S.tt("dve", dta3[:, ti, :], dt3[:, ti, :], self.ssp[:, 8:12], ALU.mult)
        yb = [self.ftmp[2], self.ftmp[3]]
        if blk.first:
            S.memset("pool", self.HTm.v(), 0.0)
            S.memset("pool", self.HTb.v(), 0.0)
        if blk.sample:
            self._load_B_sample(l)
        for ti in range(nt):
            tsl = slice(ti * 128, (ti + 1) * 128)
            sc = self.sc[ti % 2]
            for i, srcb in enumerate((xsb[0], xsb[1], BT)):
                S.tr(self.pst[:, i * 128:(i + 1) * 128], srcb[:, tsl], self.ident.v())
            S.copy("act", self.tok[:, 0:384], self.pst[:, 0:384])
            yield
            pAC, pACm = self.ps(), None
            self.chunk_decay(blk, ti, dta3[:, ti, :], sc, pAC, pACm)
            S.tt("dve", sc[:, 12:16], sc[:, 4:8], sc[:, 0:4], ALU.subtract)
            S.act(sc[:, 12:16], sc[:, 12:16], AF.Exp)
            S.tt("dve", sc[:, 12:16], sc[:, 12:16], dt3[:, ti, :], ALU.mult)
            S.act(sc[:, 16:20], sc[:, 4:8], AF.Exp)
            Edec, LT = self.bw
            S.act(Edec.v(), pAC.v(), AF.Exp)
            for h in range(4):
                hs = slice(h * 128, (h + 1) * 128)
                S.stt(LT[:, hs], pAC[:, hs], sc[:, 8 + h:9 + h], self.NEGM["s" if blk.sample else "p"].v(), ALU.add, ALU.add)
            S.act(LT.v(), LT.v(), AF.Exp)
            yield
            pG = [self.ps(), self.ps()]
            for g in range(2):
                S.mm(pG[g][:, 0:128], BT[64 * g:64 * g + 64, tsl], CT[64 * g:64 * g + 64, tsl])
            for h in range(4):
                hs = slice(h * 128, (h + 1) * 128)
                S.stt(self.Wt[:, h, :], LT[:, hs], dt3[:, ti, h:h + 1], pG[h // 2][:, 0:128], ALU.mult, ALU.mult)
            for g in range(2):
                gs = slice(64 * g, 64 * g + 64)
                S.tt("dve", self.Cdec[gs, 2 * g:2 * g + 2, :], Edec[gs, :].rr("p (h t) -> p h t", h=4)[:, 2 * g:2 * g + 2, :],
                     CT[gs, tsl].rr("p (o t) -> p o t", o=1).bc([64, 2, 128]), ALU.mult)
            for h in range(4):
                S.ts("dve", self.Bdec[:, h, :], self.tok[:, 256 + (h // 2) * 64:256 + (h // 2) * 64 + 64], sc[:, 12 + h:13 + h], ALU.mult)
            yield
            py = [self.ps(), self.ps()]
            for h in range(4):
                g, h2, fc = h // 2, h % 2, h // 2
                gs = slice(64 * g, 64 * g + 64)
                out = py[fc][64 * h2:64 * h2 + 64, 0:128]
                if blk.sample:
                    S.mm(out, self.tok[:, h * 64:(h + 1) * 64], self.Wt[:, h, :], start=True, stop=False)
                    for b in range(16):
                        S.mm(py[fc][64 * h2:64 * h2 + 64, 8 * b:8 * b + 8], self.H0T[gs, b, h2, :], self.Cdec[gs, h, 8 * b:8 * b + 8],
                             start=False, stop=(b == 15))
                else:
                    nostate = blk.first and ti == 0
                    S.mm(out, self.tok[:, h * 64:(h + 1) * 64], self.Wt[:, h, :], start=True, stop=nostate)
                    if not nostate:
                        S.mm(out, self.HTb[gs, h2, :], self.Cdec[gs, h, :], start=False, stop=True)
            for fc in range(2):
                S.stt(yb[fc][:, tsl], xsb[fc][:, tsl], self.ssD[:, fc:fc + 1], py[fc][:, 0:128], ALU.mult, ALU.add)
            if blk.sample:
                yield
                pE = self.ps()
                for h in range(4):
                    for i in range(3):
                        S.mm(pE[0:64, h * 16:(h + 1) * 16], self.dsplit[:, i, h:h + 1].bc([128, 64]), self.SeqSelb.v(),
                             start=(i == 0), stop=(i == 2))
                S.act(self.etotb.v().rr("p h b -> p (h b)"), pE[0:64, 0:64], AF.Exp)
                for h in range(4):
                    if h % 2 == 0:
                        for hh in range(2):
                            S.dma_in("sp", self.hnat(hh), self.I["state_ssd"][l][:, h + hh].rearrange("b p n -> p b n"))
                    for hf in range(2):
                        S.tt("dve", self.bexp(hf), self.Bdec[:, h, :].rr("p (o n) -> p o n", o=1).bc([128, 8, 64]),
                             self.SeqSelb[:, hf * 8:(hf + 1) * 8].rr("p (b o) -> p b o", o=1).bc([128, 8, 64]), ALU.mult)
                    yield
                    pn = [self.ps(), self.ps()]
                    for hf in range(2):
                        S.mm(pn[hf][0:64, :], self.tok[:, h * 64:(h + 1) * 64],
                             self.bexp(hf).rr("p b n -> p (b n)"))
                    for hf in range(2):
                        bs = slice(hf * 8, hf * 8 + 8)
                        S.tt("dve", self.hnat(h % 2)[:, bs, :], self.hnat(h % 2)[:, bs, :],
                             self.etotb[:, h, bs].rr("p (b o) -> p b o", o=1).bc([64, 8, 64]), ALU.mult)
                        S.tt("dve", self.hnat(h % 2)[:, bs, :], self.hnat(h % 2)[:, bs, :], pn[hf][0:64, :].rr("p (b n) -> p b n", n=64), ALU.add)
                    if h % 2 == 1:
                        for hh in range(2):
                            S.dma_out("sp", self.O["s_ssd"][l][:, h - 1 + hh].rearrange("b p n -> p b n"), self.hnat(hh))
                self.out_tiles.extend([self.xrt[1].t, self.ybuf[0].t])
            else:
                yield
                pH = self.ps()
                for h in range(4):
                    g, h2 = h // 2, h % 2
                    gs = slice(64 * g, 64 * g + 64)
                    S.mm(pH[gs, h2 * 64:(h2 + 1) * 64], self.Bdec[:, h, :], self.tok[:, h * 64:(h + 1) * 64])
                for h in range(4):
                    g, h2 = h // 2, h % 2
                    gs = slice(64 * g, 64 * g + 64)
                    S.stt(self.HTm[gs, h2, :], self.HTm[gs, h2, :], sc[gs, 16 + h:17 + h], pH[gs, h2 * 64:(h2 + 1) * 64], ALU.mult, ALU.add)
                S.copy("dve", self.HTb.v(), self.HTm.v())
        if blk.last:
            for h in range(4):
                g, h2 = h // 2, h % 2
                gs = slice(64 * g, 64 * g + 64)
                yield
                p = self.ps()
                S.tr(p[0:64, 0:64], self.HTm[gs, h2, :], self.identf[gs, gs])
                S.copy("act", self.ftmp[9][0:64, h * 64:(h + 1) * 64], p[0:64, 0:64])
            S.dma_out("sp", self.O["p_ssd"][l, 0].rearrange("h p n -> p h n"), self.ftmp[9][0:64, 0:256].rr("p (h n) -> p h n", h=4))
            self.out_tiles.append(self.ftmp[9].t)
        yg = [self.ftmp[6], self.ftmp[7]]
        for fc in range(2):
            yield
            pg = self.ps()
            self.zmm(blk, C_BG + fc * 128, 128, pg[:, 0:N])
            sg = self.ftmp[8]
            S.act(sg[:, 0:N], pg[:, 0:N], AF.Silu)
            S.tt("dve", yg[fc][:, 0:N], yb[fc][:, 0:N], sg[:, 0:N], ALU.mult)
            sq = self.btmp[4 + fc]
            S.act(sq[:, 0:N], yg[fc][:, 0:N], AF.Square)
        yield
        pss = self.ps()
        for fc in range(2):
            S.mm(pss[:, 0:N], self.onesb.v(), self.btmp[4 + fc][:, 0:N], start=(fc == 0), stop=(fc == 1))
        rstd = self.ftmp[9]
        self.rsqrt(rstd[:, 0:N], pss[:, 0:N], 1.0 / 256.0, self.eps6())
        for fc in range(2):
            S.stt(self.mixT[2 + fc][:, 0:N], yg[fc][:, 0:N], self.ssnw[:, fc:fc + 1], rstd[:, 0:N], ALU.mult, ALU.mult)

    def rsqrt(self, out, in_, scale, eps):
        self.S.act(out, in_, AF.Ln, bias=eps, scale=scale)
        self.S.act(out, out, AF.Exp, scale=-0.5)

    def eps6(self):
        if not hasattr(self, "_eps6"):
            self._eps6 = self.sb("eps6", [128, 1])
            self.S.memset("pool", self._eps6.v(), 1e-6)
        return self._eps6.v()

    def _load_C(self, l):
        S, I = self.S, self.I
        S.dma_in("sp", self.gdp[:, 0:4], I["gdn_dt_bias"][l:l + 1, :].to_broadcast([128, 4]))
        S.dma_in("sp", self.gdp[:, 4:8], I["gdn_a_log"][l:l + 1, :].to_broadcast([128, 4]))
        S.act(self.gdp[:, 8:12], self.gdp[:, 4:8], AF.Exp)
        S.ts("dve", self.gdp[:, 8:12], self.gdp[:, 8:12], -1.0, ALU.mult)
        for h2 in range(2):
            S.dma_in("sp", self.gdnw[64 * h2:64 * h2 + 64, :], I["gdn_norm_w"][l].rearrange("(p o) -> p o", o=1))

    def _load_C_sample(self, l):
        S, I = self.S, self.I
        for fc in range(2):
            for hh in range(2):
                S.dma_in("sp", self.hnat(hh), I["state_gdn"][l][:, 2 * fc + hh].rearrange("b k v -> k b v"))
            for hh in range(2):
                S.copy("act" if hh else "dve", self.H0T[64 * hh:64 * hh + 64, :, fc, :], self.hnat(hh))

    def _mixer_C(self, l, blk):
        S = self.S
        N = blk.N
        nt = blk.ntile
        mk = "s" if blk.sample else "p"
        nlev = 3 if blk.sample else 7
        qkv = self.cq
        for i in range(4):
            sq = self.gsq
            S.act(sq[:, 0:N], qkv[i][:, 0:N], AF.Square)
            yield
            pss = self.ps()
            S.mm(pss[:, 0:N], self.blockones.v(), sq[:, 0:N])
            rn = self.cf[2]
            self.rsqrt(rn[:, 0:N], pss[:, 0:N], 1.0, self.eps6())
            if i < 2:
                S.stt(qkv[i][:, 0:N], qkv[i][:, 0:N], 0.125, rn[:, 0:N], ALU.mult, ALU.mult)
            else:
                S.tt("dve", qkv[i][:, 0:N], qkv[i][:, 0:N], rn[:, 0:N], ALU.mult)
        bet, gl = self.csm[:, 0:16], self.csm[:, 16:32]
        bet3 = bet[:, 0:nt * 4].rr("p (t h) -> p t h", h=4)
        g3 = gl[:, 0:nt * 4].rr("p (t h) -> p t h", h=4)
        for ti in range(nt):
            S.act(bet3[:, ti, :], self.smtok[:, ti, 4:8], AF.Sigmoid)
            S.tt("dve", g3[:, ti, :], self.smtok[:, ti, 8:12], self.gdp[:, 0:4], ALU.add)
        S.act(gl[:, 0:nt * 4], gl[:, 0:nt * 4], AF.Exp)
        S.act(gl[:, 0:nt * 4], gl[:, 0:nt * 4], AF.Ln, bias=1.0)
        for ti in range(nt):
            S.tt("dve", g3[:, ti, :], g3[:, ti, :], self.gdp[:, 8:12], ALU.mult)
        oT = [self.cf[0], self.cf[1]]
        if blk.first:
            S.memset("pool", self.Sm.v(), 0.0)
            S.memset("pool", self.Sb.v(), 0.0)
        H4 = [(h, h // 2, h % 2, slice(64 * (h % 2), 64 * (h % 2) + 64), slice(h * 128, (h + 1) * 128)) for h in range(4)]
        for ti in range(nt):
            tsl = slice(ti * 128, (ti + 1) * 128)
            sc = self.csc[ti % 2]
            for i, srcb in enumerate((qkv[2], qkv[3], qkv[4], qkv[5])):
                S.tr(self.pst[:, i * 128:(i + 1) * 128], srcb[:, tsl], self.ident.v())
            S.copy("act", self.ctok[:, 0:512], self.pst[:, 0:512])
            yield
            pAC = self.ps()
            self.chunk_decay(blk, ti, g3[:, ti, :], sc, pAC, None, self.cdsplit, self.cdres)
            S.act(sc[:, 12:16], sc[:, 0:4], AF.Exp)
            S.tt("dve", sc[:, 16:20], sc[:, 12:16], bet3[:, ti, :], ALU.mult)
            S.tt("dve", sc[:, 20:24], sc[:, 4:8], sc[:, 0:4], ALU.subtract)
            S.act(sc[:, 20:24], sc[:, 20:24], AF.Exp)
            S.act(sc[:, 24:28], sc[:, 4:8], AF.Exp)
            S.ts("dve", sc[:, 28:32], bet3[:, ti, :], -1.0, ALU.mult)
            Edec, GT, gam = self.wtmp[0], self.wtmp[1], self.wtmp[2]
            S.act(Edec.v(), pAC.v(), AF.Exp)
            for (h, fc, h2, hb, hs) in H4:
                S.stt(GT[:, hs], pAC[:, hs], sc[:, 8 + h:9 + h], self.NEGM[mk].v(), ALU.add, ALU.add)
                S.stt(gam[:, hs], pAC[:, hs], sc[:, h:h + 1], self.POSM[mk].v(), ALU.subtract, ALU.add)
            S.act(GT.v(), GT.v(), AF.Exp)
            S.act(gam.v(), gam.v(), AF.Exp, scale=-1.0)
            yield
            pKK = [self.ps(), self.ps()]
            for (h, fc, h2, hb, hs) in H4:
                cs = slice(fc * 128, (fc + 1) * 128)
                S.mm(pKK[h2][:, cs], qkv[2 + fc][hb, tsl], qkv[2 + fc][hb, tsl])
            M0 = self.gM[2]
            for (h, fc, h2, hb, hs) in H4:
                cs = slice(fc * 128, (fc + 1) * 128)
                S.stt(M0[:, h, :], gam[:, hs], sc[:, 28 + h:29 + h], pKK[h2][:, cs], ALU.mult, ALU.mult)
            yield
            pQK = [self.ps(), self.ps()]
            for (h, fc, h2, hb, hs) in H4:
                cs = slice(fc * 128, (fc + 1) * 128)
                S.mm(pQK[h2][:, cs], qkv[2 + fc][hb, tsl], qkv[fc][hb, tsl])
            for (h, fc, h2, hb, hs) in H4:
                cs = slice(fc * 128, (fc + 1) * 128)
                S.tt("dve", self.QKG[:, h, :], GT[:, hs], pQK[h2][:, cs], ALU.mult)
            Md, Nd, Z = self.gM[0], self.gN[0], self.gX[0]
            d32 = self.D32b.v().rr("p (o t) -> p o t", o=1).bc([128, 4, 128])
            idb = self.ident.v().rr("p (o t) -> p o t", o=1).bc([128, 4, 128])
            S.tt("dve", Md.v(), M0.v(), d32, ALU.mult)
            S.tt("dve", self.Moff.v(), M0.v(), Md.v(), ALU.subtract)
            for (h, fc, h2, hb, hs) in H4:
                S.tr(self.pst[:, hs], Md[:, h, :], self.ident.v())
            S.copy("act", Nd.v().rr("p h t -> p (h t)"), self.pst[:, 0:512])
            S.tt("dve", Z.v(), self.pst[:, 0:512].rr("p (h t) -> p h t", h=4), idb, ALU.add)
            R = self.gR
            for (h, fc, h2, hb, hs) in H4:
                kc = slice(64 * h2, 64 * h2 + 64)
                vc = slice(64 * (1 - h2), 64 * (1 - h2) + 64)
                S.ts("dve", R[:, h, kc], self.ctok[:, h * 64:(h + 1) * 64], sc[:, 16 + h:17 + h], ALU.mult)
                S.ts("dve", R[:, h, vc], self.ctok[:, 256 + h * 64:256 + (h + 1) * 64], bet3[:, ti, h:h + 1], ALU.mult)
            nsq = 2 if blk.sample else 4
            for j in range(nsq):
                Mn, Nn2, Zn = self.gM[(j + 1) % 2], self.gN[(j + 1) % 2], self.gX[(j + 1) % 2]
                yield
                pM = self.ps()
                for (h, fc, h2, hb, hs) in H4:
                    S.mm(pM[:, hs], Nd[:, h, :], Md[:, h, :])
                S.copy("act", Mn.v().rr("p h t -> p (h t)"), pM.v())
                if j < nsq - 1:
                    yield
                    pN = self.ps()
                    for (h, fc, h2, hb, hs) in H4:
                        S.mm(pN[:, hs], Md[:, h, :], Nd[:, h, :])
                    S.copy("act", Nn2.v().rr("p h t -> p (h t)"), pN.v())
                yield
                pZ = self.ps()
                for (h, fc, h2, hb, hs) in H4:
                    S.mm(pZ[:, hs], Mn[:, h, :], Z[:, h, :])
                S.tt("dve", Zn.v().rr("p h t -> p (h t)"), pZ.v(), Z.v().rr("p h t -> p (h t)"), ALU.add)
                Md, Nd, Z = Mn, Nn2, Zn
            TdT = Z
            pXT, pV = None, None
            if blk.sample:
                yield
                pXT, pV = self.ps(), self.ps()
                for (h, fc, h2, hb, hs) in H4:
                    vc = slice(64 * (1 - h2), 64 * (1 - h2) + 64)
                    S.mm(pXT[:, hs], R[:, h, :], TdT[:, h, :])
                    S.mm(pV[:, h * 64:(h + 1) * 64], TdT[:, h, :], R[:, h, vc])
            else:
                yield
                pG = self.ps()
                for (h, fc, h2, hb, hs) in H4:
                    S.mm(pG[:, hs], self.Moff[:, h, :], TdT[:, h, :])
                S.copy("act", self.nGT.v().rr("p h t -> p (h t)"), pG.v())
                W = gC = self.gM[2]
                yield
                pW0 = self.ps()
                for (h, fc, h2, hb, hs) in H4:
                    S.mm(pW0[:, hs], TdT[:, h, :], R[:, h, :])
                S.copy("act", gC.v().rr("p h t -> p (h t)"), pW0.v())
                for it in range(2):
                    Wn = self.gW[it]
                    yield
                    pWi = self.ps()
                    for (h, fc, h2, hb, hs) in H4:
                        S.mm(pWi[:, hs], self.nGT[:, h, :], W[:, h, :])
                    S.tt("dve", Wn.v().rr("p h t -> p (h t)"), pWi.v(), gC.v().rr("p h t -> p (h t)"), ALU.add)
                    W = Wn
                yield
                pXT, pV = self.ps(), self.ps()
                for (h, fc, h2, hb, hs) in H4:
                    vc = slice(64 * (1 - h2), 64 * (1 - h2) + 64)
                    S.mm(pXT[:, hs], R[:, h, :], TdT[:, h, :], start=True, stop=False)
                    S.mm(pXT[:, hs], W[:, h, :], self.nGT[:, h, :], start=False, stop=True)
                    S.mm(pV[:, h * 64:(h + 1) * 64], TdT[:, h, :], R[:, h, vc], start=True, stop=False)
                    S.mm(pV[:, h * 64:(h + 1) * 64], self.nGT[:, h, :], W[:, h, vc], start=False, stop=True)
            S.copy("act", self.XTb.v().rr("p h t -> p (h t)"), pXT.v())
            S.copy("act", self.Val.v(), pV[:, 0:256])
            for (h, fc, h2, hb, hs) in H4:
                S.tt("dve", self.qdec[hb, h, :], qkv[fc][hb, tsl], Edec[hb, hs], ALU.mult)
                S.ts("dve", self.kdec[:, h, :], self.ctok[:, h * 64:(h + 1) * 64], sc[:, 20 + h:21 + h], ALU.mult)
            if not blk.sample:
                nostate = blk.first and ti == 0
                yield
                pW = [self.ps(), self.ps()]
                for (h, fc, h2, hb, hs) in H4:
                    if nostate:
                        S.copy("dve", self.wtok[:, h, :], self.Val[:, h * 64:(h + 1) * 64])
                    else:
                        S.mm(pW[h2][:, fc * 64:(fc + 1) * 64], self.XTb[hb, h, :], self.Sb[hb, fc, :])
                if not nostate:
                    for (h, fc, h2, hb, hs) in H4:
                        S.tt("dve", self.wtok[:, h, :], self.Val[:, h * 64:(h + 1) * 64], pW[h2][:, fc * 64:(fc + 1) * 64], ALU.subtract)
                yield
                po = [self.ps(), self.ps()]
                for (h, fc, h2, hb, hs) in H4:
                    out = po[h2][hb, fc * 128:(fc + 1) * 128]
                    S.mm(out, self.wtok[:, h, :], self.QKG[:, h, :], start=True, stop=nostate)
                    if not nostate:
                        S.mm(out, self.Sb[hb, fc, :], self.qdec[hb, h, :], start=False, stop=True)
                for fc in range(2):
                    S.copy("act", oT[fc][0:64, tsl], po[0][0:64, fc * 128:(fc + 1) * 128])
                    S.copy("act", oT[fc][64:128, tsl], po[1][64:128, fc * 128:(fc + 1) * 128])
                yield
                pS = self.ps()
                for (h, fc, h2, hb, hs) in H4:
                    S.mm(pS[hb, fc * 64:(fc + 1) * 64], self.kdec[:, h, :], self.wtok[:, h, :])
                for (h, fc, h2, hb, hs) in H4:
                    S.stt(self.Sm[hb, fc, :], self.Sm[hb, fc, :], sc[hb, 24 + h:25 + h], pS[hb, fc * 64:(fc + 1) * 64], ALU.mult, ALU.add)
                S.copy("dve", self.Sb.v(), self.Sm.v())
            else:
                self._load_C_sample(l)
                yield
                pWT = [self.ps(), self.ps()]
                for (h, fc, h2, hb, hs) in H4:
                    ob = slice(64 * (1 - h2), 64 * (1 - h2) + 64)
                    for b in range(16):
                        S.mm(pWT[h2][ob, fc * 128 + 8 * b:fc * 128 + 8 * b + 8], self.H0T[hb, b, fc, :], self.XTb[hb, h, 8 * b:8 * b + 8])
                for (h, fc, h2, hb, hs) in H4:
                    ob = slice(64 * (1 - h2), 64 * (1 - h2) + 64)
                    S.tt("dve", self.gW[1][ob, h, :], self.XTb[ob, h, :], pWT[h2][ob, fc * 128:(fc + 1) * 128], ALU.subtract)
                for par in range(2):
                    ob = slice(64 * (1 - par), 64 * (1 - par) + 64)
                    for h in (par, par + 2):
                        S.tr(self.pst[:, h * 64:(h + 1) * 64], self.gW[1][ob, h, :], self.ident[ob, ob])
                    for h in (par, par + 2):
                        S.copy("act", self.wtok[:, h, :], self.pst[:, h * 64:(h + 1) * 64])
                yield
                po = [self.ps(), self.ps()]
                for (h, fc, h2, hb, hs) in H4:
                    S.mm(po[h2][hb, fc * 128:(fc + 1) * 128], self.wtok[:, h, :], self.QKG[:, h, :], start=True, stop=False)
                    for b in range(16):
                        S.mm(po[h2][hb, fc * 128 + 8 * b:fc * 128 + 8 * b + 8], self.H0T[hb, b, fc, :], self.qdec[hb, h, 8 * b:8 * b + 8],
                             start=False, stop=(b == 15))
                for fc in range(2):
                    S.copy("act", oT[fc][0:64, tsl], po[0][0:64, fc * 128:(fc + 1) * 128])
                    S.copy("act", oT[fc][64:128, tsl], po[1][64:128, fc * 128:(fc + 1) * 128])
                yield
                pE = self.ps()
                for h in range(4):
                    for i in range(3):
                        S.mm(pE[0:64, h * 16:(h + 1) * 16], self.cdsplit[:, i, h:h + 1].bc([128, 64]), self.SeqSelb.v(),
                             start=(i == 0), stop=(i == 2))
                S.act(self.etotb.v().rr("p h b -> p (h b)"), pE[0:64, 0:64], AF.Exp)
                for h in range(4):
                    if h % 2 == 0:
                        for hh in range(2):
                            S.dma_in("sp", self.hnat(hh), self.I["state_gdn"][l][:, h + hh].rearrange("b k v -> k b v"))
                    for hf in range(2):
                        S.tt("dve", self.bexp(hf), self.wtok[:, h, :].rr("p (o n) -> p o n", o=1).bc([128, 8, 64]),
                             self.SeqSelb[:, hf * 8:(hf + 1) * 8].rr("p (b o) -> p b o", o=1).bc([128, 8, 64]), ALU.mult)
                    yield
                    pn = [self.ps(), self.ps()]
                    for hf in range(2):
                        S.mm(pn[hf][0:64, :], self.kdec[:, h, :], self.bexp(hf).rr("p b n -> p (b n)"))
                    for hf in range(2):
                        bs = slice(hf * 8, hf * 8 + 8)
                        S.tt("dve", self.hnat(h % 2)[:, bs, :], self.hnat(h % 2)[:, bs, :],
                             self.etotb[:, h, bs].rr("p (b o) -> p b o", o=1).bc([64, 8, 64]), ALU.mult)
                        S.tt("dve", self.hnat(h % 2)[:, bs, :], self.hnat(h % 2)[:, bs, :], pn[hf][0:64, :].rr("p (b n) -> p b n", n=64), ALU.add)
                    if h % 2 == 1:
                        for hh in range(2):
                            S.dma_out("sp", self.O["s_gdn"][l][:, h - 1 + hh].rearrange("b k v -> k b v"), self.hnat(hh))
                self.out_tiles.extend([self.xrt[1].t, self.ybuf[0].t])
        if blk.last:
            for (h, fc, h2, hb, hs) in H4:
                S.dma_out("sp", self.O["p_gdn"][l, 0, h], self.Sm[hb, fc, :])
            self.out_tiles.append(self.Sm.t)
        for fc in range(2):
            sq = self.gsq
            S.act(sq[:, 0:N], oT[fc][:, 0:N], AF.Square)
            yield
            pss = self.ps()
            S.mm(pss[:, 0:N], self.blockones.v(), sq[:, 0:N])
            rs = self.cf[2]
            self.rsqrt(rs[:, 0:N], pss[:, 0:N], 1.0 / 64.0, self.eps6())
            yield
            pg = self.ps()
            self.zmm(blk, C_CG + fc * 128, 128, pg[:, 0:N])
            sg = self.cf[3]
            S.act(sg[:, 0:N], pg[:, 0:N], AF.Silu)
            S.stt(oT[fc][:, 0:N], oT[fc][:, 0:N], self.gdnw[:, 0:1], rs[:, 0:N], ALU.mult, ALU.mult)
            S.tt("dve", self.mixT[4 + fc][:, 0:N], oT[fc][:, 0:N], sg[:, 0:N], ALU.mult)

    def sin_rr(self, dst, x, scr, iscr):
        S = self.S
        PI = math.pi
        self.reduce_pi(dst, x, scr, iscr)
        S.act(dst, dst, AF.Sin)

    def reduce_pi(self, dst, x, scr, iscr):
        S = self.S
        PI = math.pi
        S.ts("dve", scr, x, 1.0 / (2 * PI), ALU.mult)
        S.copy("dve", iscr, scr)
        S.copy("dve", scr, iscr)
        S.stt(dst, scr, -2 * PI, x, ALU.mult, ALU.add)
        S.ts("dve", scr, dst, PI, ALU.is_gt)
        S.stt(dst, scr, -2 * PI, dst, ALU.mult, ALU.add)
        S.ts("dve", scr, dst, -PI, ALU.is_lt)
        S.stt(dst, scr, 2 * PI, dst, ALU.mult, ALU.add)
        S.ts("dve", dst, dst, PI, ALU.min, -PI, ALU.max)

    def _load_D(self, l):
        S, I, nc = self.S, self.I, self.nc
        PI = math.pi
        par, sm = self.s5par, self.s5sm
        lam = lambda nm: I[nm][l].rearrange("(k g2) n -> (g2 n) k", g2=2)
        S.dma_in("sp", par[:, :, 0], lam("s5_lambda_re"), allow_slow_non_contiguous=True)
        S.dma_in("sp", par[:, :, 1], lam("s5_lambda_im"), allow_slow_non_contiguous=True)
        ldt = I["s5_log_dt"][l].rearrange("(k g2) -> g2 k", g2=2)
        for g2 in range(2):
            S.dma_in("sp", par[64 * g2:64 * g2 + 64, :, 2], ldt[g2:g2 + 1, :].to_broadcast([64, 8]))
        S.act(par[:, :, 2], par[:, :, 2], AF.Exp)
        S.tt("dve", par[:, :, 3], par[:, :, 0], par[:, :, 2], ALU.mult)
        S.tt("dve", par[:, :, 4], par[:, :, 1], par[:, :, 2], ALU.mult)
        S.act(par[:, :, 5], par[:, :, 3], AF.Exp, scale=8.0)
        w0, w1, w2 = self.wtmp
        v3 = lambda v: v.rr("p (k q) -> p k q", q=32)
        A1, A2 = w0[:, 0:256], w0[:, 256:512]
        MG, PWc = w1[:, 0:256], w1[:, 256:512]
        PWs, SC = w2[:, 0:256], w2[:, 256:512]
        col = lambda j: par[:, :, j].rr("p (k o) -> p k o", o=1).bc([128, 8, 32])
        qrow = self.s5q[:, 0:32].rr("p (o q) -> p o q", o=1).bc([128, 8, 32])
        S.tt("dve", v3(A1), col(4), qrow, ALU.mult)
        S.tt("dve", v3(MG), col(3), qrow, ALU.mult)
        S.act(MG, MG, AF.Exp)
        self.sin_rr(PWs, A1, SC, self.itmp.v())
        S.ts("dve", A2, A1, PI / 2, ALU.add)
        self.sin_rr(PWc, A2, SC, self.itmp.v())
        S.tt("dve", PWc, PWc, MG, ALU.mult)
        S.tt("dve", PWs, PWs, MG, ALU.mult)
        PWre, PWim = v3(PWc), v3(PWs)
        S.copy("dve", par[:, :, 6], PWre[:, :, 15])
        S.copy("dve", par[:, :, 7], PWim[:, :, 15])
        lr, li = par[:, :, 0], par[:, :, 1]
        t0, t1, t2, t3, fre, fim = (sm[:, :, j] for j in range(6))
        abim = PWim[:, :, 8]
        S.ts("dve", t0, PWre[:, :, 8], -1.0, ALU.add)
        S.tt("dve", t1, lr, lr, ALU.mult)
        S.tt("dve", t2, li, li, ALU.mult)
        S.tt("dve", t1, t1, t2, ALU.add)
        S.recip(t1, t1)
        S.tt("dve", t2, t0, lr, ALU.mult)
        S.tt("dve", t3, abim, li, ALU.mult)
        S.tt("dve", t2, t2, t3, ALU.add)
        S.tt("dve", fre, t2, t1, ALU.mult)
        S.tt("dve", t2, abim, lr, ALU.mult)
        S.tt("dve", t3, t0, li, ALU.mult)
        S.tt("dve", t2, t2, t3, ALU.subtract)
        S.tt("dve", fim, t2, t1, ALU.mult)
        f0, f1, f2, f3 = (self.ftmp[j][:, 0:128].rr("p (k i) -> p k i", i=16) for j in range(4))
        bsrc = lambda nm: I[nm][l].rearrange("(k g2) n i -> (g2 n) k i", g2=2)
        S.dma_in("sp", f0, bsrc("s5_b_re"))
        S.dma_in("sp", f1, bsrc("s5_b_im"))
        fb = lambda f: f.rr("p (k o) -> p k o", o=1).bc([128, 8, 16])
        s5bb = self.cf[2].v().rr("p (a k i) -> p a k i", a=2, k=8)
        s5c = self.cf[3].v().rr("p (a k i) -> p a k i", a=2, k=8)
        s5cb = [self.cq[0][:, 0:128], self.cq[0][:, 128:256], self.cq[1][:, 0:128], self.cq[1][:, 128:256]]
        bbre, bbim = s5bb[:, 0], s5bb[:, 1]
        S.tt("dve", f2, f0, fb(fre), ALU.mult)
        S.tt("dve", f3, f1, fb(fim), ALU.mult)
        S.tt("dve", bbre, f2, f3, ALU.subtract)
        S.tt("dve", f2, f1, fb(fre), ALU.mult)
        S.tt("dve", f3, f0, fb(fim), ALU.mult)
        S.tt("dve", bbim, f2, f3, ALU.add)
        for part, nm in enumerate(("s5_c_re", "s5_c_im")):
            cn = self.ftmp[4][:, 0:128].rr("p (c n) -> p c n", c=2)
            S.dma_in("sp", cn, I[nm][l].rearrange("(c gl) i n -> (gl i) c n", c=2))
            p = self.ps()
            for c in range(2):
                S.tr(p[0:64, c * 128:(c + 1) * 128], cn[:, c, :], self.identf.v())
            pv = p[0:64, 0:256].rr("n (k g2 i) -> n k g2 i", g2=2, i=16)
            for g2 in range(2):
                S.copy("act" if g2 else "dve", s5c[64 * g2:64 * g2 + 64, part, :, :], pv[:, :, g2, :])
        cre, cim = s5c[:, 0], s5c[:, 1]
        dsrc = I["s5_d"][l].rearrange("(g i) -> i g", i=16)
        for sg in range(8):
            S.dma_in("sp", self.s5Dfold[16 * sg:16 * sg + 16, :], dsrc, allow_slow_non_contiguous=True)
        T = [self.ftmp[j][:, 0:128].rr("p (s i) -> p s i", i=16) for j in range(5, 9)]
        for k in range(8):
            def cplx(outre, outim, pre, pim, xre, xim, neg_im=False, pw_outer=True):
                a = lambda v: v.rr("p (s o) -> p s o", o=1).bc([128, 8, 16])
                b = lambda v: v.rr("p (o i) -> p o i", o=1).bc([128, 8, 16])
                S.tt("dve", T[0], a(pre), b(xre), ALU.mult)
                S.tt("dve", T[1], a(pim), b(xim), ALU.mult)
                S.tt("dve", outre, T[0], T[1], ALU.subtract)
                S.tt("dve", T[2], a(pre), b(xim), ALU.mult)
                S.tt("dve", T[3], a(pim), b(xre), ALU.mult)
                if neg_im:
                    S.stt(outim, T[2], -1.0, T[3], ALU.mult, ALU.subtract)
                else:
                    S.tt("dve", outim, T[2], T[3], ALU.add)
            cb3 = [c.rr("p (s i) -> p s i", i=16) for c in s5cb]
            cplx(cb3[0], cb3[1], PWre[:, k, 16:24], PWim[:, k, 16:24], bbre[:, k, :], bbim[:, k, :], neg_im=True)
            cplx(cb3[2], cb3[3], PWre[:, k, 24:32], PWim[:, k, 24:32], cre[:, k, :], cim[:, k, :])
            pT = [self.ps(), self.ps()]
            for g2 in range(2):
                hb = slice(64 * g2, 64 * g2 + 64)
                S.mm(pT[g2][:, 0:128], s5cb[0][hb, :], s5cb[2][hb, :], start=True, stop=False)
                S.mm(pT[g2][:, 0:128], s5cb[1][hb, :], s5cb[3][hb, :], start=False, stop=True)
            for g2 in range(2):
                g = 2 * k + g2
                tm = self.ftmp[9][:, 0:128]
                S.tt("dve", tm, pT[g2][:, 0:128], self.Cmask.v(), ALU.mult)
                S.stt(self.s5Toep[:, g, :], self.identf.v(), self.s5Dfold[:, g:g + 1], tm, ALU.mult, ALU.add)
            Rre = self.ftmp[9][:, 0:128].rr("p (s i) -> p s i", i=16)
            Rim = self.ftmp[9][:, 128:256].rr("p (s i) -> p s i", i=16)
            cplx(Rre, Rim, PWre[:, k, 0:8], PWim[:, k, 0:8], bbre[:, k, :], bbim[:, k, :])
            pR = self.ps()
            for part in range(2):
                S.tr(pR[:, part * 128:(part + 1) * 128], self.ftmp[9][:, part * 128:(part + 1) * 128], self.identf.v())
            for part in range(2):
                S.copy("act", self.s5RT[:, part, 2 * k:2 * k + 2, :], pR[:, part * 128:(part + 1) * 128].rr("p (g n) -> p g n", g=2))
            cplx(self.s5P[:, 0, k, :].rr("p (s i) -> p s i", i=16), self.s5P[:, 1, k, :].rr("p (s i) -> p s i", i=16),
                 PWre[:, k, 8:16], PWim[:, k, 8:16], cre[:, k, :], cim[:, k, :], neg_im=True)
        phi = sm[:, :, 6]
        S.ts("dve", t0, par[:, :, 4], 8.0, ALU.mult)
        self.reduce_pi(phi, t0, t1, self.itmp[:, 0:8])
        th = w0[:, 0:256].rr("p (k b) -> p k b", b=32)
        S.tt("dve", th, phi.rr("p (k o) -> p k o", o=1).bc([128, 8, 32]), self.s5q[:, 32:64].rr("p (o q) -> p o q", o=1).bc([128, 8, 32]),
             ALU.mult)
        self.sin_rr(self.s5E[:, 1, :, :].rr("p k b -> p (k b)"), w0[:, 0:256], SC, self.itmp.v())
        S.ts("dve", A2, w0[:, 0:256], PI / 2, ALU.add)
        self.sin_rr(self.s5E[:, 0, :, :].rr("p k b -> p (k b)"), A2, SC, self.itmp.v())
        S.dma_in("sp", w1.v().rr("p (c f) -> p c f", c=2), I["s5_glu_w"][l].rearrange("(c p) f -> p c f", p=128))
        S.copy("dve", self.s5glu.v().rr("p c f -> p (c f)"), w1.v())
        S.dma_in("sp", self.s5gb.v(), I["s5_glu_b"][l].rearrange("(c p) -> p c", p=128), allow_slow_non_contiguous=True)

    def _mixer_D(self, l, blk):
        S = self.S
        N = blk.N
        nb = N // 8
        for c, (c0, w) in enumerate(((0, 96), (96, 96), (192, 64))):
            yield
            p = self.ps()
            self.zmm(blk, C_DU + c0, w, p[0:w, 0:N])
            S.copy("act" if c % 2 else "dve", self.s5uT[c][0:w, 0:N], p[0:w, 0:N])
        for q in range(3):
            yield
            pFq = self.ps()
            ks = list(range(q, 8, 3))
            for si, k in enumerate(ks):
                uc = k // 3
                for g2 in range(2):
                    out = pFq[:, (si * 2 + g2) * nb:(si * 2 + g2 + 1) * nb]
                    for sg in range(8):
                        rhs = self.s5uT[uc][32 * q:32 * q + 32, 0:N].rr("p (b s) -> p b s", s=8)[:, :, sg]
                        S.mm(out, self.s5Sel[32 * q:32 * q + 32, g2 * 8 + sg, :], rhs, start=(sg == 0), stop=(sg == 7))
            for si, k in enumerate(ks):
                S.copy("act" if (q + si) % 2 else "dve", self.s5Uf[:, 2 * k:2 * k + 2, 0:nb],
                       pFq[:, si * 2 * nb:(si + 1) * 2 * nb].rr("p (g b) -> p g b", g=2))
        if blk.sample:
            stg = self.ybuf[1]
            s5H0 = self.df[4].v().rr("p (a k b) -> p a k b", a=2, k=8)
            for part, nm in ((0, "state_s5_re"), (1, "state_s5_im")):
                S.dma_in("sp", stg[0:16, :], self.I[nm][l].rearrange("b g n -> b (g n)"))
                yield
                p = self.ps()
                for k in range(8):
                    S.tr(p[:, k * 16:(k + 1) * 16], stg[0:16, k * 128:(k + 1) * 128], self.identf[0:16, 0:16])
                S.copy("dve", s5H0[:, part, :, :], p[:, 0:128].rr("p (k b) -> p k b", b=16))
                S.copy("act", self.s5Hst[:, part, :, 0:16], p[:, 0:128].rr("p (k b) -> p k b", b=16))
        yield
        pXr, pXi = self.ps(), self.ps()
        for g in range(16):
            k, g2 = g // 2, g % 2
            hb = slice(64 * g2, 64 * g2 + 64)
            S.mm(pXr[hb, k * nb:(k + 1) * nb], self.s5RT[:, 0, g, :], self.s5Uf[:, g, 0:nb])
            S.mm(pXi[hb, k * nb:(k + 1) * nb], self.s5RT[:, 1, g, :], self.s5Uf[:, g, 0:nb])
        Xr3 = pXr[:, 0:8 * nb].rr("p (k b) -> p k b", b=nb)
        Xi3 = pXi[:, 0:8 * nb].rr("p (k b) -> p k b", b=nb)
        F = [self.df[j][:, 0:8 * nb].rr("p (k b) -> p k b", b=nb) for j in range(6)]
        F = F + [F[2], F[3]]
        par = self.s5par
        if not blk.sample:
            Ec, Es = self.s5E[:, 0, :, 0:nb], self.s5E[:, 1, :, 0:nb]
            S.tt("dve", F[0], Xr3, Ec, ALU.mult)
            S.tt("dve", F[1], Xi3, Es, ALU.mult)
            S.tt("dve", F[2], F[0], F[1], ALU.add)
            S.tt("dve", F[0], Xi3, Ec, ALU.mult)
            S.tt("dve", F[1], Xr3, Es, ALU.mult)
            S.tt("dve", F[3], F[0], F[1], ALU.subtract)
            for k in range(8):
                rho = par[:, k, 5:6].bc([128, nb])
                for part, Z in ((0, F[2]), (1, F[3])):
                    init = 0.0 if blk.first else self.s5H[:, part, k:k + 1]
                    S.scan(F[4 + part][:, k, :], rho, Z[:, k, :], init)
            Wr, Wi = F[4], F[5]
            S.tt("dve", F[0], Wr, Ec, ALU.mult)
            S.tt("dve", F[1], Wi, Es, ALU.mult)
            S.tt("dve", F[6], F[0], F[1], ALU.subtract)
            S.tt("dve", F[0], Wi, Ec, ALU.mult)
            S.tt("dve", F[1], Wr, Es, ALU.mult)
            S.tt("dve", F[7], F[0], F[1], ALU.add)
            for part, Hn in ((0, F[6]), (1, F[7])):
                if blk.first:
                    S.memset("pool", self.s5Hst[:, part, :, 0:1], 0.0)
                else:
                    S.copy("dve", self.s5Hst[:, part, :, 0], self.s5H[:, part, :])
                S.copy("act", self.s5Hst[:, part, :, 1:nb], Hn[:, :, 0:nb - 1])
                S.copy("dve", self.s5H[:, part, :], Hn[:, :, nb - 1])
            if blk.last:
                for part, nm in ((0, "p_s5_re"), (1, "p_s5_im")):
                    S.dma_out("sp", self.O[nm][l, 0].rearrange("(k g2) n -> (g2 n) k", g2=2), self.s5H[:, part, :], allow_slow_non_contiguous=True)
                self.out_tiles.append(self.s5H.t)
        else:
            stg = self.ybuf[1]
            s5H0 = self.df[4].v().rr("p (a k b) -> p a k b", a=2, k=8)
            a8r = par[:, :, 6].rr("p (k o) -> p k o", o=1).bc([128, 8, 16])
            a8i = par[:, :, 7].rr("p (k o) -> p k o", o=1).bc([128, 8, 16])
            h0r, h0i = s5H0[:, 0], s5H0[:, 1]
            S.tt("dve", F[0], h0r, a8r, ALU.mult)
            S.tt("dve", F[1], h0i, a8i, ALU.mult)
            S.tt("dve", F[0], F[0], F[1], ALU.subtract)
            S.tt("dve", F[2], F[0], Xr3, ALU.add)
            S.tt("dve", F[0], h0i, a8r, ALU.mult)
            S.tt("dve", F[1], h0r, a8i, ALU.mult)
            S.tt("dve", F[0], F[0], F[1], ALU.add)
            S.tt("dve", F[3], F[0], Xi3, ALU.add)
            for part, nm, Hn in ((0, "s_s5_re", F[2]), (1, "s_s5_im", F[3])):
                yield
                pp = [self.ps(), self.ps()]
                for k in range(8):
                    S.tr(pp[k // 4][0:16, (k % 4) * 128:(k % 4 + 1) * 128], Hn[:, k, :], self.identf.v())
                for hf in range(2):
                    S.copy("act" if hf else "dve", stg[0:16, hf * 512:(hf + 1) * 512], pp[hf][0:16, :])
                S.dma_out("sp", self.O[nm][l].rearrange("b g n -> b (g n)"), stg[0:16, :])
            self.out_tiles.append(stg.ts[0] if isinstance(stg, View) else stg.t)
        yield
        pY = [self.ps(), self.ps()]
        for g in range(16):
            k, g2 = g // 2, g % 2
            hb = slice(64 * g2, 64 * g2 + 64)
            out = pY[g2][:, k * nb:(k + 1) * nb]
            S.mm(out, self.s5Toep[:, g, :], self.s5Uf[:, g, 0:nb], start=True, stop=False)
            S.mm(out, self.s5P[hb, 0, k, :], self.s5Hst[hb, 0, k, 0:nb], start=False, stop=False)
            S.mm(out, self.s5P[hb, 1, k, :], self.s5Hst[hb, 1, k, 0:nb], start=False, stop=True)
        Yf4 = self.s5Yf[:, :, 0:nb].rr("p (k g2) b -> p k g2 b", g2=2)
        for g2 in range(2):
            S.copy("act" if g2 else "dve", Yf4[:, :, g2, :], pY[g2][:, 0:8 * nb].rr("p (k b) -> p k b", b=nb))
        y = [self.df[0], self.df[1]]
        yb = self.dbf
        for c in range(2):
            yield
            pU = self.ps()
            for j in range(4):
                k = 4 * c + j
                tp = (0, 96) if j == 3 else None
                for tau in range(8):
                    out = pU[32 * j:32 * j + 32, 0:N].rr("p (b s) -> p b s", s=8)[:, :, tau]
                    S.mm(out, self.s5SelU[:, tau, :], self.s5Yf[:, 2 * k, 0:nb], start=True, stop=False, tile_position=tp)
                    S.mm(out, self.s5SelU[:, 8 + tau, :], self.s5Yf[:, 2 * k + 1, 0:nb], start=False, stop=True, tile_position=tp)
            S.act(y[c][:, 0:N], pU[:, 0:N], AF.Gelu)
            S.copy("dve", yb[c][:, 0:N], y[c][:, 0:N])
        for fo in range(2):
            yield
            pg = self.ps()
            for ec in range(2):
                S.mm(pg[:, 0:N], self.s5glu[:, ec, fo * 128:(fo + 1) * 128], yb[ec][:, 0:N], start=(ec == 0), stop=(ec == 1))
            sig = self.df[4]
            S.act(sig[:, 0:N], pg[:, 0:N], AF.Sigmoid, bias=self.s5gb[:, fo:fo + 1], scale=1.0)
            yield
            pz = self.ps()
            self.zmm(blk, C_DG + fo * 128, 128, pz[:, 0:N])
            sg = self.df[5]
            S.act(sg[:, 0:N], pz[:, 0:N], AF.Silu)
            S.tt("dve", sig[:, 0:N], sig[:, 0:N], y[fo][:, 0:N], ALU.mult)
            S.tt("dve", self.mixT[6 + fo][:, 0:N], sig[:, 0:N], sg[:, 0:N], ALU.mult)


WSHAPES = {
    "w_in": [DEPTH, D, INC], "w_out": [DEPTH, D, D], "ln_g": [DEPTH, D], "ln_b": [DEPTH, D],
    "rg_conv_w": [DEPTH, 4, 256], "rg_conv_b": [DEPTH, 256], "rg_gate_a_w": [DEPTH, 4, 64, 64], "rg_gate_a_b": [DEPTH, 256],
    "rg_gate_x_w": [DEPTH, 4, 64, 64], "rg_gate_x_b": [DEPTH, 256], "rg_lambda": [DEPTH, 256],
    "ssd_conv_w": [DEPTH, 4, 512], "ssd_conv_b": [DEPTH, 512], "ssd_dt_bias": [DEPTH, 4], "ssd_a_log": [DEPTH, 4],
    "ssd_d": [DEPTH, 4], "ssd_norm_w": [DEPTH, 256],
    "gdn_conv_w": [DEPTH, 4, 768], "gdn_conv_b": [DEPTH, 768], "gdn_dt_bias": [DEPTH, 4], "gdn_a_log": [DEPTH, 4],
    "gdn_norm_w": [DEPTH, 64],
    "s5_lambda_re": [DEPTH, 16, 64], "s5_lambda_im": [DEPTH, 16, 64], "s5_log_dt": [DEPTH, 16],
    "s5_b_re": [DEPTH, 16, 64, 16], "s5_b_im": [DEPTH, 16, 64, 16], "s5_c_re": [DEPTH, 16, 16, 64], "s5_c_im": [DEPTH, 16, 16, 64],
    "s5_d": [DEPTH, 256], "s5_glu_w": [DEPTH, 256, 256], "s5_glu_b": [DEPTH, 256],
}

STATE_NAMES = ["cache_rglru_conv", "state_rglru", "cache_ssd_conv", "state_ssd", "cache_gdn_conv", "state_gdn",
               "state_s5_re", "state_s5_im"]
OUT_STATE = ["_rg_conv", "_rg", "_ssd_conv", "_ssd", "_gdn_conv", "_gdn", "_s5_re", "_s5_im"]

_PROG_CACHE = {}


def get_prog(mixers="ABCD", dbg=None):
    key = (mixers, tuple(n for n, _ in (dbg or [])))
    if key not in _PROG_CACHE:
        _PROG_CACHE[key] = Prog(mixers, dbg)
    return _PROG_CACHE[key]


def make_in_maps(inputs, cores):
    maps = []
    f = lambda a: np.ascontiguousarray(np.asarray(a, dtype=np.float32))
    for c in cores:
        m = {"xp": f(inputs["x_prompt"][c]), "xs": f(inputs["x_sample"][16 * c:16 * c + 16].reshape(128, D))}
        for n in STATE_NAMES:
            m[n] = f(inputs[n][:, 16 * c:16 * c + 16])
        for n in WSHAPES:
            m[n] = f(inputs[n])
        maps.append(m)
    return maps


def kernel(**inputs):
    prog = get_prog()
    cores = list(range(NCORES))
    res = run_bass_kernel_spmd(prog.nc, make_in_maps(inputs, cores), core_ids=cores)
    R = res.results
    yp = np.stack([R[c]["yp"] for c in cores], 0)
    ys = np.concatenate([R[c]["ys"].reshape(16, 8, D) for c in cores], 0)
    outs = [yp, ys]
    for pre in ("p", "s"):
        for n in OUT_STATE:
            outs.append(np.concatenate([R[c][pre + n] for c in cores], axis=1))
    return tuple(np.ascontiguousarray(o.astype(np.float32)) for o in outs)
```

```python
import numpy as np
import os
import math
import itertools
INTERLEAVE = int(os.environ.get("KINTER", "1"))
NYBUF = int(os.environ.get("KYBUF", "2"))
SLAT = float(os.environ.get("KSLAT", "0.15"))
NXRES = int(os.environ.get("KXRES", "1"))
CSPLIT = int(os.environ.get("KSPLIT", "4"))
KSTOP = int(os.environ.get("KSTOP", "99"))
KSUB = int(os.environ.get("KSUB", "99"))
from contextlib import ExitStack
import concourse.bass as bass
import concourse.mybir as mybir
from concourse.bass_utils import run_bass_kernel_spmd

F32 = mybir.dt.float32
BF16 = mybir.dt.bfloat16
I32 = mybir.dt.int32
AF = mybir.ActivationFunctionType
ALU = mybir.AluOpType

NCORES = 8
D = 1024
L_P = 2048
NS_SEQ = 16
LS = 8
NB = 256
DEPTH = 2
INC = 2828
ALPHA = (2.0 * DEPTH) ** 0.25
C_AX, C_AG = 0, 256
C_BX, C_BB, C_BC, C_BDT, C_BG = 512, 768, 896, 1024, 1028
C_CQ, C_CK, C_CV, C_CBETA, C_CDEC, C_CG = 1284, 1540, 1796, 2052, 2056, 2060
C_DU, C_DG = 2316, 2572
NEG = -30000.0


class T:
    __slots__ = ("name", "w", "r", "dsem", "dcnt", "psum", "gen")

    def __init__(self, name):
        self.name = name
        self.psum = False
        self.gen = 0
        self.w = None
        self.r = []
        self.dsem = None
        self.dcnt = 0


class View:
    __slots__ = ("ap", "ts", "gen")

    def __init__(self, ap, ts, gen=None):
        self.ap = ap
        self.ts = ts
        self.gen = gen

    def __getitem__(self, idx):
        return View(self.ap[idx], self.ts, self.gen)

    def rr(self, pat, **kw):
        return View(self.ap.rearrange(pat, **kw), self.ts, self.gen)

    def bc(self, shape):
        return View(self.ap.to_broadcast(shape), self.ts, self.gen)


class Buf:
    def __init__(self, tensor, name, t=None, gen=None):
        self.tensor = tensor
        self.t = t or T(name)
        self.gen = gen

    def __getitem__(self, idx):
        return View(self.tensor[idx], [self.t], self.gen)

    def v(self):
        return View(self.tensor[:], [self.t], self.gen)


class Op:
    __slots__ = ("id", "e", "fn", "preds", "dur", "tbl", "kind", "tile", "dval", "idx", "prio", "fin", "kw")

    def __init__(self, id, e, fn, preds, dur, tbl=None, kind="c", tile=None, dval=0):
        self.id, self.e, self.fn, self.preds, self.dur, self.tbl, self.kind, self.tile, self.dval = id, e, fn, preds, dur, tbl, kind, tile, dval
        self.idx = 0
        self.prio = 0.0
        self.fin = 0.0


def _free(ap):
    n = 1
    for d in list(ap.shape)[1:]:
        n *= int(d)
    return n


ACT_TBL = {}


class Sched:
    def __init__(self, nc):
        self.nc = nc
        self.engs = {"pe": nc.tensor, "dve": nc.vector, "act": nc.scalar, "pool": nc.gpsimd, "sp": nc.sync}
        self.sem = {k: nc.alloc_semaphore(name="s_" + k) for k in self.engs}
        self.ops = []
        self.ninst = 0
        self.cnt = {k: 0 for k in self.engs}
        self.out_tiles = []

    def _record(self, e, fn, rt, wt, dur, tbl=None, kind="c", tile=None, dval=0):
        preds = {}
        for t in rt:
            if t.w is not None:
                preds[t.w] = "raw"
            if t.psum:
                for r in t.r:
                    if self.ops[r].e != e:
                        preds.setdefault(r, "rar")
        for t in wt:
            if t.w is not None:
                preds.setdefault(t.w, "waw:" + t.name)
            for r in t.r:
                preds.setdefault(r, "war:" + t.name)
        op = Op(len(self.ops), e, fn, preds, dur, tbl, kind, tile, dval)
        op.kw = getattr(self, "lbl", "")
        self.ops.append(op)
        for t in rt:
            t.r.append(op.id)
        for t in wt:
            t.w = op.id
            t.r = []
        self.ninst += 1
        self.cnt[e] += 1
        return op

    def emit(self, e, fn, reads, writes, dur=0.3, tbl=None):
        rt = []
        for v in reads:
            if isinstance(v, View):
                rt.extend(v.ts)
        wt = []
        for v in writes:
            wt.extend(v.ts)
        for v in list(reads) + list(writes):
            if isinstance(v, View) and v.gen is not None:
                assert v.gen == v.ts[0].gen, f"stale PSUM handle used: bank {v.ts[0].name} was re-allocated (gen {v.gen} vs {v.ts[0].gen})"
        return self._record(e, fn, rt, wt, dur, tbl)

    def dma_in(self, e, dst, src_ap, dram_reads=(), **kw):
        assert len(dst.ts) == 1
        t = dst.ts[0]
        if t.dsem is None:
            t.dsem = self.nc.alloc_semaphore(name="d_" + t.name)
        t.dcnt += 16
        nbytes = _free(dst.ap) * 4
        sib = None
        if t.w is not None and not t.r and self.ops[t.w].kind == "dma" and self.ops[t.w].kw == "dma_in":
            sib = self.ops[t.w]
            t.w = None
        op = self._record(e, lambda: self.engs[e].dma_start(out=dst.ap, in_=src_ap, **kw), list(dram_reads), [t],
                          2.0 + nbytes * 128 / 300e3, kind="dma", tile=t, dval=t.dcnt)
        if sib is not None:
            for p, kd in sib.preds.items():
                op.preds.setdefault(p, kd)
        op.kw = "dma_in"
        return op

    def dma_out(self, e, dst_ap, src, dram_writes=(), **kw):
        assert len(src.ts) == 1
        t = src.ts[0]
        if t.dsem is None:
            t.dsem = self.nc.alloc_semaphore(name="d_" + t.name)
        t.dcnt += 16
        if t not in self.out_tiles:
            self.out_tiles.append(t)
        nbytes = _free(src.ap) * 4
        return self._record(e, lambda: self.engs[e].dma_start(out=dst_ap, in_=src.ap, **kw), [t], list(dram_writes),
                            2.0 + nbytes * 128 / 300e3, kind="dma", tile=t, dval=t.dcnt)

    def drain(self, e, tiles):
        pass

    def finalize(self, reorder=True):
        ops = self.ops
        n = len(ops)
        LAT = float(os.environ.get("KLAT", "0.35"))
        succs = [[] for _ in range(n)]
        for op in ops:
            for p in op.preds:
                succs[p].append(op.id)
        for op in reversed(ops):
            m = 0.0
            for sid in succs[op.id]:
                if ops[sid].prio > m:
                    m = ops[sid].prio
            op.prio = op.dur + LAT + m
        order = {k: [] for k in self.engs}
        if not reorder:
            for op in ops:
                order[op.e].append(op)
        else:
            import heapq
            npred = [len(op.preds) for op in ops]
            ready_t = [0.0] * n
            avail = {k: [] for k in self.engs}
            for op in ops:
                if npred[op.id] == 0:
                    heapq.heappush(avail[op.e], (-op.prio, op.id))
            free = {k: 0.0 for k in self.engs}
            cur_tbl = None
            done = 0
            K = 24
            while done < n:
                best = None
                for e, hp in avail.items():
                    if not hp:
                        continue
                    cands = heapq.nsmallest(K, hp)
                    for (npr, oid) in cands:
                        op = ops[oid]
                        st = max(free[e], ready_t[oid])
                        if e == "act" and op.tbl is not None and op.tbl != cur_tbl:
                            st += 1.3
                        key = (st, npr)
                        if best is None or key < best[0]:
                            best = (key, e, oid, (npr, oid))
                (st, _), e, oid, item = best
                avail[e].remove(item)
                heapq.heapify(avail[e])
                op = ops[oid]
                if e == "act" and op.tbl is not None:
                    cur_tbl = op.tbl
                if op.kind == "dma":
                    free[e] = st + 0.1
                    op.fin = st + op.dur
                else:
                    free[e] = st + op.dur
                    op.fin = free[e]
                order[e].append(op)
                done += 1
                for sid in succs[oid]:
                    npred[sid] -= 1
                    lat = (0.0 if e == "pe" else SLAT) if ops[sid].e == e else LAT
                    if op.fin + lat > ready_t[sid]:
                        ready_t[sid] = op.fin + lat
                    if npred[sid] == 0:
                        heapq.heappush(avail[ops[sid].e], (-ops[sid].prio, sid))
            self.makespan = max(free.values())
        for e, lst in order.items():
            i = 0
            for op in lst:
                if op.kind != "dma":
                    i += 1
                    op.idx = i
        for e, lst in order.items():
            seen = {}
            eng = self.engs[e]

            def wait(key, sem, val):
                if seen.get(key, 0) >= val:
                    return
                eng.wait_ge(sem, val)
                seen[key] = val
            for op in lst:
                dmax = {}
                for p in op.preds:
                    P = ops[p]
                    if P.kind == "dma":
                        k = id(P.tile)
                        if k not in dmax or dmax[k][1] < P.dval:
                            dmax[k] = (P.tile.dsem, P.dval)
                for k in sorted(dmax, key=lambda kk: dmax[kk][1]):
                    wait(("d", k), dmax[k][0], dmax[k][1])
                for p in sorted(op.preds):
                    P = ops[p]
                    if P.kind == "dma":
                        continue
                    elif P.e == e:
                        if e != "pe":
                            wait(e, self.sem[e], P.idx)
                    else:
                        wait(P.e, self.sem[P.e], P.idx)
                ins = op.fn()
                if op.kind == "dma":
                    ins.then_inc(op.tile.dsem, 16)
                else:
                    ins.then_inc(self.sem[e], 1)
            if e == "sp":
                for t in self.out_tiles:
                    wait(("d", id(t)), t.dsem, t.dcnt)
                for k2 in ("pe", "dve", "act", "pool"):
                    last = [o for o in order[k2] if o.kind != "dma"]
                    if last:
                        wait(k2, self.sem[k2], last[-1].idx)

    def _e(self, e):
        return self.engs[e]

    def mm(self, out, lhsT, rhs, start=True, stop=True, tile_position=None):
        kw = {} if tile_position is None else {"tile_position": tile_position}
        n = _free(rhs.ap)
        return self.emit("pe", lambda: self.nc.tensor.matmul(out.ap, lhsT=lhsT.ap, rhs=rhs.ap, start=start, stop=stop, **kw),
                         [lhsT, rhs], [out], dur=0.03 + max(64, n) / 1400.0)

    def tr(self, out, in_, ident):
        return self.emit("pe", lambda: self.nc.tensor.transpose(out.ap, in_.ap, ident.ap), [in_, ident], [out], dur=0.11)

    def act(self, out, in_, func, bias=None, scale=None, e="act"):
        kw = {}
        rd = [in_]
        if bias is not None:
            if isinstance(bias, View):
                kw["bias"] = bias.ap
                rd.append(bias)
            else:
                kw["bias"] = float(bias)
        if scale is not None:
            if isinstance(scale, View):
                kw["scale"] = scale.ap
                rd.append(scale)
            else:
                kw["scale"] = float(scale)
        tbl = {AF.Exp: "el", AF.Ln: "el", AF.Sigmoid: "sg", AF.Silu: "si", AF.Gelu: "ge", AF.Sin: "sn"}.get(func)
        return self.emit("act", lambda: self.nc.scalar.activation(out=out.ap, in_=in_.ap, func=func, **kw), rd, [out],
                         dur=0.2 + _free(out.ap) * 0.00085, tbl=tbl)

    def _dd(self, out):
        return 0.13 + _free(out.ap) * 0.0007

    def copy(self, e, out, in_):
        if e == "act":
            return self.emit("act", lambda: self.nc.scalar.copy(out=out.ap, in_=in_.ap), [in_], [out], dur=0.2 + _free(out.ap) * 0.00085)
        return self.emit(e, lambda: self._e(e).tensor_copy(out=out.ap, in_=in_.ap), [in_], [out], dur=self._dd(out))

    def tt(self, e, out, a, b, op):
        return self.emit(e, lambda: self._e(e).tensor_tensor(out=out.ap, in0=a.ap, in1=b.ap, op=op), [a, b], [out], dur=self._dd(out) * 1.3)

    def ts(self, e, out, a, s1, op0, s2=None, op1=None):
        rd = [a]
        s1a = s1.ap if isinstance(s1, View) else float(s1)
        if isinstance(s1, View):
            rd.append(s1)
        kw = {}
        if op1 is not None:
            kw["op1"] = op1
            s2a = s2.ap if isinstance(s2, View) else float(s2)
            if isinstance(s2, View):
                rd.append(s2)
        else:
            s2a = None
        return self.emit(e, lambda: self._e(e).tensor_scalar(out=out.ap, in0=a.ap, scalar1=s1a, scalar2=s2a, op0=op0, **kw),
                         rd, [out], dur=self._dd(out))

    def stt(self, out, a, s, b, op0, op1):
        rd = [a, b]
        sa = s.ap if isinstance(s, View) else float(s)
        if isinstance(s, View):
            rd.append(s)
        return self.emit("dve", lambda: self.nc.vector.scalar_tensor_tensor(out=out.ap, in0=a.ap, scalar=sa, in1=b.ap,
                                                                           op0=op0, op1=op1), rd, [out], dur=self._dd(out) * 1.3)

    def scan(self, out, d0, d1, init, op0=ALU.mult, op1=ALU.add):
        rd = [d0, d1]
        ia = init.ap if isinstance(init, View) else float(init)
        if isinstance(init, View):
            rd.append(init)
        return self.emit("dve", lambda: self.nc.vector.tensor_tensor_scan(out=out.ap, data0=d0.ap, data1=d1.ap, initial=ia,
                                                                          op0=op0, op1=op1), rd, [out], dur=0.15 + _free(out.ap) * 0.0021)

    def memset(self, e, out, val):
        return self.emit(e, lambda: self._e(e).memset(out.ap, val), [], [out], dur=self._dd(out) * (4 if e == "pool" else 1))

    def recip(self, out, in_):
        return self.emit("dve", lambda: self.nc.vector.reciprocal(out=out.ap, in_=in_.ap), [in_], [out], dur=0.15 + _free(out.ap) * 0.005)


class Blk:
    def __init__(self, idx, sample):
        self.idx = idx
        self.sample = sample
        self.N = 128 if sample else NB
        self.G = NS_SEQ if sample else 1
        self.Lg = LS if sample else NB
        self.ntile = self.N // 128
        self.tok0 = 2048 if sample else idx * NB
        self.first = (not sample) and idx == 0
        self.last = (not sample) and idx == L_P // NB - 1


class Prog:
    def __init__(self, mixers="ABCD", dbg=None):
        self.mixers = mixers
        self.dbg_req = dbg or []
        self.nc = nc = bass.Bass("TRN2", target_bir_lowering=False)
        self.es = ExitStack()
        self.es.enter_context(nc.allow_non_contiguous_dma(reason="small strided parameter/state loads"))
        self.S = Sched(nc)
        self.uid = 0
        self.out_tiles = []
        self._declare_dram()
        self._alloc()
        self._consts()
        for l in range(DEPTH):
            self._layer(l)
        self.S.finalize(reorder=bool(int(os.environ.get("KSCHED", "1"))))
        self.es.close()

    def sb(self, name, shape, dt=F32):
        self.uid += 1
        t = self.es.enter_context(self.nc.sbuf_tensor(f"{name}_{self.uid}", list(shape), dt))
        return Buf(t, name)

    def _declare_dram(self):
        nc = self.nc
        di = lambda n, s: nc.dram_tensor(n, list(s), F32, kind="ExternalInput").ap()
        do = lambda n, s: nc.dram_tensor(n, list(s), F32, kind="ExternalOutput").ap()
        self.I = I = {}
        I["xp"] = di("xp", [L_P, D])
        I["xs"] = di("xs", [128, D])
        I["cache_rglru_conv"] = di("cache_rglru_conv", [DEPTH, 16, 3, 256])
        I["state_rglru"] = di("state_rglru", [DEPTH, 16, 256])
        I["cache_ssd_conv"] = di("cache_ssd_conv", [DEPTH, 16, 3, 512])
        I["state_ssd"] = di("state_ssd", [DEPTH, 16, 4, 64, 64])
        I["cache_gdn_conv"] = di("cache_gdn_conv", [DEPTH, 16, 3, 768])
        I["state_gdn"] = di("state_gdn", [DEPTH, 16, 4, 64, 64])
        I["state_s5_re"] = di("state_s5_re", [DEPTH, 16, 16, 64])
        I["state_s5_im"] = di("state_s5_im", [DEPTH, 16, 16, 64])
        for n, s in WSHAPES.items():
            I[n] = di(n, s)
        self.O = O = {}
        O["yp"] = do("yp", [L_P, D])
        O["ys"] = do("ys", [128, D])
        for pre, nb in (("p", 1), ("s", 16)):
            O[pre + "_rg_conv"] = do(pre + "_rg_conv", [DEPTH, nb, 3, 256])
            O[pre + "_rg"] = do(pre + "_rg", [DEPTH, nb, 256])
            O[pre + "_ssd_conv"] = do(pre + "_ssd_conv", [DEPTH, nb, 3, 512])
            O[pre + "_ssd"] = do(pre + "_ssd", [DEPTH, nb, 4, 64, 64])
            O[pre + "_gdn_conv"] = do(pre + "_gdn_conv", [DEPTH, nb, 3, 768])
            O[pre + "_gdn"] = do(pre + "_gdn", [DEPTH, nb, 4, 64, 64])
            O[pre + "_s5_re"] = do(pre + "_s5_re", [DEPTH, nb, 16, 64])
            O[pre + "_s5_im"] = do(pre + "_s5_im", [DEPTH, nb, 16, 64])
        self.x1 = nc.dram_tensor("x1_scratch", [L_P + 128, D], F32).ap()
        self.x1_t = [T(f"x1row{i}") for i in range((L_P + 128) // 128)]
        self.DBG = {}
        for n, s in self.dbg_req:
            self.DBG[n] = do("dbg_" + n, s)

    def _alloc(self):
        nc = self.nc
        self.psb = []
        for i in range(7):
            t = self.es.enter_context(nc.psum_tensor(f"ps{i}", [128, 512], F32))
            self.psb.append(Buf(t, f"ps{i}"))
            self.psb[-1].t.psum = True
        t = self.es.enter_context(nc.psum_tensor("pst", [128, 1024], BF16))
        self.pst = Buf(t, "pst")
        self.pst.t.psum = True
        self.ps_i = 0
        self.psC = 0
        self.psO = 0
        self.psD = 0
        self.psS = 0
        self.WbG = []
        for (c0, c1) in ((0, C_BX), (C_BX, C_CQ), (C_CQ, C_DU), (C_DU, INC)):
            self.WbG.append((c0, c1, self.sb(f"Wb{c0}", [128, 8, c1 - c0], BF16)))
        self.Wo = self.sb("Wo", [128, 8, D], BF16)
        self.Wsm = self.sb("Wsm", [128, 8, 12], BF16)
        self.wst_i = 0
        self.xrt = [self.sb(f"xres{i}", [128, D]) for i in range(NB // 128)]
        self.xbf = [self.sb("xbf0", [128, D], BF16)]
        self.xT = self.sb("xT", [128, 8, NB], BF16)
        self.xTl = self.sb("xTl", [128, 8, 48], BF16)
        self.mixT = [self.sb(f"mixT{i}", [128, NB], BF16) for i in range(8)]
        self.lng = self.sb("lng", [128, D])
        self.lnb = self.sb("lnb", [128, D])
        self.ybuf = [self.sb(f"ybuf{i}", [128, D]) for i in range(NYBUF)]
        self.lnstat = [self.sb(f"lnstat{i}", [128, 16]) for i in range(2)]
        self.cst_i = 0
        self.cpad = [self.sb(f"cpad{i}", [128, NB + 3], BF16) for i in range(12)]
        self.cacc = [self.sb(f"cacc{i}", [128, NB]) for i in range(4)]
        self.cd_i = 0
        self.cw = self.sb("cw", [128, 4, 12])
        self.cbias = self.sb("cbias", [128, 12])
        self.chist = [self.sb(f"chist{i}", [48, 128]) for i in range(2)]
        self.ftmp = [self.sb(f"ftmp{i}", [128, NB]) for i in range(10)]
        self.wtmp = [self.sb(f"wtmp{i}", [128, 512]) for i in range(3)]
        self.btmp = [self.sb(f"btmp{i}", [128, NB], BF16) for i in range(6)]
        self.xcA = [self.sb(f"xcA{i}", [128, NB]) for i in range(2)]
        self.xcbA = [self.sb(f"xcbA{i}", [128, NB], BF16) for i in range(2)]
        self.rgW = self.sb("rgW", [128, 2, 2, 128], BF16)
        self.rgp = self.sb("rgp", [128, 2, 8])
        self.rgh = self.sb("rgh", [128, 2])
        self.rgh0 = self.sb("rgh0", [128, 2, 16])
        self.rgst = self.sb("rgst", [16, 256])
        self.rgout = self.sb("rgout", [128, 2, 16])

        self.Um = {}; self.NEGM = {}; self.SG = {}
        for i, k in enumerate(("p", "s")):
            self.NEGM[k] = self.sb("NEGM" + k, [128, 128])
        self.Umb = {k: self.sb("Umb" + k, [128, 128], BF16) for k in ("p", "s")}
        self.SGb = {k: self.sb("SGb" + k, [128, 128], BF16) for k in ("p", "s")}
        self.dsplit = self.sb("dsplit", [128, 3, 8], BF16)
        self.dres = self.sb("dres", [128, 8])
        self.SeqSel = self.sb("SeqSel", [128, 16])
        self.SeqSelb = self.sb("SeqSelb", [128, 16], BF16)
        self.onesb = self.sb("onesb", [128, 128], BF16)
        self.smtok = self.sb("smtok", [128, NB // 128, 12])
        self.ssp = self.sb("ssp", [128, 16])
        self.ssD = self.sb("ssD", [128, 2])
        self.ssnw = self.sb("ssnw", [128, 2])
        self.H0T = self.sb("H0T", [128, 16, 2, 64], BF16)
        self.HTm = self.sb("HTm", [128, 2, 64])
        self.HTb = self.sb("HTb", [128, 2, 64], BF16)
        self.cq = [self.sb(f"cq{i}", [128, NB], BF16) for i in range(6)]
        self.cf = [self.sb(f"cf{i}", [128, NB]) for i in range(4)]
        self.csm = self.sb("csm", [128, 32])
        self.ctok = self.sb("ctok", [128, 512], BF16)
        self.csc = [self.sb(f"csc{i}", [128, 32]) for i in range(2)]
        self.cdsplit = self.sb("cdsplit", [128, 3, 8], BF16)
        self.cdres = self.sb("cdres", [128, 8])
        self.bw = [self.sb(f"bw{i}", [128, 512]) for i in range(2)]
        self.gdp = self.sb("gdp", [128, 16])
        self.gdnw = self.sb("gdnw", [128, 1])
        self.POSM = {k: self.sb("POSM" + k, [128, 128]) for k in ("p", "s")}
        self.blockones = self.sb("blockones", [128, 128], BF16)
        self.gM = [self.sb(f"gM{i}", [128, 4, 128], BF16) for i in range(3)]
        self.Moff = self.sb("Moff", [128, 4, 128], BF16)
        self.gR = self.sb("gR", [128, 4, 128], BF16)
        self.D32b = self.sb("D32b", [128, 128], BF16)
        self.gN = [self.sb(f"gN{i}", [128, 4, 128], BF16) for i in range(2)]
        self.gX = [self.sb(f"gX{i}", [128, 4, 128], BF16) for i in range(2)]
        self.gW = [self.gM[0], self.gM[1]]
        self.nGT = self.gN[0]
        self.QKG = self.sb("QKG", [128, 4, 128], BF16)
        self.XTb = self.sb("XTb", [128, 4, 128], BF16)
        self.Val = self.sb("Val", [128, 256])
        self.qdec = self.sb("qdec", [128, 4, 128], BF16)
        self.kdec = self.sb("kdec", [128, 4, 64], BF16)
        self.wtok = self.sb("wtok", [128, 4, 64], BF16)
        self.gsq = self.sb("gsq", [128, NB], BF16)
        self.Sm = self.sb("Sm", [128, 2, 64])
        self.Sb = self.sb("Sb", [128, 2, 64], BF16)
        self.s5Sel = self.sb("s5Sel", [128, 16, 128], BF16)
        self.s5SelU = self.sb("s5SelU", [128, 16, 32], BF16)
        self.s5Toep = self.sb("s5Toep", [128, 16, 128], BF16)
        self.s5RT = self.sb("s5RT", [128, 2, 16, 64], BF16)
        self.s5P = self.sb("s5P", [128, 2, 8, 128], BF16)
        self.s5E = self.sb("s5E", [128, 2, 8, 32])
        self.s5par = self.sb("s5par", [128, 8, 8])
        self.s5sm = self.sb("s5sm", [128, 8, 8])
        self.s5H = self.sb("s5H", [128, 2, 8])
        self.s5Hst = self.sb("s5Hst", [128, 2, 8, 32], BF16)
        self.s5Uf = self.sb("s5Uf", [128, 16, 32], BF16)
        self.s5Yf = self.sb("s5Yf", [128, 16, 32], BF16)
        self.s5uT = [self.sb(f"s5uT{i}", [96, NB], BF16) for i in range(3)]
        self.s5glu = self.sb("s5glu", [128, 2, 256], BF16)
        self.s5gb = self.sb("s5gb", [128, 2])
        self.s5Dfold = self.sb("s5Dfold", [128, 16])
        self.Cmask = self.sb("Cmask", [128, 128])
        self.s5q = self.sb("s5q", [128, 64])
        self.s5qi = self.sb("s5qi", [128, 64], I32)
        self.itmp = self.sb("itmp", [128, 256], I32)
        self.df = [self.sb(f"df{i}", [128, NB]) for i in range(6)]
        self.dbf = [self.sb(f"dbf{i}", [128, NB], BF16) for i in range(2)]
        self.tok = self.sb("tok", [128, 512], BF16)
        self.sc = [self.sb(f"sc{i}", [128, 32]) for i in range(2)]
        self.Wt = self.sb("Wt", [128, 4, 128], BF16)
        self.Cdec = self.sb("Cdec", [128, 4, 128], BF16)
        self.Bdec = self.sb("Bdec", [128, 4, 64], BF16)
        self.etotb = self.sb("etotb", [64, 4, 16])

    def bexp(self, hf):
        return self.gX[hf].v().rr("p h (a n) -> p (h a) n", n=64)

    def hnat(self, hh):
        src = self.xrt[1][0:64, :] if hh == 0 else self.ybuf[0][0:64, :]
        return src.rr("p (b n) -> p b n", n=64)

    def ps(self):
        cur = getattr(self, "cur", None)
        if cur == "C":
            b = self.psb[self.psC % 3]
            self.psC += 1
        elif cur == "B":
            b = self.psb[3 + self.psO % 2]
            self.psO += 1
        elif cur == "S1":
            b = self.psb[self.psS % 5]
            self.psS += 1
        elif cur == "D":
            b = self.psb[5 + self.psD % 2]
            self.psD += 1
        else:
            b = self.psb[self.ps_i % 7]
            self.ps_i += 1
        b.t.gen += 1
        return Buf(b.tensor, b.t.name, t=b.t, gen=b.t.gen)

    def _consts(self):
        nc, S = self.nc, self.S
        self.identf = self.sb("identf", [128, 128])
        self.ident = self.sb("ident", [128, 128], BF16)
        S.memset("pool", self.identf.v(), 1.0)
        S.emit("pool", lambda: nc.gpsimd.affine_select(out=self.identf.tensor[:], in_=self.identf.tensor[:], pattern=[[-1, 128]],
                                                       compare_op=ALU.is_equal, fill=0.0, base=0, channel_multiplier=1),
               [self.identf.v()], [self.identf.v()])
        S.copy("dve", self.ident.v(), self.identf.v())
        class _V:
            def __init__(s_, view):
                s_.view = view
            def v(s_):
                return s_.view
        for i, k in enumerate(("p", "s")):
            self.Um[k] = _V(self.ftmp[i][:, 0:128])
            self.SG[k] = _V(self.ftmp[2 + i][:, 0:128])
        S.memset("pool", self.onesb.v(), 1.0)
        asel = lambda buf, pat, cm, base: S.emit("pool", lambda: nc.gpsimd.affine_select(
            out=buf.ap, in_=buf.ap, pattern=pat, compare_op=ALU.is_ge, fill=0.0, base=base, channel_multiplier=cm), [buf], [buf])
        S.memset("pool", self.SG["p"].v(), 1.0)
        S.memset("pool", self.SG["s"].v(), 1.0)
        sg3 = self.SG["s"].v().rr("p (b j) -> p b j", j=8)
        asel(sg3, [[-8, 16], [0, 8]], 1, 0)
        asel(sg3, [[8, 16], [0, 8]], -1, 7)
        S.memset("pool", self.Um["p"].v(), 1.0)
        asel(self.Um["p"].v(), [[1, 128]], -1, 0)
        S.tt("pool", self.Um["s"].v(), self.Um["p"].v(), self.SG["s"].v(), ALU.mult)
        for k in ("p", "s"):
            S.ts("pool", self.NEGM[k].v(), self.Um[k].v(), -NEG, ALU.mult, NEG, ALU.add)
        for k in ("p", "s"):
            S.copy("pool", self.Umb[k].v(), self.Um[k].v())
            S.copy("pool", self.SGb[k].v(), self.SG[k].v())
        for k in ("p", "s"):
            S.memset("pool", self.POSM[k].v(), 1.0)
            asel(self.POSM[k].v(), [[-1, 128]], 1, -1)
            S.tt("pool", self.POSM[k].v(), self.POSM[k].v(), self.SG[k].v(), ALU.mult)
            S.ts("pool", self.POSM[k].v(), self.POSM[k].v(), NEG, ALU.mult, -NEG, ALU.add)
        bo = _V(self.ftmp[4][:, 0:128])
        S.memset("pool", bo.v(), 1.0)
        bo3 = bo.v().rr("p (b j) -> p b j", j=64)
        asel(bo3, [[-64, 2], [0, 64]], 1, 0)
        asel(bo3, [[64, 2], [0, 64]], -1, 63)
        S.copy("pool", self.blockones.v(), bo.v())
        S.memset("pool", bo.v(), 1.0)
        bo4 = bo.v().rr("p (b j) -> p b j", j=32)
        asel(bo4, [[-32, 4], [0, 32]], 1, 0)
        asel(bo4, [[32, 4], [0, 32]], -1, 31)
        S.copy("pool", self.D32b.v(), bo.v())
        sel = self.s5Sel
        S.memset("pool", sel.v(), 1.0)
        b0 = sel[0:32, :, :].rr("p (a s) c -> p a s c", a=2)
        S.emit("pool", lambda: nc.gpsimd.affine_select(out=b0.ap, in_=b0.ap, pattern=[[16, 2], [-16, 8], [1, 128]], compare_op=ALU.is_equal,
                                                       fill=0.0, base=0, channel_multiplier=-1), [b0], [b0])
        asel(b0, [[-16, 2], [0, 8], [0, 128]], 1, 0)
        asel(b0, [[16, 2], [0, 8], [0, 128]], -1, 15)
        S.copy("pool", sel[32:64, :, :], sel[0:32, :, :])
        S.copy("pool", sel[64:96, :, :], sel[0:32, :, :])
        su = self.s5SelU.v().rr("p (a s) c -> p a s c", a=2)
        S.memset("pool", su, 1.0)
        S.emit("pool", lambda: nc.gpsimd.affine_select(out=su.ap, in_=su.ap, pattern=[[16, 2], [-16, 8], [-1, 32]], compare_op=ALU.is_equal,
                                                       fill=0.0, base=0, channel_multiplier=1), [su], [su])
        asel(su, [[-16, 2], [0, 8], [1, 32]], 0, 0)
        asel(su, [[16, 2], [0, 8], [-1, 32]], 0, 15)
        S.memset("pool", self.Cmask.v(), 1.0)
        asel(self.Cmask.v().rr("p (t i) -> p t i", i=16), [[16, 8], [0, 16]], -1, 15)
        for seg, (base, step) in enumerate(((7, -1), (1, 1), (0, -1), (0, 1))):
            S.emit("pool", lambda seg=seg, step=step, base=base: nc.gpsimd.iota(self.s5qi.tensor[:, seg * 8:(seg + 1) * 8], pattern=[[step, 8]], base=base,
                                                  channel_multiplier=0), [], [self.s5qi.v()])
        S.emit("pool", lambda: nc.gpsimd.iota(self.s5qi.tensor[:, 32:64], pattern=[[1, 32]], base=1, channel_multiplier=0),
               [], [self.s5qi.v()])
        S.copy("pool", self.s5q.v(), self.s5qi.v())
        S.memset("pool", self.SeqSel.v(), 1.0)
        asel(self.SeqSel.v(), [[-8, 16]], 1, 0)
        asel(self.SeqSel.v(), [[8, 16]], -1, 7)
        S.copy("pool", self.SeqSelb.v(), self.SeqSel.v())

    def dbg(self, name, view):
        if name in self.DBG:
            self.S.dma_out("sp", self.DBG[name], view)
            self.out_tiles.extend(view.ts)

    def _load_weights(self, l):
        S, I = self.S, self.I
        for (g0, g1, buf) in self.WbG:
            for k in range(8):
                c0 = g0
                while c0 < g1:
                    w = min(512, g1 - c0)
                    S.dma_in("pool", buf[:, k, c0 - g0:c0 - g0 + w], I["w_in"][l, k * 128:(k + 1) * 128, c0:c0 + w])
                    c0 += w
        for k in range(8):
            for hf in range(2):
                S.dma_in("pool", self.Wo[:, k, hf * 512:(hf + 1) * 512], I["w_out"][l, k * 128:(k + 1) * 128, hf * 512:(hf + 1) * 512])
        S.copy("dve", self.Wsm[:, :, 0:4], self.wb(None, C_BDT, 4))
        S.copy("dve", self.Wsm[:, :, 4:12], self.wb(None, C_CBETA, 8))
        S.dma_in("sp", self.lng.v(), I["ln_g"][l:l + 1, :].to_broadcast([128, D]))
        S.dma_in("sp", self.lnb.v(), I["ln_b"][l:l + 1, :].to_broadcast([128, D]))
        for (nm, c0, nch) in (("rg", 0, 2), ("ssd", 2, 4), ("gdn", 6, 6)):
            for j in range(4):
                S.dma_in("sp", self.cw[:, j, c0:c0 + nch], I[nm + "_conv_w"][l, j].rearrange("(c p) -> p c", p=128),
                         allow_slow_non_contiguous=True)
            S.dma_in("sp", self.cbias[:, c0:c0 + nch], I[nm + "_conv_b"][l].rearrange("(c p) -> p c", p=128),
                     allow_slow_non_contiguous=True)
        if "A" in self.mixers:
            self._load_A(l)
        if "B" in self.mixers:
            self._load_B(l)
        if "C" in self.mixers:
            self._load_C(l)
        if "D" in self.mixers:
            self._load_D(l)

    def _layer(self, l):
        self._load_weights(l)
        blocks = [Blk(i, False) for i in range(L_P // NB)] + [Blk(0, True)]
        for blk in blocks:
            self._block(l, blk)

    def _xsrc(self, l, blk):
        if l == 0:
            return self.I["xs"] if blk.sample else self.I["xp"][blk.tok0:blk.tok0 + blk.N, :]
        return self.x1[blk.tok0:blk.tok0 + blk.N, :]

    def _ydst(self, l, blk):
        if l == DEPTH - 1:
            return self.O["ys"] if blk.sample else self.O["yp"][blk.tok0:blk.tok0 + blk.N, :]
        return self.x1[blk.tok0:blk.tok0 + blk.N, :]

    def _block(self, l, blk):
        S = self.S
        N = blk.N
        src = self._xsrc(l, blk).rearrange("(t p) d -> p t d", p=128)
        self.xr_i = getattr(self, "xr_i", 0) + 1
        t0 = blk.tok0 // 128
        for ti in range(blk.ntile):
            S.dma_in("sp", self.xrt[ti].v(), src[:, ti, :], dram_reads=([self.x1_t[t0 + ti]] if l > 0 else ()))
        for ti in range(blk.ntile):
            xb = self.xbf[0]
            S.copy("act", xb.v(), self.xrt[ti].v())
            for k in range(8):
                S.tr(self.pst[:, k * 128:(k + 1) * 128], xb[:, k * 128:(k + 1) * 128], self.ident.v())
            S.copy("act", self.xT[:, :, ti * 128:(ti + 1) * 128], self.pst.v().rr("p (k t) -> p k t", k=8))
        if blk.last or blk.sample:
            self._conv_cache_out(l, blk)
        if "B" in self.mixers or "C" in self.mixers:
            for ti in range(blk.ntile):
                p = self.ps()
                for k in range(8):
                    S.mm(p[:, 0:12], self.xT[:, k, ti * 128:(ti + 1) * 128], self.Wsm[:, k, :], start=(k == 0), stop=(k == 7))
                S.copy("dve", self.smtok[:, ti, :], p[:, 0:12])
        self._conv_phase(l, blk)
        gens = {}
        for mi, m in enumerate("ABCD"):
            if m in self.mixers:
                gens[m] = getattr(self, "_mixer_" + m)(l, blk)
            else:
                for i in (2 * mi, 2 * mi + 1):
                    S.memset("pool", self.mixT[i][:, 0:N], 0.0)
        if not INTERLEAVE or "C" not in gens or (blk.sample and "D" not in gens):
            for m in "ABCD":
                if m in gens:
                    for _ in gens[m]:
                        pass
        elif blk.sample:
            streams = [("S1", itertools.chain(*[gens[m] for m in "ABC" if m in gens])), ("D", gens["D"])]
            alive = list(streams)
            while alive:
                for item in list(alive):
                    self.cur = item[0]
                    try:
                        next(item[1])
                    except StopIteration:
                        alive.remove(item)
            self.cur = None
        else:
            streams = [("C", gens["C"]), ("B", itertools.chain(*[gens[m] for m in "AB" if m in gens]))]
            if "D" in gens:
                streams.append(("D", gens["D"]))
            alive = list(streams)
            while alive:
                for item in list(alive):
                    self.cur = item[0]
                    try:
                        next(item[1])
                    except StopIteration:
                        alive.remove(item)
            self.cur = None
        dst = self._ydst(l, blk).rearrange("(t p) d -> p t d", p=128)
        for ti in range(blk.ntile):
            self._outproj_tile(l, blk, ti, dst)

    def _conv_phase(self, l, blk):
        S = self.S
        N = blk.N
        jobs = []
        if "A" in self.mixers:
            jobs += [("A", fc) for fc in range(2)]
        if "B" in self.mixers:
            jobs += [("B", i) for i in range(4)]
        if "C" in self.mixers:
            jobs += [("C", i) for i in range(6)]
        for (m, i) in jobs:
            if m == "A":
                pc = self.conv_chunk(blk, l, i, C_AX + i * 128, None)
                S.act(self.xcA[i][:, 0:N], pc, AF.Identity, bias=self.cbias[:, i:i + 1], scale=1.0)
                S.act(self.xcbA[i][:, 0:N], pc, AF.Identity, bias=self.cbias[:, i:i + 1], scale=1.0)
            elif m == "B":
                cc = 2 + i
                pc = self.conv_chunk(blk, l, cc, C_BX + i * 128, None)
                S.act(self.btmp[i][:, 0:N], pc, AF.Silu, bias=self.cbias[:, cc:cc + 1], scale=1.0)
            else:
                cc = 6 + i
                pc = self.conv_chunk(blk, l, cc, C_CQ + i * 128, None)
                S.act(self.cq[i][:, 0:N], pc, AF.Silu, bias=self.cbias[:, cc:cc + 1], scale=1.0)

    def wb(self, k, c0, w):
        for (g0, g1, buf) in self.WbG:
            if g0 <= c0 and c0 + w <= g1:
                return buf[:, :, c0 - g0:c0 - g0 + w] if k is None else buf[:, k, c0 - g0:c0 - g0 + w]
        raise AssertionError((c0, w))

    def zmm(self, blk, c0, width, out_ps):
        for k in range(8):
            self.S.mm(out_ps, self.wb(k, c0, width), self.xT[:, k, 0:blk.N], start=(k == 0), stop=(k == 7))

    def _conv_cache_out(self, l, blk):
        S = self.S
        pre = "s" if blk.sample else "p"
        if blk.sample:
            M = 48
            for k in range(8):
                S.copy("dve", self.xTl[:, k, :].rr("p (b j) -> p b j", j=3),
                       self.xT[:, k, 0:128].rr("p (b j) -> p b j", j=8)[:, :, 5:8])
            lhs = lambda k: self.xTl[:, k, :]
        else:
            M = 3
            lhs = lambda k: self.xT[:, k, NB - 3:NB]
        for (nm, c0, w) in (("_rg_conv", C_AX, 256), ("_ssd_conv", C_BX, 512), ("_gdn_conv", C_CQ, 512), ("_gdn_conv2", C_CQ + 512, 256)):
            p = self.ps()
            for k in range(8):
                S.mm(p[0:M, 0:w], lhs(k), self.wb(k, c0, w), start=(k == 0), stop=(k == 7))
            st = self.ybuf[0]
            self.cst_i += 1
            S.copy("act", st[0:M, 0:w], p[0:M, 0:w])
            if nm == "_gdn_conv2":
                dst = self.O[pre + "_gdn_conv"][l].rearrange("b j c -> (b j) c")[:, 512:768]
            elif nm == "_gdn_conv":
                dst = self.O[pre + "_gdn_conv"][l].rearrange("b j c -> (b j) c")[:, 0:512]
            else:
                dst = self.O[pre + nm][l].rearrange("b j c -> (b j) c")
            S.dma_out("sp", dst, st[0:M, 0:w])
            self.out_tiles.append(st.t)

    def _outproj_tile(self, l, blk, ti, dst):
        S, nc = self.S, self.nc
        pa, pb = self.ps(), self.ps()
        tsl = slice(ti * 128, (ti + 1) * 128)
        for half, p in ((0, pa), (1, pb)):
            for k in range(8):
                S.mm(p.v(), self.mixT[k][:, tsl], self.Wo[:, k, half * 512:(half + 1) * 512], start=(k == 0), stop=(k == 7))
        self.yb_i = getattr(self, "yb_i", 0) + 1
        r = self.ybuf[self.yb_i % NYBUF]
        st = self.lnstat[ti % 2]
        for half, p in ((0, pa), (1, pb)):
            hs = slice(half * 512, (half + 1) * 512)
            S.stt(r[:, hs], self.xrt[ti][:, hs], ALPHA, p.v(), ALU.mult, ALU.add)
            S.emit("dve", lambda half=half, hs=hs: nc.vector.bn_stats(out=st.tensor[:, half * 6:(half + 1) * 6], in_=r.tensor[:, hs]), [r.v()], [st.v()])
        S.emit("dve", lambda: nc.vector.bn_aggr(out=st.tensor[:, 12:14], in_=st.tensor[:, 0:12]), [st.v()], [st.v()])
        self.rsqrt(st[:, 14:15], st[:, 13:14], 1.0, self.eps5())
        S.stt(st[:, 15:16], st[:, 12:13], -1.0, st[:, 14:15], ALU.mult, ALU.mult)
        y = r
        S.act(y.v(), r.v(), AF.Identity, bias=st[:, 15:16], scale=st[:, 14:15])
        S.tt("dve", y.v(), y.v(), self.lng.v(), ALU.mult)
        S.tt("dve", y.v(), y.v(), self.lnb.v(), ALU.add)
        S.dma_out("sp", dst[:, ti, :], y.v(), dram_writes=([self.x1_t[blk.tok0 // 128 + ti]] if l < DEPTH - 1 else ()))
        if l == 0 and "x1" in self.DBG:
            r0 = blk.tok0 + ti * 128
            S.dma_out("sp", self.DBG["x1"][r0:r0 + 128, :], y.v())
        self.out_tiles.append(y.t)

    def eps5(self):
        if not hasattr(self, "_eps5"):
            self._eps5 = self.sb("eps5", [128, 1])
            self.S.memset("pool", self._eps5.v(), 1e-5)
        return self._eps5.v()

    def conv_chunk(self, blk, l, cc, zc0, out_ps, m0=0, m1=128):
        S = self.S
        N = blk.N
        cp = self.cpad[cc]
        acc = self.cacc[self.cd_i % 4]
        self.cd_i += 1
        pz = self.ps()
        self.zmm(blk, zc0, 128, pz[:, 0:N])
        if blk.sample:
            cpv = cp[:, 0:176].rr("p (g l) -> p g l", l=11)
            ph = self.ps()
            ch = self.chist[cc % 2]
            nm, c0 = (("cache_rglru_conv", cc * 128) if cc < 2 else ("cache_ssd_conv", (cc - 2) * 128) if cc < 6
                      else ("cache_gdn_conv", (cc - 6) * 128))
            S.dma_in("sp", ch.v(), self.I[nm][l].rearrange("b j c -> (b j) c")[:, c0:c0 + 128])
            S.tr(ph[:, 0:48], ch[0:48, :], self.identf[0:48, 0:48])
            S.copy("dve", cpv[:, :, 0:3], ph[:, 0:48].rr("p (b j) -> p b j", j=3))
            S.copy("act", cpv[:, :, 3:11], pz[:, 0:N].rr("p (b j) -> p b j", j=8))
            rhs = lambda j: cpv[:, :, j:j + 8]
            accv = acc[:, 0:N].rr("p (b j) -> p b j", j=8)
        else:
            if blk.first:
                S.memset("pool", cp[:, 0:3], 0.0)
            else:
                S.copy("dve", cp[:, 0:3], cp[:, NB:NB + 3])
            S.copy("act", cp[:, 3:NB + 3], pz[:, 0:N])
            rhs = lambda j: cp[:, j:j + NB]
            accv = acc[:, 0:N]
        S.act(accv, rhs(0), AF.Copy, scale=self.cw[:, 0, cc:cc + 1])
        for j in range(1, 4):
            S.stt(accv, rhs(j), self.cw[:, j, cc:cc + 1], accv, ALU.mult, ALU.add)
        return acc[:, 0:N]

    def _load_A(self, l):
        S, I = self.S, self.I
        rgWst = self.wtmp[0].v().rr("p (a b c) -> p a b c", a=2, b=2)
        S.memset("pool", rgWst, 0.0)
        for gi, nm in enumerate(("rg_gate_a_w", "rg_gate_x_w")):
            for fc in range(2):
                for h2 in range(2):
                    S.dma_in("sp", rgWst[h2 * 64:(h2 + 1) * 64, fc, gi, h2 * 64:(h2 + 1) * 64], I[nm][l, fc * 2 + h2])
        S.copy("dve", self.rgW.v(), rgWst)
        for i, nm in enumerate(("rg_gate_a_b", "rg_gate_x_b", "rg_lambda")):
            S.dma_in("sp", self.rgp[:, :, i], I[nm][l].rearrange("(c p) -> p c", p=128), allow_slow_non_contiguous=True)
        S.act(self.rgp[:, :, 3], self.rgp[:, :, 2], AF.Exp, scale=-1.0)
        S.act(self.rgp[:, :, 3], self.rgp[:, :, 3], AF.Ln, bias=1.0)
        S.ts("dve", self.rgp[:, :, 4], self.rgp[:, :, 3], -16.0, ALU.mult)
        S.ts("dve", self.rgp[:, :, 3], self.rgp[:, :, 3], -8.0, ALU.mult)
        S.dma_in("sp", self.rgst.v(), I["state_rglru"][l])
        for fc in range(2):
            p = self.ps()
            S.tr(p[:, 0:16], self.rgst[0:16, fc * 128:(fc + 1) * 128], self.identf[0:16, 0:16])
            S.copy("dve", self.rgh0[:, fc, :], p[:, 0:16])

    def _mixer_A(self, l, blk):
        S = self.S
        N = blk.N
        AT = self.ftmp
        for fc in range(2):
            xc = self.xcA[fc]
            xcb = self.xcbA[fc]
            yield
            pr, pi = self.ps(), self.ps()
            S.mm(pr[:, 0:N], self.rgW[:, fc, 0, :], xcb[:, 0:N])
            S.mm(pi[:, 0:N], self.rgW[:, fc, 1, :], xcb[:, 0:N])
            gr, gi = AT[1], AT[2]
            S.act(gr[:, 0:N], pr[:, 0:N], AF.Sigmoid, bias=self.rgp[:, fc, 0:1], scale=1.0)
            S.act(gi[:, 0:N], pi[:, 0:N], AF.Sigmoid, bias=self.rgp[:, fc, 1:2], scale=1.0)
            a, a2 = AT[3], AT[4]
            S.act(a[:, 0:N], gr[:, 0:N], AF.Exp, scale=self.rgp[:, fc, 3:4])
            S.act(a2[:, 0:N], gr[:, 0:N], AF.Exp, scale=self.rgp[:, fc, 4:5])
            S.act(a2[:, 0:N], a2[:, 0:N], AF.Ln, bias=1.0, scale=-1.0)
            S.act(a2[:, 0:N], a2[:, 0:N], AF.Exp, scale=0.5)
            bb = AT[5]
            S.tt("dve", bb[:, 0:N], a2[:, 0:N], gi[:, 0:N], ALU.mult)
            S.tt("dve", bb[:, 0:N], bb[:, 0:N], xc[:, 0:N], ALU.mult)
            h = AT[6]
            if blk.sample:
                a3 = a[:, 0:N].rr("p (b j) -> p b j", j=8)
                b3 = bb[:, 0:N].rr("p (b j) -> p b j", j=8)
                tmp = AT[7]
                S.tt("dve", tmp[:, 0:16], a3[:, :, 0], self.rgh0[:, fc, :], ALU.mult)
                S.tt("dve", b3[:, :, 0], b3[:, :, 0], tmp[:, 0:16], ALU.add)
                S.memset("dve", a3[:, :, 0], 0.0)
                S.scan(h[:, 0:N], a[:, 0:N], bb[:, 0:N], 0.0)
            else:
                init = 0.0 if blk.first else self.rgh[:, fc:fc + 1]
                S.scan(h[:, 0:N], a[:, 0:N], bb[:, 0:N], init)
                S.copy("dve", self.rgh[:, fc:fc + 1], h[:, N - 1:N])
            yield
            pg = self.ps()
            self.zmm(blk, C_AG + fc * 128, 128, pg[:, 0:N])
            sg = AT[8]
            S.act(sg[:, 0:N], pg[:, 0:N], AF.Silu)
            S.tt("dve", self.mixT[fc][:, 0:N], h[:, 0:N], sg[:, 0:N], ALU.mult)
            if blk.last:
                S.dma_out("sp", self.O["p_rg"][l, 0, fc * 128:(fc + 1) * 128].rearrange("(p o) -> p o", o=1), self.rgh[:, fc:fc + 1])
                self.out_tiles.append(self.rgh.t)
            if blk.sample:
                h3 = h[:, 0:N].rr("p (b j) -> p b j", j=8)
                S.copy("dve", self.rgout[:, fc, :], h3[:, :, 7])
                S.dma_out("sp", self.O["s_rg"][l][:, fc * 128:(fc + 1) * 128].rearrange("b p -> p b"), self.rgout[:, fc, :],
                          allow_slow_non_contiguous=True)
                self.out_tiles.append(self.rgout.t)

    def _load_B(self, l):
        S, I = self.S, self.I
        S.dma_in("sp", self.ssp[:, 0:4], I["ssd_dt_bias"][l:l + 1, :].to_broadcast([128, 4]))
        S.dma_in("sp", self.ssp[:, 4:8], I["ssd_a_log"][l:l + 1, :].to_broadcast([128, 4]))
        S.act(self.ssp[:, 8:12], self.ssp[:, 4:8], AF.Exp)
        S.ts("dve", self.ssp[:, 8:12], self.ssp[:, 8:12], -1.0, ALU.mult)
        for h in range(4):
            h2, fc = h % 2, h // 2
            S.dma_in("sp", self.ssD[64 * h2:64 * h2 + 64, fc:fc + 1], I["ssd_d"][l:l + 1, h:h + 1].to_broadcast([64, 1]))
        S.dma_in("sp", self.ssnw.v(), I["ssd_norm_w"][l].rearrange("(c p) -> p c", p=128), allow_slow_non_contiguous=True)

    def _load_B_sample(self, l):
        S, I = self.S, self.I
        for h in range(4):
            g, h2 = h // 2, h % 2
            if h2 == 0:
                for hh in range(2):
                    S.dma_in("sp", self.hnat(hh), I["state_ssd"][l][:, 2 * g + hh].rearrange("b p n -> p b n"))
            for bb in range(2):
                p = self.ps()
                for j in range(8):
                    S.tr(p[0:64, j * 64:(j + 1) * 64], self.hnat(h2)[:, bb * 8 + j, :], self.identf[0:64, 0:64])
                S.copy("act" if bb else "dve", self.H0T[64 * g:64 * g + 64, bb * 8:(bb + 1) * 8, h2, :],
                       p[0:64, :].rr("n (b p) -> n b p", p=64))

    def chunk_decay(self, blk, ti, dta, sc, pAC, pACm, dsplit=None, dres=None):
        S = self.S
        mk = "s" if blk.sample else "p"
        dsplit = dsplit or self.dsplit
        dres = dres or self.dres
        self.split3(dta, 4, dsplit, dres)
        p = self.ps()
        for i in range(3):
            S.mm(p[:, 0:4], self.Umb[mk].v(), dsplit[:, i, 0:4], start=(i == 0), stop=(i == 2))
        for i in range(3):
            S.mm(p[:, 4:8], self.SGb[mk].v(), dsplit[:, i, 0:4], start=(i == 0), stop=(i == 2))
        S.copy("dve", sc[:, 0:8], p[:, 0:8])
        S.ts("dve", sc[:, 8:12], sc[:, 0:4], -1.0, ALU.mult)
        for h in range(4):
            hs = slice(h * 128, (h + 1) * 128)
            for i in range(3):
                S.mm(pAC[:, hs], dsplit[:, i, h:h + 1].bc([128, 128]), self.Umb[mk].v(), start=(i == 0), stop=(i == 2))

    def split3(self, src, w, dsplit, dres):
        S = self.S
        S.copy("dve", dsplit[:, 0, 0:w], src)
        S.tt("dve", dres[:, 0:w], src, dsplit[:, 0, 0:w], ALU.subtract)
        S.copy("dve", dsplit[:, 1, 0:w], dres[:, 0:w])
        S.tt("dve", dres[:, 0:w], dres[:, 0:w], dsplit[:, 1, 0:w], ALU.subtract)
        S.copy("dve", dsplit[:, 2, 0:w], dres[:, 0:w])

    def _mixer_B(self, l, blk):
        S = self.S
        N = blk.N
        xsb = [self.btmp[0], self.btmp[1]]
        BT, CT = self.btmp[2], self.btmp[3]
        dt = self.ftmp[0]
        dta = self.ftmp[1]
        nt = blk.ntile
        dt3 = dt[:, 0:nt * 4].rr("p (t h) -> p t h", h=4)
        dta3 = dta[:, 0:nt * 4].rr("p (t h) -> p t h", h=4)
        for ti in range(nt):
            S.tt("dve", dt3[:, ti, :], self.smtok[:, ti, 0:4], self.ssp[:, 0:4], ALU.add)
        S.act(dt[:, 0:nt * 4], dt[:, 0:nt * 4], AF.Exp)
        S.act(dt[:, 0:nt * 4], dt[:, 0:nt * 4], AF.Ln, bias=1.0)
        for ti in range(nt):
            S.tt("dve", dta3[:, ti, :], dt3[:, ti, :], self.ssp[:, 8:12], ALU.mult)
        yb = [self.ftmp[2], self.ftmp[3]]
        if blk.first:
            S.memset("pool", self.HTm.v(), 0.0)
            S.memset("pool", self.HTb.v(), 0.0)
        if blk.sample:
            self._load_B_sample(l)
        for ti in range(nt):
            tsl = slice(ti * 128, (ti + 1) * 128)
            sc = self.sc[ti % 2]
            for i, srcb in enumerate((xsb[0], xsb[1], BT)):
                S.tr(self.pst[:, i * 128:(i + 1) * 128], srcb[:, tsl], self.ident.v())
            S.copy("act", self.tok[:, 0:384], self.pst[:, 0:384])
            yield
            pAC, pACm = self.ps(), None
            self.chunk_decay(blk, ti, dta3[:, ti, :], sc, pAC, pACm)
            S.tt("dve", sc[:, 12:16], sc[:, 4:8], sc[:, 0:4], ALU.subtract)
            S.act(sc[:, 12:16], sc[:, 12:16], AF.Exp)
            S.tt("dve", sc[:, 12:16], sc[:, 12:16], dt3[:, ti, :], ALU.mult)
            S.act(sc[:, 16:20], sc[:, 4:8], AF.Exp)
            Edec, LT = self.bw
            S.act(Edec.v(), pAC.v(), AF.Exp)
            for h in range(4):
                hs = slice(h * 128, (h + 1) * 128)
                S.stt(LT[:, hs], pAC[:, hs], sc[:, 8 + h:9 + h], self.NEGM["s" if blk.sample else "p"].v(), ALU.add, ALU.add)
            S.act(LT.v(), LT.v(), AF.Exp)
            yield
            pG = [self.ps(), self.ps()]
            for g in range(2):
                S.mm(pG[g][:, 0:128], BT[64 * g:64 * g + 64, tsl], CT[64 * g:64 * g + 64, tsl])
            for h in range(4):
                hs = slice(h * 128, (h + 1) * 128)
                S.stt(self.Wt[:, h, :], LT[:, hs], dt3[:, ti, h:h + 1], pG[h // 2][:, 0:128], ALU.mult, ALU.mult)
            for g in range(2):
                gs = slice(64 * g, 64 * g + 64)
                S.tt("dve", self.Cdec[gs, 2 * g:2 * g + 2, :], Edec[gs, :].rr("p (h t) -> p h t", h=4)[:, 2 * g:2 * g + 2, :],
                     CT[gs, tsl].rr("p (o t) -> p o t", o=1).bc([64, 2, 128]), ALU.mult)
            for h in range(4):
                S.ts("dve", self.Bdec[:, h, :], self.tok[:, 256 + (h // 2) * 64:256 + (h // 2) * 64 + 64], sc[:, 12 + h:13 + h], ALU.mult)
            yield
            py = [self.ps(), self.ps()]
            for h in range(4):
                g, h2, fc = h // 2, h % 2, h // 2
                gs = slice(64 * g, 64 * g + 64)
                out = py[fc][64 * h2:64 * h2 + 64, 0:128]
                if blk.sample:
                    S.mm(out, self.tok[:, h * 64:(h + 1) * 64], self.Wt[:, h, :], start=True, stop=False)
                    for b in range(16):
                        S.mm(py[fc][64 * h2:64 * h2 + 64, 8 * b:8 * b + 8], self.H0T[gs, b, h2, :], self.Cdec[gs, h, 8 * b:8 * b + 8],
                             start=False, stop=(b == 15))
                else:
                    nostate = blk.first and ti == 0
                    S.mm(out, self.tok[:, h * 64:(h + 1) * 64], self.Wt[:, h, :], start=True, stop=nostate)
                    if not nostate:
                        S.mm(out, self.HTb[gs, h2, :], self.Cdec[gs, h, :], start=False, stop=True)
            for fc in range(2):
                S.stt(yb[fc][:, tsl], xsb[fc][:, tsl], self.ssD[:, fc:fc + 1], py[fc][:, 0:128], ALU.mult, ALU.add)
            if blk.sample:
                yield
                pE = self.ps()
                for h in range(4):
                    for i in range(3):
                        S.mm(pE[0:64, h * 16:(h + 1) * 16], self.dsplit[:, i, h:h + 1].bc([128, 64]), self.SeqSelb.v(),
                             start=(i == 0), stop=(i == 2))
                S.act(self.etotb.v().rr("p h b -> p (h b)"), pE[0:64, 0:64], AF.Exp)
                for h in range(4):
                    if h % 2 == 0:
                        for hh in range(2):
                            S.dma_in("sp", self.hnat(hh), self.I["state_ssd"][l][:, h + hh].rearrange("b p n -> p b n"))
                    for hf in range(2):
                        S.tt("dve", self.bexp(hf), self.Bdec[:, h, :].rr("p (o n) -> p o n", o=1).bc([128, 8, 64]),
                             self.SeqSelb[:, hf * 8:(hf + 1) * 8].rr("p (b o) -> p b o", o=1).bc([128, 8, 64]), ALU.mult)
                    yield
                    pn = [self.ps(), self.ps()]
                    for hf in range(2):
                        S.mm(pn[hf][0:64, :], self.tok[:, h * 64:(h + 1) * 64],
                             self.bexp(hf).rr("p b n -> p (b n)"))
                    for hf in range(2):
                        bs = slice(hf * 8, hf * 8 + 8)
                        S.tt("dve", self.hnat(h % 2)[:, bs, :], self.hnat(h % 2)[:, bs, :],
                             self.etotb[:, h, bs].rr("p (b o) -> p b o", o=1).bc([64, 8, 64]), ALU.mult)
                        S.tt("dve", self.hnat(h % 2)[:, bs, :], self.hnat(h % 2)[:, bs, :], pn[hf][0:64, :].rr("p (b n) -> p b n", n=64), ALU.add)
                    if h % 2 == 1:
                        for hh in range(2):
                            S.dma_out("sp", self.O["s_ssd"][l][:, h - 1 + hh].rearrange("b p n -> p b n"), self.hnat(hh))
                self.out_tiles.extend([self.xrt[1].t, self.ybuf[0].t])
            else:
                yield
                pH = self.ps()
                for h in range(4):
                    g, h2 = h // 2, h % 2
                    gs = slice(64 * g, 64 * g + 64)
                    S.mm(pH[gs, h2 * 64:(h2 + 1) * 64], self.Bdec[:, h, :], self.tok[:, h * 64:(h + 1) * 64])
                for h in range(4):
                    g, h2 = h // 2, h % 2
                    gs = slice(64 * g, 64 * g + 64)
                    S.stt(self.HTm[gs, h2, :], self.HTm[gs, h2, :], sc[gs, 16 + h:17 + h], pH[gs, h2 * 64:(h2 + 1) * 64], ALU.mult, ALU.add)
                S.copy("dve", self.HTb.v(), self.HTm.v())
        if blk.last:
            for h in range(4):
                g, h2 = h // 2, h % 2
                gs = slice(64 * g, 64 * g + 64)
                yield
                p = self.ps()
                S.tr(p[0:64, 0:64], self.HTm[gs, h2, :], self.identf[gs, gs])
                S.copy("act", self.ftmp[9][0:64, h * 64:(h + 1) * 64], p[0:64, 0:64])
            S.dma_out("sp", self.O["p_ssd"][l, 0].rearrange("h p n -> p h n"), self.ftmp[9][0:64, 0:256].rr("p (h n) -> p h n", h=4))
            self.out_tiles.append(self.ftmp[9].t)
        yg = [self.ftmp[6], self.ftmp[7]]
        for fc in range(2):
            yield
            pg = self.ps()
            self.zmm(blk, C_BG + fc * 128, 128, pg[:, 0:N])
            sg = self.ftmp[8]
            S.act(sg[:, 0:N], pg[:, 0:N], AF.Silu)
            S.tt("dve", yg[fc][:, 0:N], yb[fc][:, 0:N], sg[:, 0:N], ALU.mult)
            sq = self.btmp[4 + fc]
            S.act(sq[:, 0:N], yg[fc][:, 0:N], AF.Square)
        yield
        pss = self.ps()
        for fc in range(2):
            S.mm(pss[:, 0:N], self.onesb.v(), self.btmp[4 + fc][:, 0:N], start=(fc == 0), stop=(fc == 1))
        rstd = self.ftmp[9]
        self.rsqrt(rstd[:, 0:N], pss[:, 0:N], 1.0 / 256.0, self.eps6())
        for fc in range(2):
            S.stt(self.mixT[2 + fc][:, 0:N], yg[fc][:, 0:N], self.ssnw[:, fc:fc + 1], rstd[:, 0:N], ALU.mult, ALU.mult)

    def rsqrt(self, out, in_, scale, eps):
        self.S.act(out, in_, AF.Ln, bias=eps, scale=scale)
        self.S.act(out, out, AF.Exp, scale=-0.5)

    def eps6(self):
        if not hasattr(self, "_eps6"):
            self._eps6 = self.sb("eps6", [128, 1])
            self.S.memset("pool", self._eps6.v(), 1e-6)
        return self._eps6.v()

    def _load_C(self, l):
        S, I = self.S, self.I
        S.dma_in("sp", self.gdp[:, 0:4], I["gdn_dt_bias"][l:l + 1, :].to_broadcast([128, 4]))
        S.dma_in("sp", self.gdp[:, 4:8], I["gdn_a_log"][l:l + 1, :].to_broadcast([128, 4]))
        S.act(self.gdp[:, 8:12], self.gdp[:, 4:8], AF.Exp)
        S.ts("dve", self.gdp[:, 8:12], self.gdp[:, 8:12], -1.0, ALU.mult)
        for h2 in range(2):
            S.dma_in("sp", self.gdnw[64 * h2:64 * h2 + 64, :], I["gdn_norm_w"][l].rearrange("(p o) -> p o", o=1))

    def _load_C_sample(self, l):
        S, I = self.S, self.I
        for fc in range(2):
            for hh in range(2):
                S.dma_in("sp", self.hnat(hh), I["state_gdn"][l][:, 2 * fc + hh].rearrange("b k v -> k b v"))
            for hh in range(2):
                S.copy("act" if hh else "dve", self.H0T[64 * hh:64 * hh + 64, :, fc, :], self.hnat(hh))

    def _mixer_C(self, l, blk):
        S = self.S
        N = blk.N
        nt = blk.ntile
        mk = "s" if blk.sample else "p"
        nlev = 3 if blk.sample else 7
        qkv = self.cq
        for i in range(4):
            sq = self.gsq
            S.act(sq[:, 0:N], qkv[i][:, 0:N], AF.Square)
            yield
            pss = self.ps()
            S.mm(pss[:, 0:N], self.blockones.v(), sq[:, 0:N])
            rn = self.cf[2]
            self.rsqrt(rn[:, 0:N], pss[:, 0:N], 1.0, self.eps6())
            if i < 2:
                S.stt(qkv[i][:, 0:N], qkv[i][:, 0:N], 0.125, rn[:, 0:N], ALU.mult, ALU.mult)
            else:
                S.tt("dve", qkv[i][:, 0:N], qkv[i][:, 0:N], rn[:, 0:N], ALU.mult)
        bet, gl = self.csm[:, 0:16], self.csm[:, 16:32]
        bet3 = bet[:, 0:nt * 4].rr("p (t h) -> p t h", h=4)
        g3 = gl[:, 0:nt * 4].rr("p (t h) -> p t h", h=4)
        for ti in range(nt):
            S.act(bet3[:, ti, :], self.smtok[:, ti, 4:8], AF.Sigmoid)
            S.tt("dve", g3[:, ti, :], self.smtok[:, ti, 8:12], self.gdp[:, 0:4], ALU.add)
        S.act(gl[:, 0:nt * 4], gl[:, 0:nt * 4], AF.Exp)
        S.act(gl[:, 0:nt * 4], gl[:, 0:nt * 4], AF.Ln, bias=1.0)
        for ti in range(nt):
            S.tt("dve", g3[:, ti, :], g3[:, ti, :], self.gdp[:, 8:12], ALU.mult)
        oT = [self.cf[0], self.cf[1]]
        if blk.first:
            S.memset("pool", self.Sm.v(), 0.0)
            S.memset("pool", self.Sb.v(), 0.0)
        H4 = [(h, h // 2, h % 2, slice(64 * (h % 2), 64 * (h % 2) + 64), slice(h * 128, (h + 1) * 128)) for h in range(4)]
        for ti in range(nt):
            tsl = slice(ti * 128, (ti + 1) * 128)
            sc = self.csc[ti % 2]
            for i, srcb in enumerate((qkv[2], qkv[3], qkv[4], qkv[5])):
                S.tr(self.pst[:, i * 128:(i + 1) * 128], srcb[:, tsl], self.ident.v())
            S.copy("act", self.ctok[:, 0:512], self.pst[:, 0:512])
            yield
            pAC = self.ps()
            self.chunk_decay(blk, ti, g3[:, ti, :], sc, pAC, None, self.cdsplit, self.cdres)
            S.act(sc[:, 12:16], sc[:, 0:4], AF.Exp)
            S.tt("dve", sc[:, 16:20], sc[:, 12:16], bet3[:, ti, :], ALU.mult)
            S.tt("dve", sc[:, 20:24], sc[:, 4:8], sc[:, 0:4], ALU.subtract)
            S.act(sc[:, 20:24], sc[:, 20:24], AF.Exp)
            S.act(sc[:, 24:28], sc[:, 4:8], AF.Exp)
            S.ts("dve", sc[:, 28:32], bet3[:, ti, :], -1.0, ALU.mult)
            Edec, GT, gam = self.wtmp[0], self.wtmp[1], self.wtmp[2]
            S.act(Edec.v(), pAC.v(), AF.Exp)
            for (h, fc, h2, hb, hs) in H4:
                S.stt(GT[:, hs], pAC[:, hs], sc[:, 8 + h:9 + h], self.NEGM[mk].v(), ALU.add, ALU.add)
                S.stt(gam[:, hs], pAC[:, hs], sc[:, h:h + 1], self.POSM[mk].v(), ALU.subtract, ALU.add)
            S.act(GT.v(), GT.v(), AF.Exp)
            S.act(gam.v(), gam.v(), AF.Exp, scale=-1.0)
            yield
            pKK = [self.ps(), self.ps()]
            for (h, fc, h2, hb, hs) in H4:
                cs = slice(fc * 128, (fc + 1) * 128)
                S.mm(pKK[h2][:, cs], qkv[2 + fc][hb, tsl], qkv[2 + fc][hb, tsl])
            M0 = self.gM[2]
            for (h, fc, h2, hb, hs) in H4:
                cs = slice(fc * 128, (fc + 1) * 128)
                S.stt(M0[:, h, :], gam[:, hs], sc[:, 28 + h:29 + h], pKK[h2][:, cs], ALU.mult, ALU.mult)
            yield
            pQK = [self.ps(), self.ps()]
            for (h, fc, h2, hb, hs) in H4:
                cs = slice(fc * 128, (fc + 1) * 128)
                S.mm(pQK[h2][:, cs], qkv[2 + fc][hb, tsl], qkv[fc][hb, tsl])
            for (h, fc, h2, hb, hs) in H4:
                cs = slice(fc * 128, (fc + 1) * 128)
                S.tt("dve", self.QKG[:, h, :], GT[:, hs], pQK[h2][:, cs], ALU.mult)
            Md, Nd, Z = self.gM[0], self.gN[0], self.gX[0]
            d32 = self.D32b.v().rr("p (o t) -> p o t", o=1).bc([128, 4, 128])
            idb = self.ident.v().rr("p (o t) -> p o t", o=1).bc([128, 4, 128])
            S.tt("dve", Md.v(), M0.v(), d32, ALU.mult)
            S.tt("dve", self.Moff.v(), M0.v(), Md.v(), ALU.subtract)
            for (h, fc, h2, hb, hs) in H4:
                S.tr(self.pst[:, hs], Md[:, h, :], self.ident.v())
            S.copy("act", Nd.v().rr("p h t -> p (h t)"), self.pst[:, 0:512])
            S.tt("dve", Z.v(), self.pst[:, 0:512].rr("p (h t) -> p h t", h=4), idb, ALU.add)
            R = self.gR
            for (h, fc, h2, hb, hs) in H4:
                kc = slice(64 * h2, 64 * h2 + 64)
                vc = slice(64 * (1 - h2), 64 * (1 - h2) + 64)
                S.ts("dve", R[:, h, kc], self.ctok[:, h * 64:(h + 1) * 64], sc[:, 16 + h:17 + h], ALU.mult)
                S.ts("dve", R[:, h, vc], self.ctok[:, 256 + h * 64:256 + (h + 1) * 64], bet3[:, ti, h:h + 1], ALU.mult)
            nsq = 2 if blk.sample else 4
            for j in range(nsq):
                Mn, Nn2, Zn = self.gM[(j + 1) % 2], self.gN[(j + 1) % 2], self.gX[(j + 1) % 2]
                yield
                pM = self.ps()
                for (h, fc, h2, hb, hs) in H4:
                    S.mm(pM[:, hs], Nd[:, h, :], Md[:, h, :])
                S.copy("act", Mn.v().rr("p h t -> p (h t)"), pM.v())
                if j < nsq - 1:
                    yield
                    pN = self.ps()
                    for (h, fc, h2, hb, hs) in H4:
                        S.mm(pN[:, hs], Md[:, h, :], Nd[:, h, :])
                    S.copy("act", Nn2.v().rr("p h t -> p (h t)"), pN.v())
                yield
                pZ = self.ps()
                for (h, fc, h2, hb, hs) in H4:
                    S.mm(pZ[:, hs], Mn[:, h, :], Z[:, h, :])
                S.tt("dve", Zn.v().rr("p h t -> p (h t)"), pZ.v(), Z.v().rr("p h t -> p (h t)"), ALU.add)
                Md, Nd, Z = Mn, Nn2, Zn
            TdT = Z
            pXT, pV = None, None
            if blk.sample:
                yield
                pXT, pV = self.ps(), self.ps()
                for (h, fc, h2, hb, hs) in H4:
                    vc = slice(64 * (1 - h2), 64 * (1 - h2) + 64)
                    S.mm(pXT[:, hs], R[:, h, :], TdT[:, h, :])
                    S.mm(pV[:, h * 64:(h + 1) * 64], TdT[:, h, :], R[:, h, vc])
            else:
                yield
                pG = self.ps()
                for (h, fc, h2, hb, hs) in H4:
                    S.mm(pG[:, hs], self.Moff[:, h, :], TdT[:, h, :])
                S.copy("act", self.nGT.v().rr("p h t -> p (h t)"), pG.v())
                W = gC = self.gM[2]
                yield
                pW0 = self.ps()
                for (h, fc, h2, hb, hs) in H4:
                    S.mm(pW0[:, hs], TdT[:, h, :], R[:, h, :])
                S.copy("act", gC.v().rr("p h t -> p (h t)"), pW0.v())
                for it in range(2):
                    Wn = self.gW[it]
                    yield
                    pWi = self.ps()
                    for (h, fc, h2, hb, hs) in H4:
                        S.mm(pWi[:, hs], self.nGT[:, h, :], W[:, h, :])
                    S.tt("dve", Wn.v().rr("p h t -> p (h t)"), pWi.v(), gC.v().rr("p h t -> p (h t)"), ALU.add)
                    W = Wn
                yield
                pXT, pV = self.ps(), self.ps()
                for (h, fc, h2, hb, hs) in H4:
                    vc = slice(64 * (1 - h2), 64 * (1 - h2) + 64)
                    S.mm(pXT[:, hs], R[:, h, :], TdT[:, h, :], start=True, stop=False)
                    S.mm(pXT[:, hs], W[:, h, :], self.nGT[:, h, :], start=False, stop=True)
                    S.mm(pV[:, h * 64:(h + 1) * 64], TdT[:, h, :], R[:, h, vc], start=True, stop=False)
                    S.mm(pV[:, h * 64:(h + 1) * 64], self.nGT[:, h, :], W[:, h, vc], start=False, stop=True)
            S.copy("act", self.XTb.v().rr("p h t -> p (h t)"), pXT.v())
            S.copy("act", self.Val.v(), pV[:, 0:256])
            for (h, fc, h2, hb, hs) in H4:
                S.tt("dve", self.qdec[hb, h, :], qkv[fc][hb, tsl], Edec[hb, hs], ALU.mult)
                S.ts("dve", self.kdec[:, h, :], self.ctok[:, h * 64:(h + 1) * 64], sc[:, 20 + h:21 + h], ALU.mult)
            if not blk.sample:
                nostate = blk.first and ti == 0
                yield
                pW = [self.ps(), self.ps()]
                for (h, fc, h2, hb, hs) in H4:
                    if nostate:
                        S.copy("dve", self.wtok[:, h, :], self.Val[:, h * 64:(h + 1) * 64])
                    else:
                        S.mm(pW[h2][:, fc * 64:(fc + 1) * 64], self.XTb[hb, h, :], self.Sb[hb, fc, :])
                if not nostate:
                    for (h, fc, h2, hb, hs) in H4:
                        S.tt("dve", self.wtok[:, h, :], self.Val[:, h * 64:(h + 1) * 64], pW[h2][:, fc * 64:(fc + 1) * 64], ALU.subtract)
                yield
                po = [self.ps(), self.ps()]
                for (h, fc, h2, hb, hs) in H4:
                    out = po[h2][hb, fc * 128:(fc + 1) * 128]
                    S.mm(out, self.wtok[:, h, :], self.QKG[:, h, :], start=True, stop=nostate)
                    if not nostate:
                        S.mm(out, self.Sb[hb, fc, :], self.qdec[hb, h, :], start=False, stop=True)
                for fc in range(2):
                    S.copy("act", oT[fc][0:64, tsl], po[0][0:64, fc * 128:(fc + 1) * 128])
                    S.copy("act", oT[fc][64:128, tsl], po[1][64:128, fc * 128:(fc + 1) * 128])
                yield
                pS = self.ps()
                for (h, fc, h2, hb, hs) in H4:
                    S.mm(pS[hb, fc * 64:(fc + 1) * 64], self.kdec[:, h, :], self.wtok[:, h, :])
                for (h, fc, h2, hb, hs) in H4:
                    S.stt(self.Sm[hb, fc, :], self.Sm[hb, fc, :], sc[hb, 24 + h:25 + h], pS[hb, fc * 64:(fc + 1) * 64], ALU.mult, ALU.add)
                S.copy("dve", self.Sb.v(), self.Sm.v())
            else:
                self._load_C_sample(l)
                yield
                pWT = [self.ps(), self.ps()]
                for (h, fc, h2, hb, hs) in H4:
                    ob = slice(64 * (1 - h2), 64 * (1 - h2) + 64)
                    for b in range(16):
                        S.mm(pWT[h2][ob, fc * 128 + 8 * b:fc * 128 + 8 * b + 8], self.H0T[hb, b, fc, :], self.XTb[hb, h, 8 * b:8 * b + 8])
                for (h, fc, h2, hb, hs) in H4:
                    ob = slice(64 * (1 - h2), 64 * (1 - h2) + 64)
                    S.tt("dve", self.gW[1][ob, h, :], self.XTb[ob, h, :], pWT[h2][ob, fc * 128:(fc + 1) * 128], ALU.subtract)
                for par in range(2):
                    ob = slice(64 * (1 - par), 64 * (1 - par) + 64)
                    for h in (par, par + 2):
                        S.tr(self.pst[:, h * 64:(h + 1) * 64], self.gW[1][ob, h, :], self.ident[ob, ob])
                    for h in (par, par + 2):
                        S.copy("act", self.wtok[:, h, :], self.pst[:, h * 64:(h + 1) * 64])
                yield
                po = [self.ps(), self.ps()]
                for (h, fc, h2, hb, hs) in H4:
                    S.mm(po[h2][hb, fc * 128:(fc + 1) * 128], self.wtok[:, h, :], self.QKG[:, h, :], start=True, stop=False)
                    for b in range(16):
                        S.mm(po[h2][hb, fc * 128 + 8 * b:fc * 128 + 8 * b + 8], self.H0T[hb, b, fc, :], self.qdec[hb, h, 8 * b:8 * b + 8],
                             start=False, stop=(b == 15))
                for fc in range(2):
                    S.copy("act", oT[fc][0:64, tsl], po[0][0:64, fc * 128:(fc + 1) * 128])
                    S.copy("act", oT[fc][64:128, tsl], po[1][64:128, fc * 128:(fc + 1) * 128])
                yield
                pE = self.ps()
                for h in range(4):
                    for i in range(3):
                        S.mm(pE[0:64, h * 16:(h + 1) * 16], self.cdsplit[:, i, h:h + 1].bc([128, 64]), self.SeqSelb.v(),
                             start=(i == 0), stop=(i == 2))
                S.act(self.etotb.v().rr("p h b -> p (h b)"), pE[0:64, 0:64], AF.Exp)
                for h in range(4):
                    if h % 2 == 0:
                        for hh in range(2):
                            S.dma_in("sp", self.hnat(hh), self.I["state_gdn"][l][:, h + hh].rearrange("b k v -> k b v"))
                    for hf in range(2):
                        S.tt("dve", self.bexp(hf), self.wtok[:, h, :].rr("p (o n) -> p o n", o=1).bc([128, 8, 64]),
                             self.SeqSelb[:, hf * 8:(hf + 1) * 8].rr("p (b o) -> p b o", o=1).bc([128, 8, 64]), ALU.mult)
                    yield
                    pn = [self.ps(), self.ps()]
                    for hf in range(2):
                        S.mm(pn[hf][0:64, :], self.kdec[:, h, :], self.bexp(hf).rr("p b n -> p (b n)"))
                    for hf in range(2):
                        bs = slice(hf * 8, hf * 8 + 8)
                        S.tt("dve", self.hnat(h % 2)[:, bs, :], self.hnat(h % 2)[:, bs, :],
                             self.etotb[:, h, bs].rr("p (b o) -> p b o", o=1).bc([64, 8, 64]), ALU.mult)
                        S.tt("dve", self.hnat(h % 2)[:, bs, :], self.hnat(h % 2)[:, bs, :], pn[hf][0:64, :].rr("p (b n) -> p b n", n=64), ALU.add)
                    if h % 2 == 1:
                        for hh in range(2):
                            S.dma_out("sp", self.O["s_gdn"][l][:, h - 1 + hh].rearrange("b k v -> k b v"), self.hnat(hh))
                self.out_tiles.extend([self.xrt[1].t, self.ybuf[0].t])
        if blk.last:
            for (h, fc, h2, hb, hs) in H4:
                S.dma_out("sp", self.O["p_gdn"][l, 0, h], self.Sm[hb, fc, :])
            self.out_tiles.append(self.Sm.t)
        for fc in range(2):
            sq = self.gsq
            S.act(sq[:, 0:N], oT[fc][:, 0:N], AF.Square)
            yield
            pss = self.ps()
            S.mm(pss[:, 0:N], self.blockones.v(), sq[:, 0:N])
            rs = self.cf[2]
            self.rsqrt(rs[:, 0:N], pss[:, 0:N], 1.0 / 64.0, self.eps6())
            yield
            pg = self.ps()
            self.zmm(blk, C_CG + fc * 128, 128, pg[:, 0:N])
            sg = self.cf[3]
            S.act(sg[:, 0:N], pg[:, 0:N], AF.Silu)
            S.stt(oT[fc][:, 0:N], oT[fc][:, 0:N], self.gdnw[:, 0:1], rs[:, 0:N], ALU.mult, ALU.mult)
            S.tt("dve", self.mixT[4 + fc][:, 0:N], oT[fc][:, 0:N], sg[:, 0:N], ALU.mult)

    def sin_rr(self, dst, x, scr, iscr):
        S = self.S
        PI = math.pi
        self.reduce_pi(dst, x, scr, iscr)
        S.act(dst, dst, AF.Sin)

    def reduce_pi(self, dst, x, scr, iscr):
        S = self.S
        PI = math.pi
        S.ts("dve", scr, x, 1.0 / (2 * PI), ALU.mult)
        S.copy("dve", iscr, scr)
        S.copy("dve", scr, iscr)
        S.stt(dst, scr, -2 * PI, x, ALU.mult, ALU.add)
        S.ts("dve", scr, dst, PI, ALU.is_gt)
        S.stt(dst, scr, -2 * PI, dst, ALU.mult, ALU.add)
        S.ts("dve", scr, dst, -PI, ALU.is_lt)
        S.stt(dst, scr, 2 * PI, dst, ALU.mult, ALU.add)
        S.ts("dve", dst, dst, PI, ALU.min, -PI, ALU.max)

    def _load_D(self, l):
        S, I, nc = self.S, self.I, self.nc
        PI = math.pi
        par, sm = self.s5par, self.s5sm
        lam = lambda nm: I[nm][l].rearrange("(k g2) n -> (g2 n) k", g2=2)
        S.dma_in("sp", par[:, :, 0], lam("s5_lambda_re"), allow_slow_non_contiguous=True)
        S.dma_in("sp", par[:, :, 1], lam("s5_lambda_im"), allow_slow_non_contiguous=True)
        ldt = I["s5_log_dt"][l].rearrange("(k g2) -> g2 k", g2=2)
        for g2 in range(2):
            S.dma_in("sp", par[64 * g2:64 * g2 + 64, :, 2], ldt[g2:g2 + 1, :].to_broadcast([64, 8]))
        S.act(par[:, :, 2], par[:, :, 2], AF.Exp)
        S.tt("dve", par[:, :, 3], par[:, :, 0], par[:, :, 2], ALU.mult)
        S.tt("dve", par[:, :, 4], par[:, :, 1], par[:, :, 2], ALU.mult)
        S.act(par[:, :, 5], par[:, :, 3], AF.Exp, scale=8.0)
        w0, w1, w2 = self.wtmp
        v3 = lambda v: v.rr("p (k q) -> p k q", q=32)
        A1, A2 = w0[:, 0:256], w0[:, 256:512]
        MG, PWc = w1[:, 0:256], w1[:, 256:512]
        PWs, SC = w2[:, 0:256], w2[:, 256:512]
        col = lambda j: par[:, :, j].rr("p (k o) -> p k o", o=1).bc([128, 8, 32])
        qrow = self.s5q[:, 0:32].rr("p (o q) -> p o q", o=1).bc([128, 8, 32])
        S.tt("dve", v3(A1), col(4), qrow, ALU.mult)
        S.tt("dve", v3(MG), col(3), qrow, ALU.mult)
        S.act(MG, MG, AF.Exp)
        self.sin_rr(PWs, A1, SC, self.itmp.v())
        S.ts("dve", A2, A1, PI / 2, ALU.add)
        self.sin_rr(PWc, A2, SC, self.itmp.v())
        S.tt("dve", PWc, PWc, MG, ALU.mult)
        S.tt("dve", PWs, PWs, MG, ALU.mult)
        PWre, PWim = v3(PWc), v3(PWs)
        S.copy("dve", par[:, :, 6], PWre[:, :, 15])
        S.copy("dve", par[:, :, 7], PWim[:, :, 15])
        lr, li = par[:, :, 0], par[:, :, 1]
        t0, t1, t2, t3, fre, fim = (sm[:, :, j] for j in range(6))
        abim = PWim[:, :, 8]
        S.ts("dve", t0, PWre[:, :, 8], -1.0, ALU.add)
        S.tt("dve", t1, lr, lr, ALU.mult)
        S.tt("dve", t2, li, li, ALU.mult)
        S.tt("dve", t1, t1, t2, ALU.add)
        S.recip(t1, t1)
        S.tt("dve", t2, t0, lr, ALU.mult)
        S.tt("dve", t3, abim, li, ALU.mult)
        S.tt("dve", t2, t2, t3, ALU.add)
        S.tt("dve", fre, t2, t1, ALU.mult)
        S.tt("dve", t2, abim, lr, ALU.mult)
        S.tt("dve", t3, t0, li, ALU.mult)
        S.tt("dve", t2, t2, t3, ALU.subtract)
        S.tt("dve", fim, t2, t1, ALU.mult)
        f0, f1, f2, f3 = (self.ftmp[j][:, 0:128].rr("p (k i) -> p k i", i=16) for j in range(4))
        bsrc = lambda nm: I[nm][l].rearrange("(k g2) n i -> (g2 n) k i", g2=2)
        S.dma_in("sp", f0, bsrc("s5_b_re"))
        S.dma_in("sp", f1, bsrc("s5_b_im"))
        fb = lambda f: f.rr("p (k o) -> p k o", o=1).bc([128, 8, 16])
        s5bb = self.cf[2].v().rr("p (a k i) -> p a k i", a=2, k=8)
        s5c = self.cf[3].v().rr("p (a k i) -> p a k i", a=2, k=8)
        s5cb = [self.cq[0][:, 0:128], self.cq[0][:, 128:256], self.cq[1][:, 0:128], self.cq[1][:, 128:256]]
        bbre, bbim = s5bb[:, 0], s5bb[:, 1]
        S.tt("dve", f2, f0, fb(fre), ALU.mult)
        S.tt("dve", f3, f1, fb(fim), ALU.mult)
        S.tt("dve", bbre, f2, f3, ALU.subtract)
        S.tt("dve", f2, f1, fb(fre), ALU.mult)
        S.tt("dve", f3, f0, fb(fim), ALU.mult)
        S.tt("dve", bbim, f2, f3, ALU.add)
        for part, nm in enumerate(("s5_c_re", "s5_c_im")):
            cn = self.ftmp[4][:, 0:128].rr("p (c n) -> p c n", c=2)
            S.dma_in("sp", cn, I[nm][l].rearrange("(c gl) i n -> (gl i) c n", c=2))
            p = self.ps()
            for c in range(2):
                S.tr(p[0:64, c * 128:(c + 1) * 128], cn[:, c, :], self.identf.v())
            pv = p[0:64, 0:256].rr("n (k g2 i) -> n k g2 i", g2=2, i=16)
            for g2 in range(2):
                S.copy("act" if g2 else "dve", s5c[64 * g2:64 * g2 + 64, part, :, :], pv[:, :, g2, :])
        cre, cim = s5c[:, 0], s5c[:, 1]
        dsrc = I["s5_d"][l].rearrange("(g i) -> i g", i=16)
        for sg in range(8):
            S.dma_in("sp", self.s5Dfold[16 * sg:16 * sg + 16, :], dsrc, allow_slow_non_contiguous=True)
        T = [self.ftmp[j][:, 0:128].rr("p (s i) -> p s i", i=16) for j in range(5, 9)]
        for k in range(8):
            def cplx(outre, outim, pre, pim, xre, xim, neg_im=False, pw_outer=True):
                a = lambda v: v.rr("p (s o) -> p s o", o=1).bc([128, 8, 16])
                b = lambda v: v.rr("p (o i) -> p o i", o=1).bc([128, 8, 16])
                S.tt("dve", T[0], a(pre), b(xre), ALU.mult)
                S.tt("dve", T[1], a(pim), b(xim), ALU.mult)
                S.tt("dve", outre, T[0], T[1], ALU.subtract)
                S.tt("dve", T[2], a(pre), b(xim), ALU.mult)
                S.tt("dve", T[3], a(pim), b(xre), ALU.mult)
                if neg_im:
                    S.stt(outim, T[2], -1.0, T[3], ALU.mult, ALU.subtract)
                else:
                    S.tt("dve", outim, T[2], T[3], ALU.add)
            cb3 = [c.rr("p (s i) -> p s i", i=16) for c in s5cb]
            cplx(cb3[0], cb3[1], PWre[:, k, 16:24], PWim[:, k, 16:24], bbre[:, k, :], bbim[:, k, :], neg_im=True)
            cplx(cb3[2], cb3[3], PWre[:, k, 24:32], PWim[:, k, 24:32], cre[:, k, :], cim[:, k, :])
            pT = [self.ps(), self.ps()]
            for g2 in range(2):
                hb = slice(64 * g2, 64 * g2 + 64)
                S.mm(pT[g2][:, 0:128], s5cb[0][hb, :], s5cb[2][hb, :], start=True, stop=False)
                S.mm(pT[g2][:, 0:128], s5cb[1][hb, :], s5cb[3][hb, :], start=False, stop=True)
            for g2 in range(2):
                g = 2 * k + g2
                tm = self.ftmp[9][:, 0:128]
                S.tt("dve", tm, pT[g2][:, 0:128], self.Cmask.v(), ALU.mult)
                S.stt(self.s5Toep[:, g, :], self.identf.v(), self.s5Dfold[:, g:g + 1], tm, ALU.mult, ALU.add)
            Rre = self.ftmp[9][:, 0:128].rr("p (s i) -> p s i", i=16)
            Rim = self.ftmp[9][:, 128:256].rr("p (s i) -> p s i", i=16)
            cplx(Rre, Rim, PWre[:, k, 0:8], PWim[:, k, 0:8], bbre[:, k, :], bbim[:, k, :])
            pR = self.ps()
            for part in range(2):
                S.tr(pR[:, part * 128:(part + 1) * 128], self.ftmp[9][:, part * 128:(part + 1) * 128], self.identf.v())
            for part in range(2):
                S.copy("act", self.s5RT[:, part, 2 * k:2 * k + 2, :], pR[:, part * 128:(part + 1) * 128].rr("p (g n) -> p g n", g=2))
            cplx(self.s5P[:, 0, k, :].rr("p (s i) -> p s i", i=16), self.s5P[:, 1, k, :].rr("p (s i) -> p s i", i=16),
                 PWre[:, k, 8:16], PWim[:, k, 8:16], cre[:, k, :], cim[:, k, :], neg_im=True)
        phi = sm[:, :, 6]
        S.ts("dve", t0, par[:, :, 4], 8.0, ALU.mult)
        self.reduce_pi(phi, t0, t1, self.itmp[:, 0:8])
        th = w0[:, 0:256].rr("p (k b) -> p k b", b=32)
        S.tt("dve", th, phi.rr("p (k o) -> p k o", o=1).bc([128, 8, 32]), self.s5q[:, 32:64].rr("p (o q) -> p o q", o=1).bc([128, 8, 32]),
             ALU.mult)
        self.sin_rr(self.s5E[:, 1, :, :].rr("p k b -> p (k b)"), w0[:, 0:256], SC, self.itmp.v())
        S.ts("dve", A2, w0[:, 0:256], PI / 2, ALU.add)
        self.sin_rr(self.s5E[:, 0, :, :].rr("p k b -> p (k b)"), A2, SC, self.itmp.v())
        S.dma_in("sp", w1.v().rr("p (c f) -> p c f", c=2), I["s5_glu_w"][l].rearrange("(c p) f -> p c f", p=128))
        S.copy("dve", self.s5glu.v().rr("p c f -> p (c f)"), w1.v())
        S.dma_in("sp", self.s5gb.v(), I["s5_glu_b"][l].rearrange("(c p) -> p c", p=128), allow_slow_non_contiguous=True)

    def _mixer_D(self, l, blk):
        S = self.S
        N = blk.N
        nb = N // 8
        for c, (c0, w) in enumerate(((0, 96), (96, 96), (192, 64))):
            yield
            p = self.ps()
            self.zmm(blk, C_DU + c0, w, p[0:w, 0:N])
            S.copy("act" if c % 2 else "dve", self.s5uT[c][0:w, 0:N], p[0:w, 0:N])
        for q in range(3):
            yield
            pFq = self.ps()
            ks = list(range(q, 8, 3))
            for si, k in enumerate(ks):
                uc = k // 3
                for g2 in range(2):
                    out = pFq[:, (si * 2 + g2) * nb:(si * 2 + g2 + 1) * nb]
                    for sg in range(8):
                        rhs = self.s5uT[uc][32 * q:32 * q + 32, 0:N].rr("p (b s) -> p b s", s=8)[:, :, sg]
                        S.mm(out, self.s5Sel[32 * q:32 * q + 32, g2 * 8 + sg, :], rhs, start=(sg == 0), stop=(sg == 7))
            for si, k in enumerate(ks):
                S.copy("act" if (q + si) % 2 else "dve", self.s5Uf[:, 2 * k:2 * k + 2, 0:nb],
                       pFq[:, si * 2 * nb:(si + 1) * 2 * nb].rr("p (g b) -> p g b", g=2))
        if blk.sample:
            stg = self.ybuf[1]
            s5H0 = self.df[4].v().rr("p (a k b) -> p a k b", a=2, k=8)
            for part, nm in ((0, "state_s5_re"), (1, "state_s5_im")):
                S.dma_in("sp", stg[0:16, :], self.I[nm][l].rearrange("b g n -> b (g n)"))
                yield
                p = self.ps()
                for k in range(8):
                    S.tr(p[:, k * 16:(k + 1) * 16], stg[0:16, k * 128:(k + 1) * 128], self.identf[0:16, 0:16])
                S.copy("dve", s5H0[:, part, :, :], p[:, 0:128].rr("p (k b) -> p k b", b=16))
                S.copy("act", self.s5Hst[:, part, :, 0:16], p[:, 0:128].rr("p (k b) -> p k b", b=16))
        yield
        pXr, pXi = self.ps(), self.ps()
        for g in range(16):
            k, g2 = g // 2, g % 2
            hb = slice(64 * g2, 64 * g2 + 64)
            S.mm(pXr[hb, k * nb:(k + 1) * nb], self.s5RT[:, 0, g, :], self.s5Uf[:, g, 0:nb])
            S.mm(pXi[hb, k * nb:(k + 1) * nb], self.s5RT[:, 1, g, :], self.s5Uf[:, g, 0:nb])
        Xr3 = pXr[:, 0:8 * nb].rr("p (k b) -> p k b", b=nb)
        Xi3 = pXi[:, 0:8 * nb].rr("p (k b) -> p k b", b=nb)
        F = [self.df[j][:, 0:8 * nb].rr("p (k b) -> p k b", b=nb) for j in range(6)]
        F = F + [F[2], F[3]]
        par = self.s5par
        if not blk.sample:
            Ec, Es = self.s5E[:, 0, :, 0:nb], self.s5E[:, 1, :, 0:nb]
            S.tt("dve", F[0], Xr3, Ec, ALU.mult)
            S.tt("dve", F[1], Xi3, Es, ALU.mult)
            S.tt("dve", F[2], F[0], F[1], ALU.add)
            S.tt("dve", F[0], Xi3, Ec, ALU.mult)
            S.tt("dve", F[1], Xr3, Es, ALU.mult)
            S.tt("dve", F[3], F[0], F[1], ALU.subtract)
            for k in range(8):
                rho = par[:, k, 5:6].bc([128, nb])
                for part, Z in ((0, F[2]), (1, F[3])):
                    init = 0.0 if blk.first else self.s5H[:, part, k:k + 1]
                    S.scan(F[4 + part][:, k, :], rho, Z[:, k, :], init)
            Wr, Wi = F[4], F[5]
            S.tt("dve", F[0], Wr, Ec, ALU.mult)
            S.tt("dve", F[1], Wi, Es, ALU.mult)
            S.tt("dve", F[6], F[0], F[1], ALU.subtract)
            S.tt("dve", F[0], Wi, Ec, ALU.mult)
            S.tt("dve", F[1], Wr, Es, ALU.mult)
            S.tt("dve", F[7], F[0], F[1], ALU.add)
            for part, Hn in ((0, F[6]), (1, F[7])):
                if blk.first:
                    S.memset("pool", self.s5Hst[:, part, :, 0:1], 0.0)
                else:
                    S.copy("dve", self.s5Hst[:, part, :, 0], self.s5H[:, part, :])
                S.copy("act", self.s5Hst[:, part, :, 1:nb], Hn[:, :, 0:nb - 1])
                S.copy("dve", self.s5H[:, part, :], Hn[:, :, nb - 1])
            if blk.last:
                for part, nm in ((0, "p_s5_re"), (1, "p_s5_im")):
                    S.dma_out("sp", self.O[nm][l, 0].rearrange("(k g2) n -> (g2 n) k", g2=2), self.s5H[:, part, :], allow_slow_non_contiguous=True)
                self.out_tiles.append(self.s5H.t)
        else:
            stg = self.ybuf[1]
            s5H0 = self.df[4].v().rr("p (a k b) -> p a k b", a=2, k=8)
            a8r = par[:, :, 6].rr("p (k o) -> p k o", o=1).bc([128, 8, 16])
            a8i = par[:, :, 7].rr("p (k o) -> p k o", o=1).bc([128, 8, 16])
            h0r, h0i = s5H0[:, 0], s5H0[:, 1]
            S.tt("dve", F[0], h0r, a8r, ALU.mult)
            S.tt("dve", F[1], h0i, a8i, ALU.mult)
            S.tt("dve", F[0], F[0], F[1], ALU.subtract)
            S.tt("dve", F[2], F[0], Xr3, ALU.add)
            S.tt("dve", F[0], h0i, a8r, ALU.mult)
            S.tt("dve", F[1], h0r, a8i, ALU.mult)
            S.tt("dve", F[0], F[0], F[1], ALU.add)
            S.tt("dve", F[3], F[0], Xi3, ALU.add)
            for part, nm, Hn in ((0, "s_s5_re", F[2]), (1, "s_s5_im", F[3])):
                yield
                pp = [self.ps(), self.ps()]
                for k in range(8):
                    S.tr(pp[k // 4][0:16, (k % 4) * 128:(k % 4 + 1) * 128], Hn[:, k, :], self.identf.v())
                for hf in range(2):
                    S.copy("act" if hf else "dve", stg[0:16, hf * 512:(hf + 1) * 512], pp[hf][0:16, :])
                S.dma_out("sp", self.O[nm][l].rearrange("b g n -> b (g n)"), stg[0:16, :])
            self.out_tiles.append(stg.ts[0] if isinstance(stg, View) else stg.t)
        yield
        pY = [self.ps(), self.ps()]
        for g in range(16):
            k, g2 = g // 2, g % 2
            hb = slice(64 * g2, 64 * g2 + 64)
            out = pY[g2][:, k * nb:(k + 1) * nb]
            S.mm(out, self.s5Toep[:, g, :], self.s5Uf[:, g, 0:nb], start=True, stop=False)
            S.mm(out, self.s5P[hb, 0, k, :], self.s5Hst[hb, 0, k, 0:nb], start=False, stop=False)
            S.mm(out, self.s5P[hb, 1, k, :], self.s5Hst[hb, 1, k, 0:nb], start=False, stop=True)
        Yf4 = self.s5Yf[:, :, 0:nb].rr("p (k g2) b -> p k g2 b", g2=2)
        for g2 in range(2):
            S.copy("act" if g2 else "dve", Yf4[:, :, g2, :], pY[g2][:, 0:8 * nb].rr("p (k b) -> p k b", b=nb))
        y = [self.df[0], self.df[1]]
        yb = self.dbf
        for c in range(2):
            yield
            pU = self.ps()
            for j in range(4):
                k = 4 * c + j
                tp = (0, 96) if j == 3 else None
                for tau in range(8):
                    out = pU[32 * j:32 * j + 32, 0:N].rr("p (b s) -> p b s", s=8)[:, :, tau]
                    S.mm(out, self.s5SelU[:, tau, :], self.s5Yf[:, 2 * k, 0:nb], start=True, stop=False, tile_position=tp)
                    S.mm(out, self.s5SelU[:, 8 + tau, :], self.s5Yf[:, 2 * k + 1, 0:nb], start=False, stop=True, tile_position=tp)
            S.act(y[c][:, 0:N], pU[:, 0:N], AF.Gelu)
            S.copy("dve", yb[c][:, 0:N], y[c][:, 0:N])
        for fo in range(2):
            yield
            pg = self.ps()
            for ec in range(2):
                S.mm(pg[:, 0:N], self.s5glu[:, ec, fo * 128:(fo + 1) * 128], yb[ec][:, 0:N], start=(ec == 0), stop=(ec == 1))
            sig = self.df[4]
            S.act(sig[:, 0:N], pg[:, 0:N], AF.Sigmoid, bias=self.s5gb[:, fo:fo + 1], scale=1.0)
            yield
            pz = self.ps()
            self.zmm(blk, C_DG + fo * 128, 128, pz[:, 0:N])
            sg = self.df[5]
            S.act(sg[:, 0:N], pz[:, 0:N], AF.Silu)
            S.tt("dve", sig[:, 0:N], sig[:, 0:N], y[fo][:, 0:N], ALU.mult)
            S.tt("dve", self.mixT[6 + fo][:, 0:N], sig[:, 0:N], sg[:, 0:N], ALU.mult)


WSHAPES = {
    "w_in": [DEPTH, D, INC], "w_out": [DEPTH, D, D], "ln_g": [DEPTH, D], "ln_b": [DEPTH, D],
    "rg_conv_w": [DEPTH, 4, 256], "rg_conv_b": [DEPTH, 256], "rg_gate_a_w": [DEPTH, 4, 64, 64], "rg_gate_a_b": [DEPTH, 256],
    "rg_gate_x_w": [DEPTH, 4, 64, 64], "rg_gate_x_b": [DEPTH, 256], "rg_lambda": [DEPTH, 256],
    "ssd_conv_w": [DEPTH, 4, 512], "ssd_conv_b": [DEPTH, 512], "ssd_dt_bias": [DEPTH, 4], "ssd_a_log": [DEPTH, 4],
    "ssd_d": [DEPTH, 4], "ssd_norm_w": [DEPTH, 256],
    "gdn_conv_w": [DEPTH, 4, 768], "gdn_conv_b": [DEPTH, 768], "gdn_dt_bias": [DEPTH, 4], "gdn_a_log": [DEPTH, 4],
    "gdn_norm_w": [DEPTH, 64],
    "s5_lambda_re": [DEPTH, 16, 64], "s5_lambda_im": [DEPTH, 16, 64], "s5_log_dt": [DEPTH, 16],
    "s5_b_re": [DEPTH, 16, 64, 16], "s5_b_im": [DEPTH, 16, 64, 16], "s5_c_re": [DEPTH, 16, 16, 64], "s5_c_im": [DEPTH, 16, 16, 64],
    "s5_d": [DEPTH, 256], "s5_glu_w": [DEPTH, 256, 256], "s5_glu_b": [DEPTH, 256],
}

STATE_NAMES = ["cache_rglru_conv", "state_rglru", "cache_ssd_conv", "state_ssd", "cache_gdn_conv", "state_gdn",
               "state_s5_re", "state_s5_im"]
OUT_STATE = ["_rg_conv", "_rg", "_ssd_conv", "_ssd", "_gdn_conv", "_gdn", "_s5_re", "_s5_im"]

_PROG_CACHE = {}


def get_prog(mixers="ABCD", dbg=None):
    key = (mixers, tuple(n for n, _ in (dbg or [])))
    if key not in _PROG_CACHE:
        _PROG_CACHE[key] = Prog(mixers, dbg)
    return _PROG_CACHE[key]


def make_in_maps(inputs, cores):
    maps = []
    f = lambda a: np.ascontiguousarray(np.asarray(a, dtype=np.float32))
    for c in cores:
        m = {"xp": f(inputs["x_prompt"][c]), "xs": f(inputs["x_sample"][16 * c:16 * c + 16].reshape(128, D))}
        for n in STATE_NAMES:
            m[n] = f(inputs[n][:, 16 * c:16 * c + 16])
        for n in WSHAPES:
            m[n] = f(inputs[n])
        maps.append(m)
    return maps


def kernel(**inputs):
    prog = get_prog()
    cores = list(range(NCORES))
    res = run_bass_kernel_spmd(prog.nc, make_in_maps(inputs, cores), core_ids=cores)
    R = res.results
    yp = np.stack([R[c]["yp"] for c in cores], 0)
    ys = np.concatenate([R[c]["ys"].reshape(16, 8, D) for c in cores], 0)
    outs = [yp, ys]
    for pre in ("p", "s"):
        for n in OUT_STATE:
            outs.append(np.concatenate([R[c][pre + n] for c in cores], axis=1))
    return tuple(np.ascontiguousarray(o.astype(np.float32)) for o in outs)
```
